# Optimizing a Trainium2 kernel written in Bass

```python
import jax, jax.numpy as jnp
from jax import lax
import numpy as np

D_MODEL = 1024
BATCH = 4
SEQ = 4096
DEPTH = 1
DEC_BATCH = 32
DEC_SEQ = 1
PAST_LEN = 8192
PAGE_SIZE = 128

D_RNN = D_MODEL
N_RNN_BLOCKS = 16
RNN_BLOCK = D_RNN // N_RNN_BLOCKS
CONV_WIDTH = 4
LRU_C = 8.0
HEAD_DIM = 64
H_G = 8
ATTN_GROUPS = ((128, 1), (512, 4), (2048, 16))
N_GROUPS = 3
D_ATTN = H_G * HEAD_DIM
Q_BLOCK = 128
ALIBI_MAX = 8.0
EPS = 1e-6
NEG_INF = -1e30
SPLIT_SIZES = (D_RNN, D_RNN, N_GROUPS * D_ATTN, N_GROUPS * D_ATTN, N_GROUPS * D_ATTN, D_ATTN, D_MODEL, D_MODEL)
D_IN = 2 * D_RNN + 3 * N_GROUPS * D_ATTN + D_ATTN + 2 * D_MODEL

kernel_name = "hawk_dilated_alibi_hybrid_step"


def split_points():
    pts, acc = [], 0
    for s in SPLIT_SIZES[:-1]:
        acc += s
        pts.append(acc)
    return pts


def rms_norm(x, gain):
    x32 = x.astype(jnp.float32)
    y = x32 * lax.rsqrt(jnp.mean(x32 * x32, axis=-1, keepdims=True) + EPS)
    return (y * gain.astype(jnp.float32)).astype(x.dtype)


def alibi_slopes():
    n = N_GROUPS * H_G
    s = 2.0 ** (-ALIBI_MAX * jnp.arange(1, n + 1, dtype=jnp.float32) / n)
    return s.reshape(N_GROUPS, H_G)


def causal_conv(u, buf, conv_w, conv_b):
    T = u.shape[1]
    ext = jnp.concatenate([buf.astype(u.dtype), u], axis=1)
    out = conv_b
    for tap in range(CONV_WIDTH):
        out = out + ext[:, tap:tap + T] * conv_w[tap]
    return out, ext[:, -(CONV_WIDTH - 1):]


def rg_lru(xc, h0, w_a, b_a, w_x, b_x, lam):
    B, T, C = xc.shape
    xb = xc.reshape(B, T, N_RNN_BLOCKS, RNN_BLOCK)
    r = jax.nn.sigmoid((jnp.einsum('btnc,ncd->btnd', xb, w_a).reshape(B, T, C) + b_a).astype(jnp.float32))
    i = jax.nn.sigmoid((jnp.einsum('btnc,ncd->btnd', xb, w_x).reshape(B, T, C) + b_x).astype(jnp.float32))
    log_a = -LRU_C * r * jax.nn.softplus(-lam.astype(jnp.float32))
    a = jnp.exp(log_a)
    b = jnp.sqrt(-jnp.expm1(2.0 * log_a)) * i * xc.astype(jnp.float32)
    b = b.at[:, 0].add(a[:, 0] * h0.astype(jnp.float32))

    def combine(left, right):
        a_l, b_l = left
        a_r, b_r = right
        return a_l * a_r, a_r * b_l + b_r

    _, h = lax.associative_scan(combine, (a, b), axis=1)
    return h.astype(xc.dtype), h[:, -1].astype(h0.dtype)


def dilated_attention_prompt(q, k, v, window, dilation, slopes):
    B, S, H, E = q.shape
    L = S // dilation
    nb = -(-L // Q_BLOCK)
    Lp = nb * Q_BLOCK
    max_steps = window // dilation

    def to_blocks(t):
        t = t.reshape(B, L, dilation, H, E).transpose(0, 2, 1, 3, 4)
        t = jnp.pad(t, ((0, 0), (0, 0), (0, Lp - L), (0, 0), (0, 0)))
        return t.reshape(B, dilation, nb, Q_BLOCK, H, E)

    def with_prev(t):
        prev = jnp.pad(t, ((0, 0), (0, 0), (1, 0), (0, 0), (0, 0), (0, 0)))[:, :, :-1]
        return jnp.concatenate([prev, t], axis=3)

    qb = to_blocks(q)
    kc = with_prev(to_blocks(k))
    vc = with_prev(to_blocks(v))
    s = jnp.einsum('brnqhe,brnkhe->brnhqk', qb, kc, preferred_element_type=jnp.float32) * (HEAD_DIM ** -0.5)
    qi = jnp.arange(Q_BLOCK)
    kj = jnp.arange(2 * Q_BLOCK)
    steps = Q_BLOCK + qi[:, None] - kj[None, :]
    key_sub = jnp.arange(nb)[:, None] * Q_BLOCK - Q_BLOCK + kj[None, :]
    mask = ((steps >= 0) & (steps <= max_steps))[None] & (key_sub >= 0)[:, None, :]
    bias = -slopes[:, None, None] * (steps * dilation).astype(jnp.float32)[None]
    s = jnp.where(mask[None, None, :, None], s + bias[None, None, None], NEG_INF)
    lse = jax.nn.logsumexp(s, axis=-1)
    p = jnp.exp(s - lse[..., None])
    o = jnp.einsum('brnhqk,brnkhe->brnqhe', p.astype(vc.dtype), vc)

    def from_blocks(t):
        t = t.reshape((B, dilation, Lp) + t.shape[4:])[:, :, :L]
        t = jnp.moveaxis(t, 1, 2)
        return t.reshape((B, S) + t.shape[3:])

    return from_blocks(o), from_blocks(jnp.swapaxes(lse, -1, -2))


def dilated_attention_sample(q, k, v, kv_buf, window, dilation, slopes):
    T = q.shape[1]
    Wb = kv_buf.shape[1]
    k_all = jnp.concatenate([kv_buf[:, :, 0].astype(k.dtype), k], axis=1)
    v_all = jnp.concatenate([kv_buf[:, :, 1].astype(v.dtype), v], axis=1)
    steps = jnp.arange(window // dilation + 1)
    idx = Wb + jnp.arange(T)[:, None] - steps[None, :] * dilation
    valid = idx >= 0
    idx_c = jnp.maximum(idx, 0)
    kg = jnp.take(k_all, idx_c, axis=1)
    vg = jnp.take(v_all, idx_c, axis=1)
    s = jnp.einsum('bthe,btkhe->bhtk', q, kg, preferred_element_type=jnp.float32) * (HEAD_DIM ** -0.5)
    bias = -slopes[:, None, None] * (steps * dilation).astype(jnp.float32)[None, None, :]
    s = jnp.where(valid[None, None], s + bias[None], NEG_INF)
    lse = jax.nn.logsumexp(s, axis=-1)
    p = jnp.exp(s - lse[..., None])
    o = jnp.einsum('bhtk,btkhe->bthe', p.astype(vg.dtype), vg)
    return o, jnp.swapaxes(lse, 1, 2)


def merge_dilations(outs, lses):
    o = jnp.stack(outs)
    w = jax.nn.softmax(jnp.stack(lses), axis=0)
    return jnp.sum(o * w[..., None].astype(o.dtype), axis=0)


def hybrid_layer(x, conv_buf, h0, kv_bufs, norm_pre, norm_post, w_in, conv_w, conv_b,
                 lru_w_a, lru_b_a, lru_w_x, lru_b_x, lru_lambda, w_rnn_out, w_attn_out, w_out):
    B, T, _ = x.shape
    xn = rms_norm(x, norm_pre)
    proj = xn @ w_in
    u, z_rnn, q, k, v, z_attn, g_rnn, g_attn = jnp.split(proj, split_points(), axis=-1)
    uc, conv_new = causal_conv(u, conv_buf, conv_w, conv_b)
    h, h_last = rg_lru(uc, h0, lru_w_a, lru_b_a, lru_w_x, lru_b_x, lru_lambda)
    y_rnn = (h * jax.nn.silu(z_rnn)) @ w_rnn_out
    q = q.reshape(B, T, N_GROUPS, H_G, HEAD_DIM)
    k = k.reshape(B, T, N_GROUPS, H_G, HEAD_DIM)
    v = v.reshape(B, T, N_GROUPS, H_G, HEAD_DIM)
    slopes = alibi_slopes()
    outs, lses, kv_new = [], [], []
    for g, (window, dilation) in enumerate(ATTN_GROUPS):
        if kv_bufs is None:
            o, lse = dilated_attention_prompt(q[:, :, g], k[:, :, g], v[:, :, g], window, dilation, slopes[g])
            keep = min(window, T)
            kv_new.append(jnp.stack([k[:, -keep:, g], v[:, -keep:, g]], axis=2))
        else:
            o, lse = dilated_attention_sample(q[:, :, g], k[:, :, g], v[:, :, g], kv_bufs[g], window, dilation, slopes[g])
            kv_new.append(jnp.stack([k[:, :, g], v[:, :, g]], axis=2))
        outs.append(o)
        lses.append(lse)
    y_attn = merge_dilations(outs, lses).reshape(B, T, D_ATTN)
    y_attn = (y_attn * jax.nn.silu(z_attn)) @ w_attn_out
    mixed = jax.nn.sigmoid(g_rnn) * y_rnn + jax.nn.sigmoid(g_attn) * y_attn
    out = rms_norm(mixed @ w_out, norm_post)
    return x + out, conv_new, h_last, kv_new


def setup_inputs(seed: int = 0) -> dict:
    key = jax.random.key(seed)
    ks = jax.random.split(key, 24)
    f32 = jnp.float32
    nrm = lambda k, shape, scale: jax.random.normal(k, shape, f32) * scale
    a_c = jax.random.uniform(ks[0], (DEPTH, D_RNN), f32, 0.9, 0.999)
    a = a_c ** (1.0 / LRU_C)
    lam = jnp.log(a) - jnp.log1p(-a)

    def kv_cache(k, window):
        return nrm(k, (DEPTH, DEC_BATCH, min(window, PAST_LEN), 2, H_G, HEAD_DIM), 1.0)

    return {
        "x_prompt": nrm(ks[1], (BATCH, SEQ, D_MODEL), 1.0),
        "x_sample": nrm(ks[2], (DEC_BATCH, DEC_SEQ, D_MODEL), 1.0),
        "state_conv": nrm(ks[3], (DEPTH, DEC_BATCH, CONV_WIDTH - 1, D_RNN), 1.0),
        "state_h": nrm(ks[4], (DEPTH, DEC_BATCH, D_RNN), 0.5),
        "cache_kv_w128": kv_cache(ks[5], ATTN_GROUPS[0][0]),
        "cache_kv_w512": kv_cache(ks[6], ATTN_GROUPS[1][0]),
        "cache_kv_w2048": kv_cache(ks[7], ATTN_GROUPS[2][0]),
        "norm_pre": 1.0 + nrm(ks[8], (DEPTH, D_MODEL), 0.05),
        "norm_post": 1.0 + nrm(ks[9], (DEPTH, D_MODEL), 0.05),
        "w_in": nrm(ks[10], (DEPTH, D_MODEL, D_IN), D_MODEL ** -0.5),
        "conv_w": nrm(ks[11], (DEPTH, CONV_WIDTH, D_RNN), CONV_WIDTH ** -0.5),
        "conv_b": nrm(ks[12], (DEPTH, D_RNN), 0.01),
        "lru_w_a": nrm(ks[13], (DEPTH, N_RNN_BLOCKS, RNN_BLOCK, RNN_BLOCK), RNN_BLOCK ** -0.5),
        "lru_b_a": nrm(ks[14], (DEPTH, D_RNN), 0.01),
        "lru_w_x": nrm(ks[15], (DEPTH, N_RNN_BLOCKS, RNN_BLOCK, RNN_BLOCK), RNN_BLOCK ** -0.5),
        "lru_b_x": nrm(ks[16], (DEPTH, D_RNN), 0.01),
        "lru_lambda": lam,
        "w_rnn_out": nrm(ks[17], (DEPTH, D_RNN, D_MODEL), D_RNN ** -0.5),
        "w_attn_out": nrm(ks[18], (DEPTH, D_ATTN, D_MODEL), D_ATTN ** -0.5),
        "w_out": nrm(ks[19], (DEPTH, D_MODEL, D_MODEL), D_MODEL ** -0.5),
    }


def reference(x_prompt, x_sample, state_conv, state_h, cache_kv_w128, cache_kv_w512, cache_kv_w2048,
              norm_pre, norm_post, w_in, conv_w, conv_b, lru_w_a, lru_b_a, lru_w_x, lru_b_x, lru_lambda,
              w_rnn_out, w_attn_out, w_out):
    yp, ys = x_prompt, x_sample
    conv_p, conv_s, h_p, h_s = [], [], [], []
    kvp = ([], [], [])
    kvs = ([], [], [])
    for l in range(DEPTH):
        params = (norm_pre[l], norm_post[l], w_in[l], conv_w[l], conv_b[l], lru_w_a[l], lru_b_a[l],
                  lru_w_x[l], lru_b_x[l], lru_lambda[l], w_rnn_out[l], w_attn_out[l], w_out[l])
        zeros_conv = jnp.zeros((yp.shape[0], CONV_WIDTH - 1, D_RNN), yp.dtype)
        zeros_h = jnp.zeros((yp.shape[0], D_RNN), state_h.dtype)
        yp, cp, hp, kv_p = hybrid_layer(yp, zeros_conv, zeros_h, None, *params)
        ys, cs, hs, kv_s = hybrid_layer(ys, state_conv[l], state_h[l],
                                        (cache_kv_w128[l], cache_kv_w512[l], cache_kv_w2048[l]), *params)
        conv_p.append(cp)
        conv_s.append(cs)
        h_p.append(hp)
        h_s.append(hs)
        for g in range(N_GROUPS):
            kvp[g].append(kv_p[g])
            kvs[g].append(kv_s[g])
    return (yp, ys, jnp.stack(conv_p), jnp.stack(conv_s), jnp.stack(h_p), jnp.stack(h_s),
            jnp.stack(kvp[0]), jnp.stack(kvs[0]), jnp.stack(kvp[1]), jnp.stack(kvs[1]),
            jnp.stack(kvp[2]), jnp.stack(kvs[2]))
```

```python
from contextlib import ExitStack
import numpy as np
import concourse.bass as bass
import concourse.mybir as mybir
from concourse.bass_utils import run_bass_kernel_spmd

F32 = mybir.dt.float32
BF16 = mybir.dt.bfloat16
AF = mybir.ActivationFunctionType
ALU = mybir.AluOpType
AX = mybir.AxisListType

NT = 2048
XW = 4100
FW = 2052
DIL = (1, 4, 16)
EPS = 1e-6


class Res:
    __slots__ = ("name", "lastw", "readers", "sem", "semcnt", "excl")

    def __init__(self, name):
        self.name = name
        self.lastw = None
        self.readers = []
        self.sem = None
        self.semcnt = 0
        self.excl = name.startswith("pb") and name[2:].isdigit()


class Op:
    __slots__ = ("idx", "eng", "fn", "owner", "preds", "dur", "tok", "fin")

    def __init__(self, idx, eng, fn, owner, preds, dur):
        self.idx = idx
        self.eng = eng
        self.fn = fn
        self.owner = owner
        self.preds = preds
        self.dur = dur
        self.tok = None
        self.fin = 0.0


class Sched:
    ENGS = ("sync", "scalar", "gpsimd", "vector", "tensor")
    WINDOW = 160

    def __init__(self, nc, stack):
        self.nc = nc
        self.stack = stack
        self.out = {e: [] for e in self.ENGS}
        self.esem = {}
        self.ecnt = {e: 0 for e in self.ENGS}
        self.known = {e: {} for e in self.ENGS}
        self.nsem = 0
        self.owners = []
        self.seg = []
        self.nops = 0
        for e in ("scalar", "gpsimd", "vector", "tensor"):
            self.esem[e] = self.newsem("e_" + e)

    def newsem(self, name):
        self.nsem += 1
        return self.stack.enter_context(self.nc.semaphore(name))

    def _record(self, eng, fn, owner, reads, writes):
        ex = [r for r in reads if r.excl]
        if ex:
            writes = list(writes) + [r for r in ex if r not in writes]
            reads = [r for r in reads if not r.excl]
        preds = set()
        for r in reads:
            if r.lastw is not None:
                preds.add(r.lastw)
        for w in writes:
            if w.lastw is not None:
                preds.add(w.lastw)
            preds.update(w.readers)
        op = Op(self.nops, eng, fn, owner, preds, getattr(fn, "dur", 0.2))
        self.nops += 1
        for w in writes:
            w.lastw = op
            w.readers = []
        for r in reads:
            if r not in writes:
                r.readers.append(op)
        self.seg.append(op)
        return op

    def op(self, eng, fn, reads=(), writes=()):
        self._record(eng, fn, None, reads, writes)

    def dma(self, eng, fn, owner, reads=(), writes=()):
        if owner.sem is None:
            owner.sem = self.newsem("d_" + owner.name)
            self.owners.append(owner)
        self._record(eng, fn, owner, reads, writes)

    def _schedule_segment(self):
        seg = self.seg
        self.seg = []
        if not seg:
            return
        segset = set(id(o) for o in seg)
        pend = {e: [] for e in self.ENGS}
        for o in seg:
            o.preds = [p for p in o.preds if id(p) in segset]
            pend[o.eng].append(o)
        pos = {e: 0 for e in self.ENGS}
        free = {e: 0.0 for e in self.ENGS}
        done = set()
        order = {e: [] for e in self.ENGS}
        remaining = len(seg)
        W = self.WINDOW
        while remaining:
            best = None
            for e in self.ENGS:
                lst = pend[e]
                if not lst:
                    continue
                fe = free[e]
                cnt = 0
                for o in lst:
                    cnt += 1
                    if cnt > W:
                        break
                    rdy = 0.0
                    ok = True
                    for p in o.preds:
                        if id(p) not in done:
                            ok = False
                            break
                        if p.fin > rdy:
                            rdy = p.fin
                    if not ok:
                        continue
                    st = fe if fe > rdy else rdy
                    if best is None or st < best[0] - 1e-9:
                        best = (st, e, o)
                    if rdy <= fe:
                        break
            assert best is not None, "scheduler deadlock"
            st, e, o = best
            pend[e].remove(o)
            if e in ("sync", "gpsimd") and o.owner is not None:
                free[e] = st + 0.06
                o.fin = st + o.dur
            else:
                free[e] = st + o.dur
                o.fin = st + o.dur
            done.add(id(o))
            order[e].append(o)
            remaining -= 1
        allorder = sorted(seg, key=lambda o: (o.fin - o.dur if o.owner is None else o.fin - o.dur, o.idx))
        for e in self.ENGS:
            for o in order[e]:
                if o.owner is not None:
                    o.owner.semcnt += 16
                    o.tok = (o.owner.sem, o.owner.semcnt)
                else:
                    self.ecnt[e] += 1
                    o.tok = (self.esem[e], self.ecnt[e])
        for e in self.ENGS:
            kn = self.known[e]
            for o in order[e]:
                waits = []
                for p in o.preds:
                    if e == "tensor" and p.eng == "tensor" and p.owner is None:
                        continue
                    s_, v_ = p.tok
                    if kn.get(id(s_), 0) < v_:
                        kn[id(s_)] = v_
                        waits.append((s_, v_))
                self.out[e].append((waits, o.fn, o.tok[0], 16 if o.owner is not None else 1))

    def barrier(self):
        self._schedule_segment()
        allsems = [(self.esem[e], self.ecnt[e]) for e in self.esem] + [(o.sem, o.semcnt) for o in self.owners]
        for eng in self.ENGS:
            kn = self.known[eng]
            waits = []
            for s, v in allsems:
                if v > 0 and kn.get(id(s), 0) < v:
                    kn[id(s)] = v
                    waits.append((s, v))
            self.out[eng].append((waits, None, None, 0))

    def emit(self):
        self._schedule_segment()
        nc = self.nc
        with nc.Block() as block:
            for ename in self.ENGS:
                ops = self.out[ename]
                if not ops:
                    continue

                def body(e, ops=ops):
                    for waits, fn, sem, inc in ops:
                        for s, v in waits:
                            e.wait_ge(s, v)
                        if fn is not None:
                            fn(e).then_inc(sem, inc)

                getattr(block, ename)(body)


def _fsz(ap):
    n = 1
    for d in ap.shape[1:]:
        n *= d
    return n


def _d(f, dur):
    f.dur = dur
    return f


def COPY(out, in_):
    return _d(lambda e: e.tensor_copy(out=out, in_=in_), (_fsz(in_) + 60) / 960.0)


def ACT(out, in_, func, **kw):
    return _d(lambda e: e.activation(out=out, in_=in_, func=func, **kw), (_fsz(in_) + 250) / 1200.0)


def MM(out, lhsT, rhs, start, stop, skip=False):
    n = max(_fsz(rhs), 64) * (4 if rhs.dtype == F32 else 1)
    if skip:
        return _d(lambda e: e.matmul(out, lhsT=lhsT, rhs=rhs, start=start, stop=stop, skip_group_check=True), n / 2400.0 + 0.03)
    return _d(lambda e: e.matmul(out, lhsT=lhsT, rhs=rhs, start=start, stop=stop), n / 2400.0 + 0.03)


def TR(out, in_, ident):
    return _d(lambda e: e.transpose(out=out, in_=in_, identity=ident), 0.12 * (4 if in_.dtype == F32 else 1))


def TT(out, a, b, op):
    return _d(lambda e: e.tensor_tensor(out=out, in0=a, in1=b, op=op), (_fsz(a) + 60) / 960.0)


def STT(out, in0, scalar, in1, op0, op1):
    return _d(lambda e: e.scalar_tensor_tensor(out=out, in0=in0, scalar=scalar, in1=in1, op0=op0, op1=op1), (_fsz(in0) + 60) / 960.0)


def TS(out, in0, s1, s2, op0, op1=None):
    du = (_fsz(in0) + 60) / 960.0
    if op1 is None:
        return _d(lambda e: e.tensor_scalar(out=out, in0=in0, scalar1=s1, scalar2=None, op0=op0), du)
    return _d(lambda e: e.tensor_scalar(out=out, in0=in0, scalar1=s1, scalar2=s2, op0=op0, op1=op1), du)


def DMA(out, in_):
    nb = 4 * in_.shape[0] * _fsz(in_)
    return _d(lambda e: e.dma_start(out=out, in_=in_), 2.0 + nb / 150000.0)


def MEMSET(ap, v):
    return _d(lambda e: e.memset(ap, v), (_fsz(ap) + 60) / 960.0)


def RECIP(out, in_):
    return _d(lambda e: e.reciprocal(out=out, in_=in_), (_fsz(in_) + 60) / 960.0)


def SCAN(out, d0, d1, init):
    return _d(lambda e: e.tensor_tensor_scan(out=out, data0=d0, data1=d1, initial=init, op0=ALU.mult, op1=ALU.add), (2 * _fsz(d0) + 60) / 960.0)


def REDUCE(out, in_):
    return _d(lambda e: e.tensor_reduce(out=out, in_=in_, axis=AX.X, op=ALU.add), (_fsz(in_) + 60) / 960.0)


class Arena:
    def __init__(self, t, size):
        self.t = t
        self.size = size
        self.top = 0
        self.reserve_top = 0

    def f32(self, n):
        off = self.top
        self.top += n
        assert self.top <= self.size - self.reserve_top, (self.top, self.size, self.reserve_top)
        return self.t[:, off:off + n]

    def bf16(self, n):
        words = (n + 1) // 2
        return self.f32(words).bitcast(BF16)[:, 0:n]


def SL(start, n, step):
    return slice(start, start + (n - 1) * step + 1, step)


def block_cols(g, rho, n):
    d = DIL[g]
    return 2048 + rho + d * 128 * n, d


def build_program(upto=99, dbgfn=None, sub=99):
    nc = bass.Bass("TRN2", target_bir_lowering=False)

    def din(name, shape):
        return nc.dram_tensor(name, shape, F32, kind="ExternalInput").ap()

    def dout(name, shape):
        return nc.dram_tensor(name, shape, F32, kind="ExternalOutput").ap()

    xm = din("xm", [2048, 1024])
    xp = din("xp", [2048, 1024])
    pvd = din("pv", [128, 1])
    xsd = din("xs", [4, 1024])
    sst = din("sst", [16, 1024])
    sconv = din("sconv", [4, 3, 1024])
    cache = [din("c0", [4, 128, 2, 512]), din("c1", [4, 512, 2, 512]), din("c2", [4, 2048, 2, 512])]
    win = din("win", [72, 128, 1024])
    wqkv = din("wqkv", [6, 128, 6144])
    wro = din("wro", [8, 128, 1024])
    wao = din("wao", [8, 128, 512])
    wout = din("wout", [128, 8192])
    wga = din("wga", [2, 2, 64, 512])
    prm = din("prm", [64, 128])
    cbrow = din("cbrow", [1, 1024])
    gpre = din("gpre", [1, 1024])
    gpost = din("gpost", [1, 1024])
    etab = din("etab", [3, 3, 128, 1024])
    sbias = din("sbias", [128, 24])
    identin = din("ident", [128, 128])
    selq = din("selq", [4, 512])
    selo = din("selo", [128, 16])

    y = dout("y", [2048, 1024])
    convp = dout("convp", [128, 24])
    hp = dout("hp", [128, 8])
    kvo = [dout("kvo0", [128, 2, 512]), dout("kvo1", [512, 2, 512]), dout("kvo2", [2048, 2, 512])]
    ys = dout("ys", [4, 1024])
    convs_u = dout("convs_u", [128, 32])
    convs_s = dout("convs_s", [4, 2, 1024])
    hs = dout("hs", [128, 32])
    kvs = dout("kvs", [3, 4, 2, 512])
    dbg = dout("dbg", [128, 8192]) if upto != 99 else None
    osc = [nc.dram_tensor(f"osc{g}", [2048, 2, 260], F32).ap() for g in range(3)]

    with ExitStack() as st:
        S = Sched(nc, st)
        AW = 53200
        arena_t = st.enter_context(nc.sbuf_tensor("arena", [128, AW], F32))
        A = Arena(arena_t, AW)
        ps = [st.enter_context(nc.psum_tensor(f"ps{i}", [128, 512], F32)) for i in range(8)]
        PB = [Res(f"pb{i}") for i in range(8)]
        bank_i = [0]

        def nextbank():
            i = bank_i[0]
            bank_i[0] = (i + 1) % 8
            return ps[i], PB[i]

        XNT = A.bf16(8 * XW).rearrange("p (k n) -> p k n", k=8)
        HZT = A.bf16(8 * FW).rearrange("p (k n) -> p k n", k=8)
        IDENTF = A.f32(128)
        IDENT = A.bf16(128)
        PRM = A.f32(64)
        CLT = A.f32(16)
        PVT = A.f32(1)
        HPV = A.f32(1)
        HB = A.f32(16)
        WG = A.bf16(8 * 2 * 128).rearrange("p (c g o) -> p c g o", c=8, g=2)
        HOUT = A.f32(8)
        CONVP = A.f32(24).rearrange("p (c t) -> p c t", c=8)
        SNUM = A.f32(520)
        SELQ = A.f32(512)
        SELO = A.f32(16)
        SBIAS = A.f32(24)
        SMALL = A.f32(64)
        SSTT = A.f32(128)
        SUC = A.f32(32)
        SZS = A.f32(32)
        SH = A.f32(32)
        rXNT = [Res(f"xnt{i}") for i in range(33)]
        rHZT = [Res(f"hzt{c}") for c in range(8)]
        rHZTs = Res("hzts")
        rC = Res("consts")
        rSNUM = Res("snum")
        rSM = Res("small")
        rSST = Res("sstt")
        rSUC = Res("suc")
        rSZS = Res("szs")
        rSH = Res("sh")
        rOUTS = Res("outs_small")
        SCR0 = A.top

        S.dma("sync", DMA(IDENTF, identin), rC, writes=[rC])
        S.dma("sync", DMA(PVT, pvd), rC, writes=[rC])
        S.dma("sync", DMA(SELQ[0:4, :], selq), rC, writes=[rC])
        S.dma("sync", DMA(SELO, selo), rC, writes=[rC])
        S.dma("sync", DMA(SBIAS, sbias), rC, writes=[rC])
        A.top = SCR0
        PRMRAW = A.f32(128)
        SSRAW = A.f32(1024)
        GPRE = A.f32(1024)
        rT = Res("p0tmp")
        rG = Res("gpre")
        rWGL = Res("wgload")
        S.dma("sync", DMA(GPRE, gpre.partition_broadcast(128)), rG, writes=[rG])
        S.dma("sync", DMA(PRMRAW[0:64, :], prm), rT, writes=[rT])
        S.dma("sync", DMA(SSRAW[0:16, :], sst), rT, writes=[rT])
        S.op("vector", COPY(IDENT, IDENTF), reads=[rC], writes=[rC])
        S.op("vector", MEMSET(WG.rearrange("p c g o -> p (c g o)"), 0.0), writes=[rC])
        S.op("vector", MEMSET(SNUM[0:4, :], 0.0), writes=[rSNUM])
        for g in range(2):
            for hf in range(2):
                S.dma("gpsimd", DMA(WG[hf * 64:(hf + 1) * 64, :, g, hf * 64:(hf + 1) * 64],
                                    wga[g, hf].rearrange("i (c o) -> i c o", c=8)), rWGL, reads=[rC], writes=[rC])
        pb, rb = nextbank()
        S.op("tensor", TR(pb[:, 0:64], PRMRAW[0:64, :], IDENTF[0:64, 0:64]), reads=[rT, rC], writes=[rb])
        S.op("vector", COPY(PRM, pb[:, 0:64]), reads=[rb], writes=[rC])
        pb, rb = nextbank()
        for c in range(8):
            S.op("tensor", TR(pb[:, c * 16:(c + 1) * 16], SSRAW[0:16, c * 128:(c + 1) * 128], IDENTF[0:16, 0:16]),
                 reads=[rT, rC], writes=[rb])
        S.op("vector", COPY(SSTT, pb[:, 0:128]), reads=[rb], writes=[rSST])
        SSTT3 = SSTT.rearrange("p (c r) -> p c r", c=8)
        T1 = SMALL[:, 0:8]
        T2 = SMALL[:, 8:16]
        T3 = SMALL[:, 16:24]
        T4 = SMALL[:, 24:32]
        S.op("scalar", ACT(T1, PRM[:, 56:64], AF.Exp, scale=-1.0), reads=[rC], writes=[rSM])
        S.op("vector", TS(T2, T1, 2.0, None, ALU.add), reads=[rSM], writes=[rSM])
        S.op("vector", RECIP(T2, T2), reads=[rSM], writes=[rSM])
        S.op("vector", TT(T3, T1, T2, ALU.mult), reads=[rSM], writes=[rSM])
        S.op("vector", TT(T4, T3, T3, ALU.mult), reads=[rSM], writes=[rSM])
        S.op("vector", TS(T2, T4, 1.0 / 9, 1.0 / 7, ALU.mult, ALU.add), reads=[rSM], writes=[rSM])
        for cst in (1.0 / 5, 1.0 / 3, 1.0):
            S.op("vector", TT(T2, T2, T4, ALU.mult), reads=[rSM], writes=[rSM])
            S.op("vector", TS(T2, T2, cst, None, ALU.add), reads=[rSM], writes=[rSM])
        S.op("vector", TT(T2, T2, T3, ALU.mult), reads=[rSM], writes=[rSM])
        S.op("vector", TS(CLT[:, 0:8], T2, -8.0, None, ALU.mult), reads=[rSM], writes=[rC])
        S.op("vector", TS(CLT[:, 8:16], T2, -16.0, None, ALU.mult), reads=[rSM], writes=[rC])
        S.op("vector", TS(HB, PRM[:, 40:56], 0.5, None, ALU.mult), reads=[rC], writes=[rC])
        S.op("vector", TS(HPV, PVT, 0.5, None, ALU.mult), reads=[rC], writes=[rC])
        S.dma("sync", DMA(convs_s, sconv[:, 1:3, :]), rOUTS, writes=[rOUTS])

        NX = 3
        XT = [A.f32(1024) for _ in range(NX)]
        XB = [A.bf16(1024) for _ in range(NX)]
        rXT = [Res("xt%d" % i) for i in range(NX)]
        rXB = [Res("xb%d" % i) for i in range(NX)]
        tiles = [(xp, i, i * 128, 128) for i in range(16)] + [(xm, i, 2048 + i * 128, 128) for i in range(16)] + [(xsd, 0, 4096, 4)]

        def load_x(idx):
            src, i, col, np_ = tiles[idx]
            b = idx % NX
            S.dma("sync", DMA(XT[b][0:np_, :], src[i * 128:i * 128 + np_, :]), rXT[b], writes=[rXT[b]])

        load_x(0)
        load_x(1)
        for idx, (src, i, col, np_) in enumerate(tiles):
            b = idx % NX
            if idx + 2 < len(tiles):
                load_x(idx + 2)
            ss = SMALL[0:np_, 32 + b:33 + b]
            rs = SMALL[0:np_, 36 + b:37 + b]
            rss = Res("ss")
            S.op("scalar", ACT(XB[b][0:np_, :], XT[b][0:np_, :], AF.Square, accum_out=ss), reads=[rXT[b]], writes=[rXB[b], rss])
            S.op("scalar", ACT(rs, ss, AF.Sqrt, scale=1.0 / 1024, bias=EPS), reads=[rss], writes=[rss])
            S.op("vector", RECIP(rs, rs), reads=[rss], writes=[rss])
            S.op("vector", STT(XB[b][0:np_, :], XT[b][0:np_, :], rs, GPRE[0:np_, :], ALU.mult, ALU.mult),
                 reads=[rXT[b], rss, rG], writes=[rXB[b]])
            pb, rb = nextbank()
            pbb = pb[:, :].bitcast(BF16)
            for kc in range(8):
                S.op("tensor", TR(pbb[:, kc * np_:(kc + 1) * np_], XB[b][0:np_, kc * 128:(kc + 1) * 128], IDENT[0:np_, 0:np_]),
                     reads=[rXB[b], rC], writes=[rb])
            S.op("vector", COPY(XNT[:, :, col:col + np_], pbb[:, 0:8 * np_].rearrange("p (k n) -> p k n", k=8)),
                 reads=[rb], writes=[rXNT[idx]])
        def xr(col, n):
            return [rXNT[i] for i in range(col // 128, (col + n - 1) // 128 + 1)]

        NS = 1024
        WU = [A.bf16(1024).rearrange("p (k n) -> p k n", k=8) for _ in range(2)]
        WZ = [A.bf16(1024).rearrange("p (k n) -> p k n", k=8) for _ in range(2)]
        rWU = [Res("wu0"), Res("wu1")]
        rWZ = [Res("wz0"), Res("wz1")]
        NSET = 3
        Ub = [A.bf16(NS + 4) for _ in range(NSET)]
        DG = [A.bf16(512).rearrange("p (t n) -> p t n", t=4) for _ in range(2)]
        CBROW = A.bf16(1024)
        ONES = A.bf16(512)
        INITS = A.f32(8)
        rDG = [Res("dg0"), Res("dg1")]
        rCB = Res("cbrow")
        S.dma("gpsimd", DMA(CBROW[0:1, :], cbrow), rCB, writes=[rCB])
        S.op("vector", MEMSET(ONES[0:1, :], 1.0), writes=[rCB])
        UC = [A.f32(NS) for _ in range(NSET)]
        Rb = [A.f32(NS) for _ in range(NSET)]
        Ib = [A.f32(NS) for _ in range(NSET)]
        Ab = [A.f32(NS) for _ in range(NSET)]
        UCB = [A.bf16(NS) for _ in range(NSET)]
        rU = [Res("u%d" % i) for i in range(NSET)]
        rUC = [Res("uc%d" % i) for i in range(NSET)]
        rR = [Res("r%d" % i) for i in range(NSET)]
        rI = [Res("i%d" % i) for i in range(NSET)]
        rA = [Res("a%d" % i) for i in range(NSET)]
        rUCB = [Res("ucb%d" % i) for i in range(NSET)]

        def load_w2(c):
            b = c % 2
            S.dma("gpsimd", DMA(WU[b], win[c].rearrange("p (k n) -> p k n", k=8)), rWU[b], writes=[rWU[b]])
            S.dma("gpsimd", DMA(WZ[b], win[8 + c].rearrange("p (k n) -> p k n", k=8)), rWZ[b], writes=[rWZ[b]])

        load_w2(0)

        def prm_of(c):
            return dict(w0=PRM[:, 0 + c:1 + c], w1=PRM[:, 8 + c:9 + c], w2=PRM[:, 16 + c:17 + c], w3=PRM[:, 24 + c:25 + c],
                        cb=PRM[:, 32 + c:33 + c], ba=HB[:, c:c + 1], bx=HB[:, 8 + c:9 + c], clh=CLT[:, c:c + 1], cl=CLT[:, 8 + c:9 + c])

        def stageA(it):
            c, s = it // 4, it % 4
            P = prm_of(c)
            wb = c % 2
            tb = it % NSET
            pbuf = (it - 1) % NSET
            col0 = s * NS
            U, rUU = Ub[tb], rU[tb]
            dg, rdg = DG[c % 2], rDG[c % 2]
            if s == 0:
                for tap, wt in enumerate((P["w0"], P["w1"], P["w2"], P["w3"])):
                    S.op("vector", TS(dg[:, tap, :], IDENTF, wt, None, ALU.mult), reads=[rC], writes=[rdg])
                S.op("vector", MEMSET(U[:, 0:3], 0.0), writes=[rUU])
            else:
                S.op("vector", COPY(U[:, 0:3], Ub[pbuf][:, NS:NS + 3]), reads=[rU[pbuf]], writes=[rUU])
            for tt in range(2):
                pb, rb = nextbank()
                for kc in range(8):
                    S.op("tensor", MM(pb[:, :], WU[wb][:, kc, :], XNT[:, kc, col0 + tt * 512:col0 + (tt + 1) * 512], kc == 0, kc == 7),
                         reads=[rWU[wb]] + xr(col0 + tt * 512, 512), writes=[rb])
                S.op("vector", COPY(U[:, 3 + tt * 512:3 + (tt + 1) * 512], pb[:, :]), reads=[rb], writes=[rUU])
                if s == 3 and tt == 1:
                    S.op("vector", COPY(CONVP[:, c, :], pb[:, 509:512]), reads=[rb], writes=[rOUTS])

        def stageA2(it):
            c, s = it // 4, it % 4
            P = prm_of(c)
            tb = it % NSET
            U, rUU = Ub[tb], rU[tb]
            dg, rdg = DG[c % 2], rDG[c % 2]
            cbanks = []
            for tt in range(2):
                pb, rb = nextbank()
                for tap in range(4):
                    S.op("tensor", MM(pb[:, :], dg[:, tap, :], U[:, tap + tt * 512:tap + (tt + 1) * 512], tap == 0, False),
                         reads=[rdg, rUU], writes=[rb])
                S.op("tensor", MM(pb[:, :], CBROW[0:1, c * 128:(c + 1) * 128], ONES[0:1, :], False, True), reads=[rCB], writes=[rb])
                sl = slice(tt * 512, (tt + 1) * 512)
                if tt == 0 or it % 2 == 1:
                    S.op("vector", COPY(UCB[tb][:, sl], pb[:, :]), reads=[rb], writes=[rUCB[tb]])
                else:
                    S.op("scalar", ACT(UCB[tb][:, sl], pb[:, :], AF.Copy), reads=[rb], writes=[rUCB[tb]])
                cbanks.append((pb, rb))
            for tt in range(2):
                sl = slice(tt * 512, (tt + 1) * 512)
                pb, rb = nextbank()
                S.op("tensor", MM(pb[:, :], WG[:, c, 0, :], UCB[tb][:, sl], True, True), reads=[rC, rUCB[tb]], writes=[rb])
                S.op("scalar", ACT(Rb[tb][:, sl], pb[:, :], AF.Tanh, bias=P["ba"], scale=0.5), reads=[rb, rC], writes=[rR[tb]])
                pb, rb = nextbank()
                S.op("tensor", MM(pb[:, :], WG[:, c, 1, :], UCB[tb][:, sl], True, True), reads=[rC, rUCB[tb]], writes=[rb])
                S.op("scalar", ACT(Ib[tb][:, sl], pb[:, :], AF.Tanh, bias=P["bx"], scale=0.5), reads=[rb, rC], writes=[rI[tb]])
                cpb, crb = cbanks[tt]
                S.op("vector", STT(Ib[tb][:, sl], Ib[tb][:, sl], 1.0, cpb[:, :], ALU.add, ALU.mult), reads=[crb, rI[tb]], writes=[rI[tb]])
            S.op("scalar", ACT(Ab[tb], Rb[tb], AF.Exp, scale=P["clh"], bias=P["clh"]), reads=[rR[tb], rC], writes=[rA[tb]])
            S.op("scalar", ACT(Rb[tb], Rb[tb], AF.Exp, scale=P["cl"], bias=P["cl"]), reads=[rR[tb], rC], writes=[rR[tb]])

        def stageB(it):
            c, s = it // 4, it % 4
            wb = c % 2
            tb = it % NSET
            pbuf = (it - 1) % NSET
            col0 = s * NS
            S.op("scalar", ACT(Rb[tb], Rb[tb], AF.Sqrt, scale=-1.0, bias=1.0), reads=[rR[tb]], writes=[rR[tb]])
            S.op("vector", TT(Ib[tb], Ib[tb], Rb[tb], ALU.mult), reads=[rI[tb], rR[tb]], writes=[rI[tb]])
            if s == 0:
                init, rds = 0.0, [rA[tb], rI[tb]]
            elif s == 2:
                init = INITS[:, c:c + 1]
                S.op("vector", TT(init, UC[pbuf][:, NS - 1:NS], PVT[:, 0:1], ALU.mult), reads=[rUC[pbuf], rC], writes=[rSM])
                rds = [rA[tb], rI[tb], rSM]
            else:
                init, rds = UC[pbuf][:, NS - 1:NS], [rA[tb], rI[tb], rUC[pbuf]]
            S.op("vector", SCAN(UC[tb], Ab[tb], Ib[tb], init), reads=rds, writes=[rUC[tb]])
            if s >= 2:
                for tt in range(2):
                    sl = slice(tt * 512, (tt + 1) * 512)
                    pb, rb = nextbank()
                    for kc in range(8):
                        S.op("tensor", MM(pb[:, :], WZ[wb][:, kc, :], XNT[:, kc, col0 + tt * 512:col0 + (tt + 1) * 512], kc == 0, kc == 7),
                             reads=[rWZ[wb]] + xr(col0 + tt * 512, 512), writes=[rb])
                    S.op("scalar", ACT(Rb[tb][:, sl], pb[:, :], AF.Tanh, scale=0.5), reads=[rb], writes=[rR[tb]])
                    S.op("vector", STT(Rb[tb][:, sl], Rb[tb][:, sl], 1.0, pb[:, :], ALU.add, ALU.mult), reads=[rb, rR[tb]], writes=[rR[tb]])
                S.op("vector", STT(HZT[:, c, (s - 2) * NS:(s - 1) * NS], Rb[tb], 0.25, UC[tb], ALU.mult, ALU.mult), reads=[rUC[tb], rR[tb]], writes=[rHZT[c]])
            if s == 3:
                S.op("vector", TS(HOUT[:, c:c + 1], UC[tb][:, NS - 1:NS], 0.5, None, ALU.mult), reads=[rUC[tb]], writes=[rOUTS])
                pb, rb = nextbank()
                for kc in range(8):
                    S.op("tensor", MM(pb[:, 0:4], WU[wb][:, kc, :], XNT[:, kc, 4096:4100], kc == 0, kc == 7), reads=[rWU[wb], rXNT[32]], writes=[rb])
                S.op("vector", COPY(SUC[:, c * 4:(c + 1) * 4], pb[:, 0:4]), reads=[rb], writes=[rSUC])
                pb, rb = nextbank()
                for kc in range(8):
                    S.op("tensor", MM(pb[:, 0:4], WZ[wb][:, kc, :], XNT[:, kc, 4096:4100], kc == 0, kc == 7), reads=[rWZ[wb], rXNT[32]], writes=[rb])
                S.op("scalar", ACT(SZS[:, c * 4:(c + 1) * 4], pb[:, 0:4], AF.Tanh, scale=0.5), reads=[rb], writes=[rSZS])
                S.op("vector", STT(SZS[:, c * 4:(c + 1) * 4], SZS[:, c * 4:(c + 1) * 4], 1.0, pb[:, 0:4], ALU.add, ALU.mult), reads=[rb, rSZS], writes=[rSZS])
                S.op("vector", TS(SZS[:, c * 4:(c + 1) * 4], SZS[:, c * 4:(c + 1) * 4], 0.5, None, ALU.mult), reads=[rSZS], writes=[rSZS])

        SW = A.f32(64)
        SWB = A.bf16(8)
        rSW = Res("sw")
        def sample_step(c):
                usl = SUC[:, c * 4:(c + 1) * 4]
                uc = SW[:, 0:4]
                st_ = SSTT3[:, c, 0:12].rearrange("p (b t) -> p b t", b=4)
                S.op("vector", TS(uc, usl, PRM[:, 24 + c:25 + c], PRM[:, 32 + c:33 + c], ALU.mult, ALU.add), reads=[rSUC, rC], writes=[rSW])
                for tap in range(3):
                    S.op("vector", STT(uc, st_[:, :, tap], PRM[:, tap * 8 + c:tap * 8 + c + 1], uc, ALU.mult, ALU.add), reads=[rSST, rC, rSW], writes=[rSW])
                S.op("vector", COPY(SWB[:, 0:4], uc), reads=[rSW], writes=[rSW])
                pb, rb = nextbank()
                S.op("tensor", MM(pb[:, 0:4], WG[:, c, 0, :], SWB[:, 0:4], True, True), reads=[rC, rSW], writes=[rb])
                S.op("tensor", MM(pb[:, 4:8], WG[:, c, 1, :], SWB[:, 0:4], True, True), reads=[rC, rSW], writes=[rb])
                r_ = SW[:, 4:8]
                i_ = SW[:, 8:12]
                a_ = SW[:, 12:16]
                S.op("scalar", ACT(r_, pb[:, 0:4], AF.Tanh, bias=HB[:, c:c + 1], scale=0.5), reads=[rb, rC], writes=[rSW])
                S.op("scalar", ACT(i_, pb[:, 4:8], AF.Tanh, bias=HB[:, 8 + c:9 + c], scale=0.5), reads=[rb, rC], writes=[rSW])
                S.op("scalar", ACT(a_, r_, AF.Exp, scale=CLT[:, c:c + 1], bias=CLT[:, c:c + 1]), reads=[rSW, rC], writes=[rSW])
                S.op("scalar", ACT(r_, r_, AF.Exp, scale=CLT[:, 8 + c:9 + c], bias=CLT[:, 8 + c:9 + c]), reads=[rSW, rC], writes=[rSW])
                S.op("scalar", ACT(r_, r_, AF.Sqrt, scale=-1.0, bias=1.0), reads=[rSW], writes=[rSW])
                S.op("vector", STT(i_, i_, 1.0, r_, ALU.add, ALU.mult), reads=[rSW], writes=[rSW])
                S.op("vector", STT(i_, i_, 0.5, uc, ALU.mult, ALU.mult), reads=[rSW], writes=[rSW])
                S.op("vector", TT(a_, a_, SSTT3[:, c, 12:16], ALU.mult), reads=[rSW, rSST], writes=[rSW])
                S.op("vector", TT(SH[:, c * 4:(c + 1) * 4], a_, i_, ALU.add), reads=[rSW], writes=[rSH])
                S.op("vector", TT(HZT[:, c, 2048:2052], SH[:, c * 4:(c + 1) * 4], SZS[:, c * 4:(c + 1) * 4], ALU.mult), reads=[rSH, rSZS], writes=[rHZTs])

        NIT = 32
        load_w2(1)
        WQKV_first = arena_t[:, SCR0:SCR0 + 3072].bitcast(BF16).rearrange("p (k n) -> p k n", k=8)
        rWQ2 = [Res("wq0"), Res("wq1")]
        stageA(0)
        stageA(1)
        stageA2(0)
        for it in range(NIT):
            if it + 2 < NIT:
                stageA(it + 2)
            if it + 1 < NIT:
                stageA2(it + 1)
            stageB(it)
            if it % 4 == 3:
                if it // 4 + 2 < 8:
                    load_w2(it // 4 + 2)
                sample_step(it // 4)
            if it == 13:
                S.dma("gpsimd", DMA(WQKV_first, wqkv[0].rearrange("p (k n) -> p k n", k=8)), rWQ2[0],
                      writes=[rWQ2[0], rT, rG, rXT[0], rXT[1]])
        S.dma("sync", DMA(convs_u, SUC), rSUC, reads=[rSUC])
        S.dma("sync", DMA(hs, SH), rSH, reads=[rSH])
        S.dma("sync", DMA(hp, HOUT), rOUTS, reads=[rOUTS])
        S.dma("sync", DMA(convp, CONVP.rearrange("p c t -> p (c t)")), rOUTS, reads=[rOUTS])
        S.barrier()
        rX = Res("xnt_all")
        if upto == 2:
            if dbgfn is not None:
                dbgfn(locals())
                S.barrier()
            S.emit()
            return nc

        A.top = SCR0
        WQKV2 = [A.bf16(8 * 768).rearrange("p (k n) -> p k n", k=8) for _ in range(2)]
        assert A.top == SCR0 + 6144
        NRB = 6
        KT = A.bf16(2 * NRB * 128).rearrange("p (h b n) -> p h b n", h=2, b=NRB)
        Vflat = A.bf16(NRB * 4 * 66)
        Vb = Vflat.rearrange("p (b h e) -> p b h e", b=NRB, h=4)
        Vm = Vflat.rearrange("p (m e) -> p m e", e=66)
        QT2 = [A.bf16(2 * 2048).rearrange("p (h n) -> p h n", h=2) for _ in range(2)]
        ET2 = [A.f32(3 * 512).rearrange("p (k n) -> p k n", k=3) for _ in range(2)]
        NR = 3
        KVF = [A.f32(512) for _ in range(2)] + [None]
        KB = [A.bf16(256) for _ in range(NR)]
        EX = [A.f32(512) for _ in range(NR)]
        PT = [A.bf16(512) for _ in range(NR)]
        OS = [A.f32(260) for _ in range(NR)]
        CK = [A.f32(512) for _ in range(2)]
        SQKV = A.f32(768)
        SPR = A.f32(260)
        SPV = A.f32(260)
        slot_of = {}
        rKT = [Res(f"kt{i}") for i in range(32)]
        rV = [Res(f"v{i}") for i in range(32)]
        rVones = Res("vones")
        rQT2 = [[[Res(f"qt{q}{h}{t}") for t in range(4)] for h in range(2)] for q in range(2)]
        rET2 = [Res("et0"), Res("et1")]
        rKVF = [Res("kvf0"), Res("kvf1")]
        rKB = [Res("kb%d" % i) for i in range(3)]
        rEX = [Res("ex%d" % i) for i in range(3)]
        rPT = [Res("pt%d" % i) for i in range(3)]
        rOS = [Res("os%d" % i) for i in range(3)]
        rCK = [Res("ck0"), Res("ck1")]
        rSQKV, rSPR, rSPV = Res("sqkv"), Res("spr"), Res("spv")
        rOSC = Res("osc")
        cnt = {"kvf": 0, "kb": 0, "ex": 0, "os": 0, "ck": 0, "blk": 0}
        S.op("vector", MEMSET(Vm[:, :, 64:65], 1.0), writes=[rVones] + rV[:NRB])
        for g in range(3):
            d = DIL[g]
            nb = 16 // d
            for hh in range(2):
                sp = g * 2 + hh
                if sp + 1 < 6:
                    S.dma("gpsimd", DMA(WQKV2[(sp + 1) % 2], wqkv[sp + 1].rearrange("p (k n) -> p k n", k=8)), rWQ2[(sp + 1) % 2], writes=[rWQ2[(sp + 1) % 2]])
                WQKV = WQKV2[sp % 2]
                WQ = WQKV[:, :, 0:256]
                WKV = WQKV[:, :, 256:768]
                rWQ = rWQ2[sp % 2]
                rWKV = rWQ
                QT, rQT = QT2[sp % 2], rQT2[sp % 2]
                ET, rET = ET2[sp % 2], rET2[sp % 2]
                for kind in range(3):
                    S.dma("sync", DMA(ET[:, kind, :], etab[g, kind][:, hh * 512:(hh + 1) * 512]), rET, writes=[rET])
                for hpp in range(2):
                    for tt in range(4):
                        pb, rb = nextbank()
                        for kc in range(8):
                            S.op("tensor", MM(pb[:, :], WQ[:, kc, hpp * 128:(hpp + 1) * 128], XNT[:, kc, 2048 + tt * 512:2048 + (tt + 1) * 512], kc == 0, kc == 7),
                                 reads=[rWQ, rX], writes=[rb])
                        md = 512 // d
                        qdst = QT[:, hpp, :].rearrange("p (r m) -> p r m", r=d)[:, :, tt * md:(tt + 1) * md]
                        S.op("scalar", ACT(qdst, pb[:, :].rearrange("p (m r) -> p r m", r=d), AF.Copy, scale=0.125), reads=[rb], writes=[rQT[hpp][tt]])
                pb, rb = nextbank()
                for kc in range(8):
                    S.op("tensor", MM(pb[0:4, 0:256], XNT[:, kc, 4096:4100], WQ[:, kc, :], kc == 0, kc == 7), reads=[rWQ, rX], writes=[rb])
                S.op("vector", COPY(SQKV[0:4, 0:256], pb[0:4, 0:256]), reads=[rb], writes=[rSQKV])
                pb, rb = nextbank()
                for kc in range(8):
                    S.op("tensor", MM(pb[0:4, :], XNT[:, kc, 4096:4100], WKV[:, kc, :], kc == 0, kc == 7), reads=[rWKV, rX], writes=[rb])
                S.op("vector", COPY(SQKV[0:4, 256:768], pb[0:4, :]), reads=[rb], writes=[rSQKV])
                S.dma("sync", DMA(kvs[g, :, :, hh * 256:(hh + 1) * 256], SQKV[0:4, 256:768].rearrange("p (t n) -> p t n", t=2)), rSQKV, reads=[rSQKV])

                def produce(rho, n):
                    bi = cnt["blk"] % NRB
                    cnt["blk"] += 1
                    slot_of[(g, hh, rho, n)] = bi
                    c0, stp = block_cols(g, rho, n)
                    pb, rb = nextbank()
                    for kc in range(8):
                        S.op("tensor", MM(pb[:, :], XNT[:, kc, SL(c0, 128, stp)], WKV[:, kc, :], kc == 0, kc == 7),
                             reads=[rX, rWKV], writes=[rb])
                    if n == nb - 1:
                        fb = cnt["kvf"] % 2
                        cnt["kvf"] += 1
                        S.op("scalar", ACT(KVF[fb], pb[:, :], AF.Copy), reads=[rb], writes=[rKVF[fb]])
                        dst = kvo[g][SL(rho, 128, d), :, hh * 256:(hh + 1) * 256]
                        S.dma("sync", DMA(dst, KVF[fb].rearrange("p (t n) -> p t n", t=2)), rKVF[fb], reads=[rKVF[fb]])
                    kb = cnt["kb"] % NR
                    cnt["kb"] += 1
                    S.op("vector", COPY(KB[kb], pb[:, 0:256]), reads=[rb], writes=[rKB[kb]])
                    S.op("vector", COPY(Vb[:, bi, :, 0:64], pb[:, 256:512].rearrange("p (h e) -> p h e", h=4)), reads=[rb, rVones], writes=[rV[bi]])
                    pb2, rb2 = nextbank()
                    pbb = pb2[:, :].bitcast(BF16)
                    for hpp in range(2):
                        S.op("tensor", TR(pbb[:, hpp * 128:(hpp + 1) * 128], KB[kb][:, hpp * 128:(hpp + 1) * 128], IDENT), reads=[rKB[kb], rC], writes=[rb2])
                    S.op("scalar", ACT(KT[:, :, bi, :], pbb[:, 0:256].rearrange("p (h n) -> p h n", h=2), AF.Copy), reads=[rb2], writes=[rKT[bi]])

                def attend(rho, n):
                    q0 = rho + d * 128 * n
                    qb0 = rho * (2048 // d) + 128 * n
                    qr = [rQT[0][t] for t in range(4)] + [rQT[1][t] for t in range(4)]
                    po, ro = nextbank()
                    for bk, nn in enumerate((n - 1, n)):
                        bi = slot_of[(g, hh, rho, nn)]
                        pbs = [nextbank(), nextbank()]
                        for h4 in range(4):
                            hpp, hf = h4 // 2, h4 % 2
                            pb, rb = pbs[hf]
                            S.op("tensor", MM(pb[:, hpp * 128:(hpp + 1) * 128], KT[hf * 64:(hf + 1) * 64, hpp, bi, :],
                                              QT[hf * 64:(hf + 1) * 64, hpp, qb0:qb0 + 128], True, True),
                                 reads=[rKT[bi]] + qr, writes=[rb])
                        xb = cnt["ex"] % NR
                        cnt["ex"] += 1
                        EX4 = EX[xb].rearrange("p (a f n) -> p a f n", a=2, f=2)
                        for hf in range(2):
                            pb, rb = pbs[hf]
                            S.op("scalar", ACT(EX4[:, :, hf, :], pb[:, 0:256].rearrange("p (a n) -> p a n", a=2), AF.Exp), reads=[rb], writes=[rEX[xb]])
                        kind = 0 if bk == 1 else (2 if n == 0 else 1)
                        S.op("vector", TT(PT[xb], EX[xb], ET[:, kind, :], ALU.mult), reads=[rEX[xb], rET], writes=[rPT[xb]])
                        for h4 in range(4):
                            S.op("tensor", MM(po[:, h4 * 65:(h4 + 1) * 65], PT[xb][:, h4 * 128:(h4 + 1) * 128], Vb[:, bi, h4, 0:65],
                                              bk == 0 and h4 == 0, bk == 1, skip=True),
                                 reads=[rPT[xb], rV[bi], rVones], writes=[ro])
                    ob = cnt["os"] % NR
                    cnt["os"] += 1
                    S.op("vector", COPY(OS[ob], po[:, 0:260]), reads=[ro], writes=[rOS[ob]])
                    S.dma("sync", DMA(osc[g][SL(q0, 128, d), hh, :], OS[ob]), rOS[ob], reads=[rOS[ob]], writes=[rOSC])

                for rho in range(d):
                    for n in range(-1, nb):
                        produce(rho, n)
                        if n >= 0:
                            attend(rho, n)
                po, ro = nextbank()
                for b in range(4):
                    cb_ = cnt["ck"] % 2
                    cnt["ck"] += 1
                    S.dma("sync", DMA(CK[cb_].rearrange("p (t n) -> p t n", t=2), cache[g][b, SL(0, 128, d), :, hh * 256:(hh + 1) * 256]), rCK[cb_], writes=[rCK[cb_]])
                    pb, rb = nextbank()
                    S.op("tensor", MM(pb[:, 0:256], SELQ[0:4, b * 128:(b + 1) * 128], SQKV[0:4, 0:256], True, True), reads=[rC, rSQKV], writes=[rb])
                    S.op("vector", TT(SPR[:, 0:256], CK[cb_][:, 0:256], pb[:, 0:256], ALU.mult), reads=[rCK[cb_], rb], writes=[rSPR])
                    sc = SMALL[:, 40:44]
                    S.op("vector", REDUCE(sc, SPR[:, 0:256].rearrange("p (h e) -> p h e", h=4)), reads=[rSPR], writes=[rSM])
                    S.op("vector", STT(sc, sc, 0.125, SBIAS[:, g * 8 + hh * 4:g * 8 + hh * 4 + 4], ALU.mult, ALU.add), reads=[rSM, rC], writes=[rSM])
                    SPV3 = SPV.rearrange("p (h e) -> p h e", h=4)
                    S.op("scalar", ACT(SPV3[:, :, 64], sc, AF.Exp), reads=[rSM], writes=[rSPV])
                    S.op("vector", TT(SPV3[:, :, 0:64], CK[cb_][:, 256:512].rearrange("p (h e) -> p h e", h=4),
                                      SPV3[:, :, 64:65].to_broadcast([128, 4, 64]), ALU.mult), reads=[rCK[cb_], rSPV], writes=[rSPV])
                    S.op("tensor", MM(po[0:4, 0:260], SELO[:, b * 4:(b + 1) * 4], SPV, b == 0, b == 3), reads=[rC, rSPV], writes=[ro])
                if sub == 5:
                    S.barrier()
                    S.emit()
                    return nc
                q_s, k_s, v_s = SQKV[0:4, 0:256], SQKV[0:4, 256:512], SQKV[0:4, 512:768]
                S.op("vector", TT(SPR[0:4, 0:256], q_s, k_s, ALU.mult), reads=[rSQKV], writes=[rSPR])
                sc = SMALL[0:4, 44:48]
                S.op("vector", REDUCE(sc, SPR[0:4, 0:256].rearrange("p (h e) -> p h e", h=4)), reads=[rSPR], writes=[rSM])
                SPVn = SPR[0:4, 0:260].rearrange("p (h e) -> p h e", h=4)
                S.op("scalar", ACT(SPVn[:, :, 64], sc, AF.Exp, scale=0.125), reads=[rSM], writes=[rSPR])
                S.op("vector", TT(SPVn[:, :, 0:64], v_s.rearrange("p (h e) -> p h e", h=4), SPVn[:, :, 64:65].to_broadcast([4, 4, 64]), ALU.mult),
                     reads=[rSQKV, rSPR], writes=[rSPR])
                sn = SNUM[0:4, hh * 260:(hh + 1) * 260]
                S.op("vector", TT(sn, sn, SPR[0:4, 0:260], ALU.add), reads=[rSPR, rSNUM], writes=[rSNUM])
                S.op("vector", TT(sn, sn, po[0:4, 0:260], ALU.add), reads=[ro, rSNUM], writes=[rSNUM])
        S.barrier()
        if upto == 3:
            if dbgfn is not None:
                dbgfn(locals())
                S.barrier()
            S.emit()
            return nc

        A.top = SCR0
        A.reserve_top = 4096
        WOUT = arena_t[:, AW - 4096:AW].bitcast(BF16).rearrange("p (k n) -> p k n", k=8)
        rWOUT = Res("wout")
        for kq in range(4):
            S.dma("gpsimd", DMA(WOUT[:, 2 * kq:2 * kq + 2, :], wout[:, kq * 2048:(kq + 1) * 2048].rearrange("p (k n) -> p k n", k=2)), rWOUT, writes=[rWOUT])
        AZT = A.bf16(4 * FW).rearrange("p (k n) -> p k n", k=4)
        SCR1 = A.top
        SZA = A.f32(4 * FW).rearrange("p (k n) -> p k n", k=4)
        WZA = [A.bf16(1024).rearrange("p (k n) -> p k n", k=8) for _ in range(2)]
        NO = 4
        OT = [A.f32(3 * 520).rearrange("p (g n) -> p g n", g=3) for _ in range(NO)]
        MG = [A.f32(512) for _ in range(NO)]
        rSZA = [Res(f"sza{i}") for i in range(4)]
        rWZA = [Res("wza0"), Res("wza1")]
        rOT = [Res("ot%d" % i) for i in range(4)]
        rMG = [Res("mg%d" % i) for i in range(4)]
        rAZT = Res("azt")
        ttiles = [(2048 + t * 512, t * 512, 512) for t in range(4)] + [(4096, 2048, 4)]
        for c4 in range(4):
            wb = c4 % 2
            S.dma("gpsimd", DMA(WZA[wb], win[52 + c4].rearrange("p (k n) -> p k n", k=8)), rWZA[wb], writes=[rWZA[wb]])
            for (xc, fc, nn) in ttiles:
                pb, rb = nextbank()
                for kc in range(8):
                    S.op("tensor", MM(pb[:, 0:nn], WZA[wb][:, kc, :], XNT[:, kc, xc:xc + nn], kc == 0, kc == 7), reads=[rWZA[wb], rX], writes=[rb])
                S.op("scalar", ACT(SZA[:, c4, fc:fc + nn], pb[:, 0:nn], AF.Silu), reads=[rb], writes=[rSZA[c4]])

        def load_ot(t):
            b = t % NO
            for g in range(3):
                S.dma("sync", DMA(OT[b][:, g, :], osc[g][t * 128:(t + 1) * 128].rearrange("p h n -> p (h n)")), rOT[b], writes=[rOT[b]])

        for t0 in range(NO - 1):
            load_ot(t0)
        for t in range(17):
            b = t % NO
            if t < 16:
                if t + NO - 1 < 16:
                    load_ot(t + NO - 1)
                np_ = 128
                s1 = OT[b][:, 0, :]
                S.op("vector", TT(s1, s1, OT[b][:, 1, :], ALU.add), reads=[rOT[b]], writes=[rOT[b]])
                S.op("vector", TT(s1, s1, OT[b][:, 2, :], ALU.add), reads=[rOT[b]], writes=[rOT[b]])
                rsrc = rOT[b]
                fc = t * 128
            else:
                np_ = 4
                s1 = SNUM[0:4, :]
                rsrc = rSNUM
                fc = 2048
            s3 = s1[0:np_, :].rearrange("p (h e) -> p h e", h=8)
            rd = SMALL[0:np_, 48:56]
            S.op("vector", RECIP(rd, s3[:, :, 64]), reads=[rsrc], writes=[rSM])
            S.op("vector", TT(MG[b][0:np_, :].rearrange("p (h e) -> p h e", h=8), s3[:, :, 0:64], rd.unsqueeze(2).to_broadcast([np_, 8, 64]), ALU.mult),
                 reads=[rsrc, rSM], writes=[rMG[b]])
            pb, rb = nextbank()
            for c4 in range(4):
                S.op("tensor", TR(pb[:, c4 * np_:(c4 + 1) * np_], MG[b][0:np_, c4 * 128:(c4 + 1) * 128], IDENTF[0:np_, 0:np_]), reads=[rMG[b], rC], writes=[rb])
            S.op("vector", TT(AZT[:, :, fc:fc + np_], pb[:, 0:4 * np_].rearrange("p (k n) -> p k n", k=4), SZA[:, :, fc:fc + np_], ALU.mult),
                 reads=[rb] + rSZA, writes=[rAZT])
        S.barrier()
        if upto == 4:
            if dbgfn is not None:
                dbgfn(locals())
                S.barrier()
            S.emit()
            return nc

        A.top = SCR1
        MIXT = A.bf16(8 * FW).rearrange("p (k n) -> p k n", k=8)
        WGR = [A.bf16(1024).rearrange("p (k n) -> p k n", k=8) for _ in range(2)]
        WGA = [A.bf16(1024).rearrange("p (k n) -> p k n", k=8) for _ in range(2)]
        WRO = [A.bf16(1024).rearrange("p (k n) -> p k n", k=8) for _ in range(2)]
        WAO = [A.bf16(512).rearrange("p (k n) -> p k n", k=4) for _ in range(2)]
        SGR = [A.f32(512) for _ in range(2)]
        SGA = [A.f32(512) for _ in range(2)]
        rW5 = [Res("w5_0"), Res("w5_1")]
        rSGR = [Res("sgr0"), Res("sgr1")]
        rSGA = [Res("sga0"), Res("sga1")]
        rMIX = [Res(f"mix{j}") for j in range(8)]

        def load_w5(j):
            b = j % 2
            S.dma("gpsimd", DMA(WGR[b], win[56 + j].rearrange("p (k n) -> p k n", k=8)), rW5[b], writes=[rW5[b]])
            S.dma("gpsimd", DMA(WGA[b], win[64 + j].rearrange("p (k n) -> p k n", k=8)), rW5[b], writes=[rW5[b]])
            S.dma("gpsimd", DMA(WRO[b], wro[j].rearrange("p (k n) -> p k n", k=8)), rW5[b], writes=[rW5[b]])
            S.dma("gpsimd", DMA(WAO[b], wao[j].rearrange("p (k n) -> p k n", k=4)), rW5[b], writes=[rW5[b]])

        load_w5(0)
        k5 = 0
        for j in range(8):
            wb = j % 2
            if j + 1 < 8:
                load_w5(j + 1)
            for (xc, fc, nn) in ttiles:
                sb = k5 % 2
                k5 += 1
                p1, r1 = nextbank()
                for kc in range(8):
                    S.op("tensor", MM(p1[:, 0:nn], WGR[wb][:, kc, :], XNT[:, kc, xc:xc + nn], kc == 0, kc == 7), reads=[rW5[wb], rX], writes=[r1])
                S.op("scalar", ACT(SGR[sb][:, 0:nn], p1[:, 0:nn], AF.Sigmoid), reads=[r1], writes=[rSGR[sb]])
                p2, r2 = nextbank()
                for kc in range(8):
                    S.op("tensor", MM(p2[:, 0:nn], WGA[wb][:, kc, :], XNT[:, kc, xc:xc + nn], kc == 0, kc == 7), reads=[rW5[wb], rX], writes=[r2])
                S.op("scalar", ACT(SGA[sb][:, 0:nn], p2[:, 0:nn], AF.Sigmoid), reads=[r2], writes=[rSGA[sb]])
                p3, r3 = nextbank()
                for kc in range(8):
                    S.op("tensor", MM(p3[:, 0:nn], WRO[wb][:, kc, :], HZT[:, kc, fc:fc + nn], kc == 0, kc == 7), reads=[rW5[wb]], writes=[r3])
                S.op("vector", TT(SGR[sb][:, 0:nn], SGR[sb][:, 0:nn], p3[:, 0:nn], ALU.mult), reads=[r3, rSGR[sb]], writes=[rSGR[sb]])
                p4, r4 = nextbank()
                for kc in range(4):
                    S.op("tensor", MM(p4[:, 0:nn], WAO[wb][:, kc, :], AZT[:, kc, fc:fc + nn], kc == 0, kc == 3), reads=[rW5[wb]], writes=[r4])
                S.op("vector", TT(SGA[sb][:, 0:nn], SGA[sb][:, 0:nn], p4[:, 0:nn], ALU.mult), reads=[r4, rSGA[sb]], writes=[rSGA[sb]])
                S.op("vector", TT(MIXT[:, j, fc:fc + nn], SGR[sb][:, 0:nn], SGA[sb][:, 0:nn], ALU.add), reads=[rSGR[sb], rSGA[sb]], writes=[rMIX[j]])
        S.barrier()
        if upto == 5:
            if dbgfn is not None:
                dbgfn(locals())
                S.barrier()
            S.emit()
            return nc

        A.top = 0
        N6 = 4
        XT6 = [A.f32(1024) for _ in range(N6)]
        YT = [A.f32(1024) for _ in range(N6)]
        SQ6 = A.f32(512)
        SM6 = A.f32(16)
        GPOST = A.f32(1024)
        rGP = Res("gpost")
        S.dma("sync", DMA(GPOST, gpost.partition_broadcast(128)), rGP, writes=[rGP])
        rXT6 = [Res("xt6_%d" % i) for i in range(4)]
        rYT = [Res("yt%d" % i) for i in range(4)]
        rSQ6 = Res("sq6")
        t6 = [(xm, t * 128, y, t * 128, t * 128, 128) for t in range(16)] + [(xsd, 0, ys, 0, 2048, 4)]

        def load_x6(i):
            src, r0, _, _, _, np_ = t6[i]
            b = i % N6
            S.dma("sync", DMA(XT6[b][0:np_, :], src[r0:r0 + np_, :]), rXT6[b], writes=[rXT6[b]])

        for i0 in range(N6 - 1):
            load_x6(i0)
        for i, (src, r0, dst, d0, fc, np_) in enumerate(t6):
            b = i % N6
            if i + N6 - 1 < len(t6):
                load_x6(i + N6 - 1)
            banks = []
            for hf in range(2):
                pb, rb = nextbank()
                for kc in range(8):
                    S.op("tensor", MM(pb[0:np_, :], MIXT[:, kc, fc:fc + np_], WOUT[:, kc, hf * 512:(hf + 1) * 512], kc == 0, kc == 7), reads=[rWOUT], writes=[rb])
                banks.append((pb, rb))
            rss = Res("ss6")
            ssa = SM6[0:np_, 4 * b:4 * b + 1]
            ssb = SM6[0:np_, 4 * b + 1:4 * b + 2]
            rs = SM6[0:np_, 4 * b + 2:4 * b + 3]
            S.op("scalar", ACT(SQ6[0:np_, :], banks[0][0][0:np_, :], AF.Square, accum_out=ssa), reads=[banks[0][1]], writes=[rSQ6, rss])
            S.op("scalar", ACT(SQ6[0:np_, :], banks[1][0][0:np_, :], AF.Square, accum_out=ssb), reads=[banks[1][1], rss], writes=[rSQ6, rss])
            S.op("vector", TT(rs, ssa, ssb, ALU.add), reads=[rss], writes=[rss])
            S.op("scalar", ACT(rs, rs, AF.Sqrt, scale=1.0 / 1024, bias=EPS), reads=[rss], writes=[rss])
            S.op("vector", RECIP(rs, rs), reads=[rss], writes=[rss])
            for hf in range(2):
                sl = slice(hf * 512, (hf + 1) * 512)
                S.op("vector", STT(YT[b][0:np_, sl], banks[hf][0][0:np_, :], rs, GPOST[0:np_, sl], ALU.mult, ALU.mult),
                     reads=[banks[hf][1], rss, rGP], writes=[rYT[b]])
            S.op("vector", TT(YT[b][0:np_, :], YT[b][0:np_, :], XT6[b][0:np_, :], ALU.add), reads=[rYT[b], rXT6[b]], writes=[rYT[b]])
            S.dma("sync", DMA(dst[d0:d0 + np_, :], YT[b][0:np_, :]), rYT[b], reads=[rYT[b]])
        S.barrier()
        assert S.nsem <= 100, S.nsem
        S.emit()
    return nc


def _alibi_slopes():
    n = 24
    return (2.0 ** (-8.0 * np.arange(1, n + 1, dtype=np.float64) / n)).reshape(3, 8)


def _tables(half):
    sl = _alibi_slopes()
    i = np.arange(128)
    et = np.zeros((3, 3, 128, 8, 128), np.float32)
    for g in range(3):
        d = DIL[g]
        for h in range(8):
            s = sl[g, h]
            steps_cur = (i[None, :] - i[:, None]).astype(np.float64)
            cur = np.where(steps_cur >= 0, np.exp(-s * d * steps_cur), 0.0)
            steps_prev = 128 + steps_cur
            prev = np.where(steps_prev <= 128, np.exp(-s * d * steps_prev), 0.0)
            et[g, 0, :, h, :] = cur
            et[g, 1, :, h, :] = prev
            et[g, 2, :, h, :] = prev * float(half)
    sb = np.zeros((128, 3, 8), np.float32)
    m = np.arange(128)
    for g in range(3):
        d = DIL[g]
        for h in range(8):
            sb[:, g, h] = -sl[g, h] * d * (128 - m)
    return et.reshape(3, 3, 128, 1024), sb.reshape(128, 24)


_NC_CACHE = {}


def prep_inputs(x_prompt, x_sample, state_conv, state_h, cache_kv_w128, cache_kv_w512, cache_kv_w2048,
                norm_pre, norm_post, w_in, conv_w, conv_b, lru_w_a, lru_b_a, lru_w_x, lru_b_x, lru_lambda,
                w_rnn_out, w_attn_out, w_out):
    f = lambda a: np.ascontiguousarray(np.asarray(a, dtype=np.float32))
    x_prompt, x_sample, state_conv, state_h = f(x_prompt), f(x_sample), f(state_conv), f(state_h)
    caches = [f(cache_kv_w128), f(cache_kv_w512), f(cache_kv_w2048)]
    w_in = f(w_in)[0]
    def tile_w(w, nchunk, kcs):
        return np.ascontiguousarray(w.reshape(kcs, 128, nchunk, 128).transpose(2, 1, 0, 3).reshape(nchunk, 128, kcs * 128))
    win = tile_w(w_in, 72, 8)
    w4 = w_in.reshape(8, 128, 9216)
    parts = []
    for g in range(3):
        for hh in range(2):
            cs = g * 512 + hh * 256
            parts.append(np.concatenate([w4[:, :, 2048 + cs:2048 + cs + 256], w4[:, :, 3584 + cs:3584 + cs + 256],
                                         w4[:, :, 5120 + cs:5120 + cs + 256]], axis=2).transpose(1, 0, 2).reshape(128, 6144))
    wqkv = np.ascontiguousarray(np.stack(parts))
    wro = tile_w(f(w_rnn_out)[0], 8, 8)
    wao = tile_w(f(w_attn_out)[0], 8, 4)
    wout = np.ascontiguousarray(f(w_out)[0].reshape(8, 128, 1024).transpose(1, 0, 2).reshape(128, 8192))
    wga = np.stack([np.ascontiguousarray(w.reshape(8, 2, 64, 64).transpose(1, 2, 0, 3).reshape(2, 64, 512))
                    for w in (f(lru_w_a)[0], f(lru_w_x)[0])])
    prm = np.concatenate([f(conv_w)[0].reshape(32, 128), f(conv_b)[0].reshape(8, 128), f(lru_b_a)[0].reshape(8, 128),
                          f(lru_b_x)[0].reshape(8, 128), f(lru_lambda)[0].reshape(8, 128)], axis=0)
    ident = np.eye(128, dtype=np.float32)
    selq = np.zeros((4, 4, 128), np.float32)
    selo = np.zeros((128, 4, 4), np.float32)
    for b in range(4):
        selq[b, b, :] = 1.0
        selo[:, b, b] = 1.0
    selq = selq.reshape(4, 512)
    selo = selo.reshape(128, 16)
    tabs = [_tables(0), _tables(1)]
    in_maps = []
    for c in range(8):
        b, half = c // 2, c % 2
        xm = x_prompt[b, half * 2048:(half + 1) * 2048]
        xp = x_prompt[b, 0:2048] if half == 1 else np.zeros((2048, 1024), np.float32)
        sl = slice(4 * c, 4 * c + 4)
        sst = np.concatenate([state_conv[0, sl].reshape(12, 1024), state_h[0, sl]], axis=0)
        in_maps.append({
            "xm": np.ascontiguousarray(xm), "xp": np.ascontiguousarray(xp),
            "pv": np.full((128, 1), float(half), np.float32),
            "xs": np.ascontiguousarray(x_sample[sl, 0, :]), "sst": np.ascontiguousarray(sst),
            "sconv": np.ascontiguousarray(state_conv[0, sl]),
            "c0": np.ascontiguousarray(caches[0][0, sl].reshape(4, 128, 2, 512)),
            "c1": np.ascontiguousarray(caches[1][0, sl].reshape(4, 512, 2, 512)),
            "c2": np.ascontiguousarray(caches[2][0, sl].reshape(4, 2048, 2, 512)),
            "win": win, "wqkv": wqkv, "wro": wro, "wao": wao, "wout": wout, "wga": wga, "prm": prm, "cbrow": f(conv_b).reshape(1, 1024),
            "gpre": f(norm_pre).reshape(1, 1024), "gpost": f(norm_post).reshape(1, 1024),
            "etab": tabs[half][0], "sbias": tabs[half][1], "ident": ident, "selq": selq, "selo": selo,
        })
    return in_maps


def kernel(**inputs):
    in_maps = prep_inputs(**inputs)
    if "nc" not in _NC_CACHE:
        _NC_CACHE["nc"] = build_program()
    nc = _NC_CACHE["nc"]
    res = run_bass_kernel_spmd(nc, in_maps, core_ids=list(range(8)))
    R = res.results
    yp = np.zeros((4, 4096, 1024), np.float32)
    ys = np.zeros((32, 1, 1024), np.float32)
    conv_p = np.zeros((1, 4, 3, 1024), np.float32)
    conv_s = np.zeros((1, 32, 3, 1024), np.float32)
    h_p = np.zeros((1, 4, 1024), np.float32)
    h_s = np.zeros((1, 32, 1024), np.float32)
    kvp = [np.zeros((1, 4, 128 * DIL[g], 2, 8, 64), np.float32) for g in range(3)]
    kvsm = [np.zeros((1, 32, 1, 2, 8, 64), np.float32) for g in range(3)]
    for c in range(8):
        b, half = c // 2, c % 2
        r = R[c]
        yp[b, half * 2048:(half + 1) * 2048] = r["y"]
        sl = slice(4 * c, 4 * c + 4)
        ys[sl, 0] = r["ys"]
        conv_s[0, sl, 0:2] = r["convs_s"]
        conv_s[0, sl, 2] = r["convs_u"].reshape(128, 8, 4).transpose(2, 1, 0).reshape(4, 1024)
        h_s[0, sl] = r["hs"].reshape(128, 8, 4).transpose(2, 1, 0).reshape(4, 1024)
        for g in range(3):
            kvsm[g][0, sl, 0] = r["kvs"][g].reshape(4, 2, 8, 64)
        if half == 1:
            conv_p[0, b] = r["convp"].reshape(128, 8, 3).transpose(2, 1, 0).reshape(3, 1024)
            h_p[0, b] = r["hp"].reshape(128, 8).transpose(1, 0).reshape(1024)
            for g in range(3):
                kvp[g][0, b] = r[f"kvo{g}"].reshape(128 * DIL[g], 2, 8, 64)
    return (yp, ys, conv_p, conv_s, h_p, h_s, kvp[0], kvsm[0], kvp[1], kvsm[1], kvp[2], kvsm[2])
```

```python
from contextlib import ExitStack
import numpy as np
import concourse.bass as bass
import concourse.mybir as mybir
from concourse.bass_utils import run_bass_kernel_spmd

F32 = mybir.dt.float32
BF16 = mybir.dt.bfloat16
AF = mybir.ActivationFunctionType
ALU = mybir.AluOpType
AX = mybir.AxisListType

NT = 2048
XW = 4100
FW = 2052
DIL = (1, 4, 16)
EPS = 1e-6


class Res:
    __slots__ = ("name", "lastw", "readers", "sem", "semcnt", "excl")

    def __init__(self, name):
        self.name = name
        self.lastw = None
        self.readers = []
        self.sem = None
        self.semcnt = 0
        self.excl = name.startswith("pb") and name[2:].isdigit()


class Op:
    __slots__ = ("idx", "eng", "fn", "owner", "preds", "dur", "tok", "fin")

    def __init__(self, idx, eng, fn, owner, preds, dur):
        self.idx = idx
        self.eng = eng
        self.fn = fn
        self.owner = owner
        self.preds = preds
        self.dur = dur
        self.tok = None
        self.fin = 0.0


class Sched:
    ENGS = ("sync", "scalar", "gpsimd", "vector", "tensor")
    WINDOW = 160

    def __init__(self, nc, stack):
        self.nc = nc
        self.stack = stack
        self.out = {e: [] for e in self.ENGS}
        self.esem = {}
        self.ecnt = {e: 0 for e in self.ENGS}
        self.known = {e: {} for e in self.ENGS}
        self.nsem = 0
        self.owners = []
        self.seg = []
        self.nops = 0
        for e in ("scalar", "gpsimd", "vector", "tensor"):
            self.esem[e] = self.newsem("e_" + e)

    def newsem(self, name):
        self.nsem += 1
        return self.stack.enter_context(self.nc.semaphore(name))

    def _record(self, eng, fn, owner, reads, writes):
        ex = [r for r in reads if r.excl]
        if ex:
            writes = list(writes) + [r for r in ex if r not in writes]
            reads = [r for r in reads if not r.excl]
        preds = set()
        for r in reads:
            if r.lastw is not None:
                preds.add(r.lastw)
        for w in writes:
            if w.lastw is not None:
                preds.add(w.lastw)
            preds.update(w.readers)
        op = Op(self.nops, eng, fn, owner, preds, getattr(fn, "dur", 0.2))
        self.nops += 1
        for w in writes:
            w.lastw = op
            w.readers = []
        for r in reads:
            if r not in writes:
                r.readers.append(op)
        self.seg.append(op)
        return op

    def op(self, eng, fn, reads=(), writes=()):
        self._record(eng, fn, None, reads, writes)

    def dma(self, eng, fn, owner, reads=(), writes=()):
        if owner.sem is None:
            owner.sem = self.newsem("d_" + owner.name)
            self.owners.append(owner)
        self._record(eng, fn, owner, reads, writes)

    def _schedule_segment(self):
        seg = self.seg
        self.seg = []
        if not seg:
            return
        segset = set(id(o) for o in seg)
        pend = {e: [] for e in self.ENGS}
        for o in seg:
            o.preds = [p for p in o.preds if id(p) in segset]
            pend[o.eng].append(o)
        pos = {e: 0 for e in self.ENGS}
        free = {e: 0.0 for e in self.ENGS}
        done = set()
        order = {e: [] for e in self.ENGS}
        remaining = len(seg)
        W = self.WINDOW
        while remaining:
            best = None
            for e in self.ENGS:
                lst = pend[e]
                if not lst:
                    continue
                fe = free[e]
                cnt = 0
                for o in lst:
                    cnt += 1
                    if cnt > W:
                        break
                    rdy = 0.0
                    ok = True
                    for p in o.preds:
                        if id(p) not in done:
                            ok = False
                            break
                        if p.fin > rdy:
                            rdy = p.fin
                    if not ok:
                        continue
                    st = fe if fe > rdy else rdy
                    if best is None or st < best[0] - 1e-9:
                        best = (st, e, o)
                    if rdy <= fe:
                        break
            assert best is not None, "scheduler deadlock"
            st, e, o = best
            pend[e].remove(o)
            if e in ("sync", "gpsimd") and o.owner is not None:
                free[e] = st + 0.06
                o.fin = st + o.dur
            else:
                free[e] = st + o.dur
                o.fin = st + o.dur
            done.add(id(o))
            order[e].append(o)
            remaining -= 1
        allorder = sorted(seg, key=lambda o: (o.fin - o.dur if o.owner is None else o.fin - o.dur, o.idx))
        for e in self.ENGS:
            for o in order[e]:
                if o.owner is not None:
                    o.owner.semcnt += 16
                    o.tok = (o.owner.sem, o.owner.semcnt)
                else:
                    self.ecnt[e] += 1
                    o.tok = (self.esem[e], self.ecnt[e])
        for e in self.ENGS:
            kn = self.known[e]
            for o in order[e]:
                waits = []
                for p in o.preds:
                    if e == "tensor" and p.eng == "tensor" and p.owner is None:
                        continue
                    s_, v_ = p.tok
                    if kn.get(id(s_), 0) < v_:
                        kn[id(s_)] = v_
                        waits.append((s_, v_))
                self.out[e].append((waits, o.fn, o.tok[0], 16 if o.owner is not None else 1))

    def barrier(self):
        self._schedule_segment()
        allsems = [(self.esem[e], self.ecnt[e]) for e in self.esem] + [(o.sem, o.semcnt) for o in self.owners]
        for eng in self.ENGS:
            kn = self.known[eng]
            waits = []
            for s, v in allsems:
                if v > 0 and kn.get(id(s), 0) < v:
                    kn[id(s)] = v
                    waits.append((s, v))
            self.out[eng].append((waits, None, None, 0))

    def emit(self):
        self._schedule_segment()
        nc = self.nc
        with nc.Block() as block:
            for ename in self.ENGS:
                ops = self.out[ename]
                if not ops:
                    continue

                def body(e, ops=ops):
                    for waits, fn, sem, inc in ops:
                        for s, v in waits:
                            e.wait_ge(s, v)
                        if fn is not None:
                            fn(e).then_inc(sem, inc)

                getattr(block, ename)(body)


def _fsz(ap):
    n = 1
    for d in ap.shape[1:]:
        n *= d
    return n


def _d(f, dur):
    f.dur = dur
    return f


def COPY(out, in_):
    return _d(lambda e: e.tensor_copy(out=out, in_=in_), (_fsz(in_) + 60) / 960.0)


def ACT(out, in_, func, **kw):
    return _d(lambda e: e.activation(out=out, in_=in_, func=func, **kw), (_fsz(in_) + 250) / 1200.0)


def MM(out, lhsT, rhs, start, stop, skip=False):
    n = max(_fsz(rhs), 64) * (4 if rhs.dtype == F32 else 1)
    if skip:
        return _d(lambda e: e.matmul(out, lhsT=lhsT, rhs=rhs, start=start, stop=stop, skip_group_check=True), n / 2400.0 + 0.03)
    return _d(lambda e: e.matmul(out, lhsT=lhsT, rhs=rhs, start=start, stop=stop), n / 2400.0 + 0.03)


def TR(out, in_, ident):
    return _d(lambda e: e.transpose(out=out, in_=in_, identity=ident), 0.12 * (4 if in_.dtype == F32 else 1))


def TT(out, a, b, op):
    return _d(lambda e: e.tensor_tensor(out=out, in0=a, in1=b, op=op), (_fsz(a) + 60) / 960.0)


def STT(out, in0, scalar, in1, op0, op1):
    return _d(lambda e: e.scalar_tensor_tensor(out=out, in0=in0, scalar=scalar, in1=in1, op0=op0, op1=op1), (_fsz(in0) + 60) / 960.0)


def TS(out, in0, s1, s2, op0, op1=None):
    du = (_fsz(in0) + 60) / 960.0
    if op1 is None:
        return _d(lambda e: e.tensor_scalar(out=out, in0=in0, scalar1=s1, scalar2=None, op0=op0), du)
    return _d(lambda e: e.tensor_scalar(out=out, in0=in0, scalar1=s1, scalar2=s2, op0=op0, op1=op1), du)


def DMA(out, in_):
    nb = 4 * in_.shape[0] * _fsz(in_)
    return _d(lambda e: e.dma_start(out=out, in_=in_), 2.0 + nb / 150000.0)


def MEMSET(ap, v):
    return _d(lambda e: e.memset(ap, v), (_fsz(ap) + 60) / 960.0)


def RECIP(out, in_):
    return _d(lambda e: e.reciprocal(out=out, in_=in_), (_fsz(in_) + 60) / 960.0)


def SCAN(out, d0, d1, init):
    return _d(lambda e: e.tensor_tensor_scan(out=out, data0=d0, data1=d1, initial=init, op0=ALU.mult, op1=ALU.add), (2 * _fsz(d0) + 60) / 960.0)


def REDUCE(out, in_):
    return _d(lambda e: e.tensor_reduce(out=out, in_=in_, axis=AX.X, op=ALU.add), (_fsz(in_) + 60) / 960.0)


class Arena:
    def __init__(self, t, size):
        self.t = t
        self.size = size
        self.top = 0
        self.reserve_top = 0

    def f32(self, n):
        off = self.top
        self.top += n
        assert self.top <= self.size - self.reserve_top, (self.top, self.size, self.reserve_top)
        return self.t[:, off:off + n]

    def bf16(self, n):
        words = (n + 1) // 2
        return self.f32(words).bitcast(BF16)[:, 0:n]


def SL(start, n, step):
    return slice(start, start + (n - 1) * step + 1, step)


def block_cols(g, rho, n):
    d = DIL[g]
    return 2048 + rho + d * 128 * n, d


def build_program(upto=99, dbgfn=None, sub=99):
    nc = bass.Bass("TRN2", target_bir_lowering=False)

    def din(name, shape):
        return nc.dram_tensor(name, shape, F32, kind="ExternalInput").ap()

    def dout(name, shape):
        return nc.dram_tensor(name, shape, F32, kind="ExternalOutput").ap()

    xm = din("xm", [2048, 1024])
    xp = din("xp", [2048, 1024])
    pvd = din("pv", [128, 1])
    xsd = din("xs", [4, 1024])
    sst = din("sst", [16, 1024])
    sconv = din("sconv", [4, 3, 1024])
    cache = [din("c0", [4, 128, 2, 512]), din("c1", [4, 512, 2, 512]), din("c2", [4, 2048, 2, 512])]
    win = din("win", [72, 128, 1024])
    wqkv = din("wqkv", [6, 128, 6144])
    wro = din("wro", [8, 128, 1024])
    wao = din("wao", [8, 128, 512])
    wout = din("wout", [128, 8192])
    wga = din("wga", [2, 2, 64, 512])
    prm = din("prm", [64, 128])
    cbrow = din("cbrow", [1, 1024])
    gpre = din("gpre", [1, 1024])
    gpost = din("gpost", [1, 1024])
    etab = din("etab", [3, 3, 128, 1024])
    sbias = din("sbias", [128, 24])
    identin = din("ident", [128, 128])
    selq = din("selq", [4, 512])
    selo = din("selo", [128, 16])

    y = dout("y", [2048, 1024])
    convp = dout("convp", [128, 24])
    hp = dout("hp", [128, 8])
    kvo = [dout("kvo0", [128, 2, 512]), dout("kvo1", [512, 2, 512]), dout("kvo2", [2048, 2, 512])]
    ys = dout("ys", [4, 1024])
    convs_u = dout("convs_u", [128, 32])
    convs_s = dout("convs_s", [4, 2, 1024])
    hs = dout("hs", [128, 32])
    kvs = dout("kvs", [3, 4, 2, 512])
    dbg = dout("dbg", [128, 8192]) if upto != 99 else None
    osc = [nc.dram_tensor(f"osc{g}", [2048, 2, 260], F32).ap() for g in range(3)]

    with ExitStack() as st:
        S = Sched(nc, st)
        AW = 53200
        arena_t = st.enter_context(nc.sbuf_tensor("arena", [128, AW], F32))
        A = Arena(arena_t, AW)
        ps = [st.enter_context(nc.psum_tensor(f"ps{i}", [128, 512], F32)) for i in range(8)]
        PB = [Res(f"pb{i}") for i in range(8)]
        bank_i = [0]

        def nextbank():
            i = bank_i[0]
            bank_i[0] = (i + 1) % 8
            return ps[i], PB[i]

        XNT = A.bf16(8 * XW).rearrange("p (k n) -> p k n", k=8)
        HZT = A.bf16(8 * FW).rearrange("p (k n) -> p k n", k=8)
        IDENTF = A.f32(128)
        IDENT = A.bf16(128)
        PRM = A.f32(64)
        CLT = A.f32(16)
        PVT = A.f32(1)
        HPV = A.f32(1)
        HB = A.f32(16)
        WG = A.bf16(8 * 2 * 128).rearrange("p (c g o) -> p c g o", c=8, g=2)
        HOUT = A.f32(8)
        CONVP = A.f32(24).rearrange("p (c t) -> p c t", c=8)
        SNUM = A.f32(520)
        SELQ = A.f32(512)
        SELO = A.f32(16)
        SBIAS = A.f32(24)
        SMALL = A.f32(64)
        SSTT = A.f32(128)
        SUC = A.f32(32)
        SZS = A.f32(32)
        SH = A.f32(32)
        rXNT = [Res(f"xnt{i}") for i in range(33)]
        rHZT = [Res(f"hzt{c}") for c in range(8)]
        rHZTs = Res("hzts")
        rC = Res("consts")
        rSNUM = Res("snum")
        rSM = Res("small")
        rSST = Res("sstt")
        rSUC = Res("suc")
        rSZS = Res("szs")
        rSH = Res("sh")
        rOUTS = Res("outs_small")
        SCR0 = A.top

        S.dma("sync", DMA(IDENTF, identin), rC, writes=[rC])
        S.dma("sync", DMA(PVT, pvd), rC, writes=[rC])
        S.dma("sync", DMA(SELQ[0:4, :], selq), rC, writes=[rC])
        S.dma("sync", DMA(SELO, selo), rC, writes=[rC])
        S.dma("sync", DMA(SBIAS, sbias), rC, writes=[rC])
        A.top = SCR0
        PRMRAW = A.f32(128)
        SSRAW = A.f32(1024)
        GPRE = A.f32(1024)
        rT = Res("p0tmp")
        rG = Res("gpre")
        rWGL = Res("wgload")
        S.dma("sync", DMA(GPRE, gpre.partition_broadcast(128)), rG, writes=[rG])
        S.dma("sync", DMA(PRMRAW[0:64, :], prm), rT, writes=[rT])
        S.dma("sync", DMA(SSRAW[0:16, :], sst), rT, writes=[rT])
        S.op("vector", COPY(IDENT, IDENTF), reads=[rC], writes=[rC])
        S.op("vector", MEMSET(WG.rearrange("p c g o -> p (c g o)"), 0.0), writes=[rC])
        S.op("vector", MEMSET(SNUM[0:4, :], 0.0), writes=[rSNUM])
        for g in range(2):
            for hf in range(2):
                S.dma("gpsimd", DMA(WG[hf * 64:(hf + 1) * 64, :, g, hf * 64:(hf + 1) * 64],
                                    wga[g, hf].rearrange("i (c o) -> i c o", c=8)), rWGL, reads=[rC], writes=[rC])
        pb, rb = nextbank()
        S.op("tensor", TR(pb[:, 0:64], PRMRAW[0:64, :], IDENTF[0:64, 0:64]), reads=[rT, rC], writes=[rb])
        S.op("vector", COPY(PRM, pb[:, 0:64]), reads=[rb], writes=[rC])
        pb, rb = nextbank()
        for c in range(8):
            S.op("tensor", TR(pb[:, c * 16:(c + 1) * 16], SSRAW[0:16, c * 128:(c + 1) * 128], IDENTF[0:16, 0:16]),
                 reads=[rT, rC], writes=[rb])
        S.op("vector", COPY(SSTT, pb[:, 0:128]), reads=[rb], writes=[rSST])
        SSTT3 = SSTT.rearrange("p (c r) -> p c r", c=8)
        T1 = SMALL[:, 0:8]
        T2 = SMALL[:, 8:16]
        T3 = SMALL[:, 16:24]
        T4 = SMALL[:, 24:32]
        S.op("scalar", ACT(T1, PRM[:, 56:64], AF.Exp, scale=-1.0), reads=[rC], writes=[rSM])
        S.op("vector", TS(T2, T1, 2.0, None, ALU.add), reads=[rSM], writes=[rSM])
        S.op("vector", RECIP(T2, T2), reads=[rSM], writes=[rSM])
        S.op("vector", TT(T3, T1, T2, ALU.mult), reads=[rSM], writes=[rSM])
        S.op("vector", TT(T4, T3, T3, ALU.mult), reads=[rSM], writes=[rSM])
        S.op("vector", TS(T2, T4, 1.0 / 9, 1.0 / 7, ALU.mult, ALU.add), reads=[rSM], writes=[rSM])
        for cst in (1.0 / 5, 1.0 / 3, 1.0):
            S.op("vector", TT(T2, T2, T4, ALU.mult), reads=[rSM], writes=[rSM])
            S.op("vector", TS(T2, T2, cst, None, ALU.add), reads=[rSM], writes=[rSM])
        S.op("vector", TT(T2, T2, T3, ALU.mult), reads=[rSM], writes=[rSM])
        S.op("vector", TS(CLT[:, 0:8], T2, -8.0, None, ALU.mult), reads=[rSM], writes=[rC])
        S.op("vector", TS(CLT[:, 8:16], T2, -16.0, None, ALU.mult), reads=[rSM], writes=[rC])
        S.op("vector", TS(HB, PRM[:, 40:56], 0.5, None, ALU.mult), reads=[rC], writes=[rC])
        S.op("vector", TS(HPV, PVT, 0.5, None, ALU.mult), reads=[rC], writes=[rC])
        S.dma("sync", DMA(convs_s, sconv[:, 1:3, :]), rOUTS, writes=[rOUTS])

        NX = 3
        XT = [A.f32(1024) for _ in range(NX)]
        XB = [A.bf16(1024) for _ in range(NX)]
        rXT = [Res("xt%d" % i) for i in range(NX)]
        rXB = [Res("xb%d" % i) for i in range(NX)]
        tiles = [(xp, i, i * 128, 128) for i in range(16)] + [(xm, i, 2048 + i * 128, 128) for i in range(16)] + [(xsd, 0, 4096, 4)]

        def load_x(idx):
            src, i, col, np_ = tiles[idx]
            b = idx % NX
            S.dma("sync", DMA(XT[b][0:np_, :], src[i * 128:i * 128 + np_, :]), rXT[b], writes=[rXT[b]])

        load_x(0)
        load_x(1)
        for idx, (src, i, col, np_) in enumerate(tiles):
            b = idx % NX
            if idx + 2 < len(tiles):
                load_x(idx + 2)
            ss = SMALL[0:np_, 32 + b:33 + b]
            rs = SMALL[0:np_, 36 + b:37 + b]
            rss = Res("ss")
            S.op("scalar", ACT(XB[b][0:np_, :], XT[b][0:np_, :], AF.Square, accum_out=ss), reads=[rXT[b]], writes=[rXB[b], rss])
            S.op("scalar", ACT(rs, ss, AF.Sqrt, scale=1.0 / 1024, bias=EPS), reads=[rss], writes=[rss])
            S.op("vector", RECIP(rs, rs), reads=[rss], writes=[rss])
            S.op("vector", STT(XB[b][0:np_, :], XT[b][0:np_, :], rs, GPRE[0:np_, :], ALU.mult, ALU.mult),
                 reads=[rXT[b], rss, rG], writes=[rXB[b]])
            pb, rb = nextbank()
            pbb = pb[:, :].bitcast(BF16)
            for kc in range(8):
                S.op("tensor", TR(pbb[:, kc * np_:(kc + 1) * np_], XB[b][0:np_, kc * 128:(kc + 1) * 128], IDENT[0:np_, 0:np_]),
                     reads=[rXB[b], rC], writes=[rb])
            S.op("vector", COPY(XNT[:, :, col:col + np_], pbb[:, 0:8 * np_].rearrange("p (k n) -> p k n", k=8)),
                 reads=[rb], writes=[rXNT[idx]])
        def xr(col, n):
            return [rXNT[i] for i in range(col // 128, (col + n - 1) // 128 + 1)]

        NS = 1024
        WU = [A.bf16(1024).rearrange("p (k n) -> p k n", k=8) for _ in range(2)]
        WZ = [A.bf16(1024).rearrange("p (k n) -> p k n", k=8) for _ in range(2)]
        rWU = [Res("wu0"), Res("wu1")]
        rWZ = [Res("wz0"), Res("wz1")]
        NSET = 3
        Ub = [A.bf16(NS + 4) for _ in range(NSET)]
        DG = [A.bf16(512).rearrange("p (t n) -> p t n", t=4) for _ in range(2)]
        CBROW = A.bf16(1024)
        ONES = A.bf16(512)
        INITS = A.f32(8)
        rDG = [Res("dg0"), Res("dg1")]
        rCB = Res("cbrow")
        S.dma("gpsimd", DMA(CBROW[0:1, :], cbrow), rCB, writes=[rCB])
        S.op("vector", MEMSET(ONES[0:1, :], 1.0), writes=[rCB])
        UC = [A.f32(NS) for _ in range(NSET)]
        Rb = [A.f32(NS) for _ in range(NSET)]
        Ib = [A.f32(NS) for _ in range(NSET)]
        Ab = [A.f32(NS) for _ in range(NSET)]
        UCB = [A.bf16(NS) for _ in range(NSET)]
        rU = [Res("u%d" % i) for i in range(NSET)]
        rUC = [Res("uc%d" % i) for i in range(NSET)]
        rR = [Res("r%d" % i) for i in range(NSET)]
        rI = [Res("i%d" % i) for i in range(NSET)]
        rA = [Res("a%d" % i) for i in range(NSET)]
        rUCB = [Res("ucb%d" % i) for i in range(NSET)]

        def load_w2(c):
            b = c % 2
            S.dma("gpsimd", DMA(WU[b], win[c].rearrange("p (k n) -> p k n", k=8)), rWU[b], writes=[rWU[b]])
            S.dma("gpsimd", DMA(WZ[b], win[8 + c].rearrange("p (k n) -> p k n", k=8)), rWZ[b], writes=[rWZ[b]])

        load_w2(0)

        def prm_of(c):
            return dict(w0=PRM[:, 0 + c:1 + c], w1=PRM[:, 8 + c:9 + c], w2=PRM[:, 16 + c:17 + c], w3=PRM[:, 24 + c:25 + c],
                        cb=PRM[:, 32 + c:33 + c], ba=HB[:, c:c + 1], bx=HB[:, 8 + c:9 + c], clh=CLT[:, c:c + 1], cl=CLT[:, 8 + c:9 + c])

        def stageA(it):
            c, s = it // 4, it % 4
            P = prm_of(c)
            wb = c % 2
            tb = it % NSET
            pbuf = (it - 1) % NSET
            col0 = s * NS
            U, rUU = Ub[tb], rU[tb]
            dg, rdg = DG[c % 2], rDG[c % 2]
            if s == 0:
                for tap, wt in enumerate((P["w0"], P["w1"], P["w2"], P["w3"])):
                    S.op("vector", TS(dg[:, tap, :], IDENTF, wt, None, ALU.mult), reads=[rC], writes=[rdg])
                S.op("vector", MEMSET(U[:, 0:3], 0.0), writes=[rUU])
            else:
                S.op("vector", COPY(U[:, 0:3], Ub[pbuf][:, NS:NS + 3]), reads=[rU[pbuf]], writes=[rUU])
            for tt in range(2):
                pb, rb = nextbank()
                for kc in range(8):
                    S.op("tensor", MM(pb[:, :], WU[wb][:, kc, :], XNT[:, kc, col0 + tt * 512:col0 + (tt + 1) * 512], kc == 0, kc == 7),
                         reads=[rWU[wb]] + xr(col0 + tt * 512, 512), writes=[rb])
                S.op("vector", COPY(U[:, 3 + tt * 512:3 + (tt + 1) * 512], pb[:, :]), reads=[rb], writes=[rUU])
                if s == 3 and tt == 1:
                    S.op("vector", COPY(CONVP[:, c, :], pb[:, 509:512]), reads=[rb], writes=[rOUTS])

        def stageA2(it):
            c, s = it // 4, it % 4
            P = prm_of(c)
            tb = it % NSET
            U, rUU = Ub[tb], rU[tb]
            dg, rdg = DG[c % 2], rDG[c % 2]
            cbanks = []
            for tt in range(2):
                pb, rb = nextbank()
                for tap in range(4):
                    S.op("tensor", MM(pb[:, :], dg[:, tap, :], U[:, tap + tt * 512:tap + (tt + 1) * 512], tap == 0, False),
                         reads=[rdg, rUU], writes=[rb])
                S.op("tensor", MM(pb[:, :], CBROW[0:1, c * 128:(c + 1) * 128], ONES[0:1, :], False, True), reads=[rCB], writes=[rb])
                sl = slice(tt * 512, (tt + 1) * 512)
                if tt == 0:
                    S.op("vector", COPY(UCB[tb][:, sl], pb[:, :]), reads=[rb], writes=[rUCB[tb]])
                else:
                    S.op("scalar", ACT(UCB[tb][:, sl], pb[:, :], AF.Copy), reads=[rb], writes=[rUCB[tb]])
                cbanks.append((pb, rb))
            for tt in range(2):
                sl = slice(tt * 512, (tt + 1) * 512)
                pb, rb = nextbank()
                S.op("tensor", MM(pb[:, :], WG[:, c, 0, :], UCB[tb][:, sl], True, True), reads=[rC, rUCB[tb]], writes=[rb])
                S.op("scalar", ACT(Rb[tb][:, sl], pb[:, :], AF.Tanh, bias=P["ba"], scale=0.5), reads=[rb, rC], writes=[rR[tb]])
                pb, rb = nextbank()
                S.op("tensor", MM(pb[:, :], WG[:, c, 1, :], UCB[tb][:, sl], True, True), reads=[rC, rUCB[tb]], writes=[rb])
                S.op("scalar", ACT(Ib[tb][:, sl], pb[:, :], AF.Tanh, bias=P["bx"], scale=0.5), reads=[rb, rC], writes=[rI[tb]])
                cpb, crb = cbanks[tt]
                S.op("vector", STT(Ib[tb][:, sl], Ib[tb][:, sl], 1.0, cpb[:, :], ALU.add, ALU.mult), reads=[crb, rI[tb]], writes=[rI[tb]])
            S.op("scalar", ACT(Ab[tb], Rb[tb], AF.Exp, scale=P["clh"], bias=P["clh"]), reads=[rR[tb], rC], writes=[rA[tb]])
            S.op("scalar", ACT(Rb[tb], Rb[tb], AF.Exp, scale=P["cl"], bias=P["cl"]), reads=[rR[tb], rC], writes=[rR[tb]])

        def stageB(it):
            c, s = it // 4, it % 4
            wb = c % 2
            tb = it % NSET
            pbuf = (it - 1) % NSET
            col0 = s * NS
            S.op("scalar", ACT(Rb[tb], Rb[tb], AF.Sqrt, scale=-1.0, bias=1.0), reads=[rR[tb]], writes=[rR[tb]])
            S.op("vector", TT(Ib[tb], Ib[tb], Rb[tb], ALU.mult), reads=[rI[tb], rR[tb]], writes=[rI[tb]])
            if s == 0:
                init, rds = 0.0, [rA[tb], rI[tb]]
            elif s == 2:
                init = INITS[:, c:c + 1]
                S.op("vector", TT(init, UC[pbuf][:, NS - 1:NS], PVT[:, 0:1], ALU.mult), reads=[rUC[pbuf], rC], writes=[rSM])
                rds = [rA[tb], rI[tb], rSM]
            else:
                init, rds = UC[pbuf][:, NS - 1:NS], [rA[tb], rI[tb], rUC[pbuf]]
            S.op("vector", SCAN(UC[tb], Ab[tb], Ib[tb], init), reads=rds, writes=[rUC[tb]])
            if s >= 2:
                for tt in range(2):
                    sl = slice(tt * 512, (tt + 1) * 512)
                    pb, rb = nextbank()
                    for kc in range(8):
                        S.op("tensor", MM(pb[:, :], WZ[wb][:, kc, :], XNT[:, kc, col0 + tt * 512:col0 + (tt + 1) * 512], kc == 0, kc == 7),
                             reads=[rWZ[wb]] + xr(col0 + tt * 512, 512), writes=[rb])
                    S.op("scalar", ACT(Rb[tb][:, sl], pb[:, :], AF.Tanh, scale=0.5), reads=[rb], writes=[rR[tb]])
                    S.op("vector", STT(Rb[tb][:, sl], Rb[tb][:, sl], 1.0, pb[:, :], ALU.add, ALU.mult), reads=[rb, rR[tb]], writes=[rR[tb]])
                S.op("vector", STT(HZT[:, c, (s - 2) * NS:(s - 1) * NS], Rb[tb], 0.25, UC[tb], ALU.mult, ALU.mult), reads=[rUC[tb], rR[tb]], writes=[rHZT[c]])
            if s == 3:
                S.op("vector", TS(HOUT[:, c:c + 1], UC[tb][:, NS - 1:NS], 0.5, None, ALU.mult), reads=[rUC[tb]], writes=[rOUTS])
                pb, rb = nextbank()
                for kc in range(8):
                    S.op("tensor", MM(pb[:, 0:4], WU[wb][:, kc, :], XNT[:, kc, 4096:4100], kc == 0, kc == 7), reads=[rWU[wb], rXNT[32]], writes=[rb])
                S.op("vector", COPY(SUC[:, c * 4:(c + 1) * 4], pb[:, 0:4]), reads=[rb], writes=[rSUC])
                pb, rb = nextbank()
                for kc in range(8):
                    S.op("tensor", MM(pb[:, 0:4], WZ[wb][:, kc, :], XNT[:, kc, 4096:4100], kc == 0, kc == 7), reads=[rWZ[wb], rXNT[32]], writes=[rb])
                S.op("scalar", ACT(SZS[:, c * 4:(c + 1) * 4], pb[:, 0:4], AF.Tanh, scale=0.5), reads=[rb], writes=[rSZS])
                S.op("vector", STT(SZS[:, c * 4:(c + 1) * 4], SZS[:, c * 4:(c + 1) * 4], 1.0, pb[:, 0:4], ALU.add, ALU.mult), reads=[rb, rSZS], writes=[rSZS])
                S.op("vector", TS(SZS[:, c * 4:(c + 1) * 4], SZS[:, c * 4:(c + 1) * 4], 0.5, None, ALU.mult), reads=[rSZS], writes=[rSZS])

        SW = A.f32(64)
        SWB = A.bf16(8)
        rSW = Res("sw")
        def sample_step(c):
                usl = SUC[:, c * 4:(c + 1) * 4]
                uc = SW[:, 0:4]
                st_ = SSTT3[:, c, 0:12].rearrange("p (b t) -> p b t", b=4)
                S.op("vector", TS(uc, usl, PRM[:, 24 + c:25 + c], PRM[:, 32 + c:33 + c], ALU.mult, ALU.add), reads=[rSUC, rC], writes=[rSW])
                for tap in range(3):
                    S.op("vector", STT(uc, st_[:, :, tap], PRM[:, tap * 8 + c:tap * 8 + c + 1], uc, ALU.mult, ALU.add), reads=[rSST, rC, rSW], writes=[rSW])
                S.op("vector", COPY(SWB[:, 0:4], uc), reads=[rSW], writes=[rSW])
                pb, rb = nextbank()
                S.op("tensor", MM(pb[:, 0:4], WG[:, c, 0, :], SWB[:, 0:4], True, True), reads=[rC, rSW], writes=[rb])
                S.op("tensor", MM(pb[:, 4:8], WG[:, c, 1, :], SWB[:, 0:4], True, True), reads=[rC, rSW], writes=[rb])
                r_ = SW[:, 4:8]
                i_ = SW[:, 8:12]
                a_ = SW[:, 12:16]
                S.op("scalar", ACT(r_, pb[:, 0:4], AF.Tanh, bias=HB[:, c:c + 1], scale=0.5), reads=[rb, rC], writes=[rSW])
                S.op("scalar", ACT(i_, pb[:, 4:8], AF.Tanh, bias=HB[:, 8 + c:9 + c], scale=0.5), reads=[rb, rC], writes=[rSW])
                S.op("scalar", ACT(a_, r_, AF.Exp, scale=CLT[:, c:c + 1], bias=CLT[:, c:c + 1]), reads=[rSW, rC], writes=[rSW])
                S.op("scalar", ACT(r_, r_, AF.Exp, scale=CLT[:, 8 + c:9 + c], bias=CLT[:, 8 + c:9 + c]), reads=[rSW, rC], writes=[rSW])
                S.op("scalar", ACT(r_, r_, AF.Sqrt, scale=-1.0, bias=1.0), reads=[rSW], writes=[rSW])
                S.op("vector", STT(i_, i_, 1.0, r_, ALU.add, ALU.mult), reads=[rSW], writes=[rSW])
                S.op("vector", STT(i_, i_, 0.5, uc, ALU.mult, ALU.mult), reads=[rSW], writes=[rSW])
                S.op("vector", TT(a_, a_, SSTT3[:, c, 12:16], ALU.mult), reads=[rSW, rSST], writes=[rSW])
                S.op("vector", TT(SH[:, c * 4:(c + 1) * 4], a_, i_, ALU.add), reads=[rSW], writes=[rSH])
                S.op("vector", TT(HZT[:, c, 2048:2052], SH[:, c * 4:(c + 1) * 4], SZS[:, c * 4:(c + 1) * 4], ALU.mult), reads=[rSH, rSZS], writes=[rHZTs])

        NIT = 32
        load_w2(1)
        WQKV_first = arena_t[:, SCR0:SCR0 + 3072].bitcast(BF16).rearrange("p (k n) -> p k n", k=8)
        rWQ2 = [Res("wq0"), Res("wq1")]
        stageA(0)
        stageA(1)
        stageA2(0)
        for it in range(NIT):
            if it + 2 < NIT:
                stageA(it + 2)
            if it + 1 < NIT:
                stageA2(it + 1)
            stageB(it)
            if it % 4 == 3:
                if it // 4 + 2 < 8:
                    load_w2(it // 4 + 2)
                sample_step(it // 4)
            if it == 13:
                S.dma("gpsimd", DMA(WQKV_first, wqkv[0].rearrange("p (k n) -> p k n", k=8)), rWQ2[0],
                      writes=[rWQ2[0], rT, rG, rXT[0], rXT[1]])
        S.dma("sync", DMA(convs_u, SUC), rSUC, reads=[rSUC])
        S.dma("sync", DMA(hs, SH), rSH, reads=[rSH])
        S.dma("sync", DMA(hp, HOUT), rOUTS, reads=[rOUTS])
        S.dma("sync", DMA(convp, CONVP.rearrange("p c t -> p (c t)")), rOUTS, reads=[rOUTS])
        S.barrier()
        rX = Res("xnt_all")
        if upto == 2:
            if dbgfn is not None:
                dbgfn(locals())
                S.barrier()
            S.emit()
            return nc

        A.top = SCR0
        WQKV2 = [A.bf16(8 * 768).rearrange("p (k n) -> p k n", k=8) for _ in range(2)]
        assert A.top == SCR0 + 6144
        NRB = 6
        KT = A.bf16(2 * NRB * 128).rearrange("p (h b n) -> p h b n", h=2, b=NRB)
        Vflat = A.bf16(NRB * 4 * 66)
        Vb = Vflat.rearrange("p (b h e) -> p b h e", b=NRB, h=4)
        Vm = Vflat.rearrange("p (m e) -> p m e", e=66)
        QT2 = [A.bf16(2 * 2048).rearrange("p (h n) -> p h n", h=2) for _ in range(2)]
        ET2 = [A.f32(3 * 512).rearrange("p (k n) -> p k n", k=3) for _ in range(2)]
        NR = 3
        KVF = [A.f32(512) for _ in range(2)] + [None]
        KB = [A.bf16(256) for _ in range(NR)]
        EX = [A.f32(512) for _ in range(NR)]
        PT = [A.bf16(512) for _ in range(NR)]
        OS = [A.f32(260) for _ in range(NR)]
        CK = [A.f32(512) for _ in range(2)]
        SQKV = A.f32(768)
        SPR = A.f32(260)
        SPV = A.f32(260)
        slot_of = {}
        rKT = [Res(f"kt{i}") for i in range(32)]
        rV = [Res(f"v{i}") for i in range(32)]
        rVones = Res("vones")
        rQT2 = [[[Res(f"qt{q}{h}{t}") for t in range(4)] for h in range(2)] for q in range(2)]
        rET2 = [Res("et0"), Res("et1")]
        rKVF = [Res("kvf0"), Res("kvf1")]
        rKB = [Res("kb%d" % i) for i in range(3)]
        rEX = [Res("ex%d" % i) for i in range(3)]
        rPT = [Res("pt%d" % i) for i in range(3)]
        rOS = [Res("os%d" % i) for i in range(3)]
        rCK = [Res("ck0"), Res("ck1")]
        rSQKV, rSPR, rSPV = Res("sqkv"), Res("spr"), Res("spv")
        rOSC = Res("osc")
        cnt = {"kvf": 0, "kb": 0, "ex": 0, "os": 0, "ck": 0, "blk": 0}
        S.op("vector", MEMSET(Vm[:, :, 64:65], 1.0), writes=[rVones] + rV[:NRB])
        for g in range(3):
            d = DIL[g]
            nb = 16 // d
            for hh in range(2):
                sp = g * 2 + hh
                if sp + 1 < 6:
                    S.dma("gpsimd", DMA(WQKV2[(sp + 1) % 2], wqkv[sp + 1].rearrange("p (k n) -> p k n", k=8)), rWQ2[(sp + 1) % 2], writes=[rWQ2[(sp + 1) % 2]])
                WQKV = WQKV2[sp % 2]
                WQ = WQKV[:, :, 0:256]
                WKV = WQKV[:, :, 256:768]
                rWQ = rWQ2[sp % 2]
                rWKV = rWQ
                QT, rQT = QT2[sp % 2], rQT2[sp % 2]
                ET, rET = ET2[sp % 2], rET2[sp % 2]
                for kind in range(3):
                    S.dma("sync", DMA(ET[:, kind, :], etab[g, kind][:, hh * 512:(hh + 1) * 512]), rET, writes=[rET])
                for hpp in range(2):
                    for tt in range(4):
                        pb, rb = nextbank()
                        for kc in range(8):
                            S.op("tensor", MM(pb[:, :], WQ[:, kc, hpp * 128:(hpp + 1) * 128], XNT[:, kc, 2048 + tt * 512:2048 + (tt + 1) * 512], kc == 0, kc == 7),
                                 reads=[rWQ, rX], writes=[rb])
                        md = 512 // d
                        qdst = QT[:, hpp, :].rearrange("p (r m) -> p r m", r=d)[:, :, tt * md:(tt + 1) * md]
                        S.op("scalar", ACT(qdst, pb[:, :].rearrange("p (m r) -> p r m", r=d), AF.Copy, scale=0.125), reads=[rb], writes=[rQT[hpp][tt]])
                pb, rb = nextbank()
                for kc in range(8):
                    S.op("tensor", MM(pb[0:4, 0:256], XNT[:, kc, 4096:4100], WQ[:, kc, :], kc == 0, kc == 7), reads=[rWQ, rX], writes=[rb])
                S.op("vector", COPY(SQKV[0:4, 0:256], pb[0:4, 0:256]), reads=[rb], writes=[rSQKV])
                pb, rb = nextbank()
                for kc in range(8):
                    S.op("tensor", MM(pb[0:4, :], XNT[:, kc, 4096:4100], WKV[:, kc, :], kc == 0, kc == 7), reads=[rWKV, rX], writes=[rb])
                S.op("vector", COPY(SQKV[0:4, 256:768], pb[0:4, :]), reads=[rb], writes=[rSQKV])
                S.dma("sync", DMA(kvs[g, :, :, hh * 256:(hh + 1) * 256], SQKV[0:4, 256:768].rearrange("p (t n) -> p t n", t=2)), rSQKV, reads=[rSQKV])

                def produce(rho, n):
                    bi = cnt["blk"] % NRB
                    cnt["blk"] += 1
                    slot_of[(g, hh, rho, n)] = bi
                    c0, stp = block_cols(g, rho, n)
                    pb, rb = nextbank()
                    for kc in range(8):
                        S.op("tensor", MM(pb[:, :], XNT[:, kc, SL(c0, 128, stp)], WKV[:, kc, :], kc == 0, kc == 7),
                             reads=[rX, rWKV], writes=[rb])
                    if n == nb - 1:
                        fb = cnt["kvf"] % 2
                        cnt["kvf"] += 1
                        S.op("scalar", ACT(KVF[fb], pb[:, :], AF.Copy), reads=[rb], writes=[rKVF[fb]])
                        dst = kvo[g][SL(rho, 128, d), :, hh * 256:(hh + 1) * 256]
                        S.dma("sync", DMA(dst, KVF[fb].rearrange("p (t n) -> p t n", t=2)), rKVF[fb], reads=[rKVF[fb]])
                    kb = cnt["kb"] % NR
                    cnt["kb"] += 1
                    S.op("vector", COPY(KB[kb], pb[:, 0:256]), reads=[rb], writes=[rKB[kb]])
                    S.op("vector", COPY(Vb[:, bi, :, 0:64], pb[:, 256:512].rearrange("p (h e) -> p h e", h=4)), reads=[rb, rVones], writes=[rV[bi]])
                    pb2, rb2 = nextbank()
                    pbb = pb2[:, :].bitcast(BF16)
                    for hpp in range(2):
                        S.op("tensor", TR(pbb[:, hpp * 128:(hpp + 1) * 128], KB[kb][:, hpp * 128:(hpp + 1) * 128], IDENT), reads=[rKB[kb], rC], writes=[rb2])
                    S.op("scalar", ACT(KT[:, :, bi, :], pbb[:, 0:256].rearrange("p (h n) -> p h n", h=2), AF.Copy), reads=[rb2], writes=[rKT[bi]])

                def attend(rho, n):
                    q0 = rho + d * 128 * n
                    qb0 = rho * (2048 // d) + 128 * n
                    qr = [rQT[0][t] for t in range(4)] + [rQT[1][t] for t in range(4)]
                    po, ro = nextbank()
                    for bk, nn in enumerate((n - 1, n)):
                        bi = slot_of[(g, hh, rho, nn)]
                        pbs = [nextbank(), nextbank()]
                        for h4 in range(4):
                            hpp, hf = h4 // 2, h4 % 2
                            pb, rb = pbs[hf]
                            S.op("tensor", MM(pb[:, hpp * 128:(hpp + 1) * 128], KT[hf * 64:(hf + 1) * 64, hpp, bi, :],
                                              QT[hf * 64:(hf + 1) * 64, hpp, qb0:qb0 + 128], True, True),
                                 reads=[rKT[bi]] + qr, writes=[rb])
                        xb = cnt["ex"] % NR
                        cnt["ex"] += 1
                        EX4 = EX[xb].rearrange("p (a f n) -> p a f n", a=2, f=2)
                        for hf in range(2):
                            pb, rb = pbs[hf]
                            S.op("scalar", ACT(EX4[:, :, hf, :], pb[:, 0:256].rearrange("p (a n) -> p a n", a=2), AF.Exp), reads=[rb], writes=[rEX[xb]])
                        kind = 0 if bk == 1 else (2 if n == 0 else 1)
                        S.op("vector", TT(PT[xb], EX[xb], ET[:, kind, :], ALU.mult), reads=[rEX[xb], rET], writes=[rPT[xb]])
                        for h4 in range(4):
                            S.op("tensor", MM(po[:, h4 * 65:(h4 + 1) * 65], PT[xb][:, h4 * 128:(h4 + 1) * 128], Vb[:, bi, h4, 0:65],
                                              bk == 0 and h4 == 0, bk == 1, skip=True),
                                 reads=[rPT[xb], rV[bi], rVones], writes=[ro])
                    ob = cnt["os"] % NR
                    cnt["os"] += 1
                    S.op("vector", COPY(OS[ob], po[:, 0:260]), reads=[ro], writes=[rOS[ob]])
                    S.dma("sync", DMA(osc[g][SL(q0, 128, d), hh, :], OS[ob]), rOS[ob], reads=[rOS[ob]], writes=[rOSC])

                for rho in range(d):
                    for n in range(-1, nb):
                        produce(rho, n)
                        if n >= 0:
                            attend(rho, n)
                po, ro = nextbank()
                for b in range(4):
                    cb_ = cnt["ck"] % 2
                    cnt["ck"] += 1
                    S.dma("sync", DMA(CK[cb_].rearrange("p (t n) -> p t n", t=2), cache[g][b, SL(0, 128, d), :, hh * 256:(hh + 1) * 256]), rCK[cb_], writes=[rCK[cb_]])
                    pb, rb = nextbank()
                    S.op("tensor", MM(pb[:, 0:256], SELQ[0:4, b * 128:(b + 1) * 128], SQKV[0:4, 0:256], True, True), reads=[rC, rSQKV], writes=[rb])
                    S.op("vector", TT(SPR[:, 0:256], CK[cb_][:, 0:256], pb[:, 0:256], ALU.mult), reads=[rCK[cb_], rb], writes=[rSPR])
                    sc = SMALL[:, 40:44]
                    S.op("vector", REDUCE(sc, SPR[:, 0:256].rearrange("p (h e) -> p h e", h=4)), reads=[rSPR], writes=[rSM])
                    S.op("vector", STT(sc, sc, 0.125, SBIAS[:, g * 8 + hh * 4:g * 8 + hh * 4 + 4], ALU.mult, ALU.add), reads=[rSM, rC], writes=[rSM])
                    SPV3 = SPV.rearrange("p (h e) -> p h e", h=4)
                    S.op("scalar", ACT(SPV3[:, :, 64], sc, AF.Exp), reads=[rSM], writes=[rSPV])
                    S.op("vector", TT(SPV3[:, :, 0:64], CK[cb_][:, 256:512].rearrange("p (h e) -> p h e", h=4),
                                      SPV3[:, :, 64:65].to_broadcast([128, 4, 64]), ALU.mult), reads=[rCK[cb_], rSPV], writes=[rSPV])
                    S.op("tensor", MM(po[0:4, 0:260], SELO[:, b * 4:(b + 1) * 4], SPV, b == 0, b == 3), reads=[rC, rSPV], writes=[ro])
                if sub == 5:
                    S.barrier()
                    S.emit()
                    return nc
                q_s, k_s, v_s = SQKV[0:4, 0:256], SQKV[0:4, 256:512], SQKV[0:4, 512:768]
                S.op("vector", TT(SPR[0:4, 0:256], q_s, k_s, ALU.mult), reads=[rSQKV], writes=[rSPR])
                sc = SMALL[0:4, 44:48]
                S.op("vector", REDUCE(sc, SPR[0:4, 0:256].rearrange("p (h e) -> p h e", h=4)), reads=[rSPR], writes=[rSM])
                SPVn = SPR[0:4, 0:260].rearrange("p (h e) -> p h e", h=4)
                S.op("scalar", ACT(SPVn[:, :, 64], sc, AF.Exp, scale=0.125), reads=[rSM], writes=[rSPR])
                S.op("vector", TT(SPVn[:, :, 0:64], v_s.rearrange("p (h e) -> p h e", h=4), SPVn[:, :, 64:65].to_broadcast([4, 4, 64]), ALU.mult),
                     reads=[rSQKV, rSPR], writes=[rSPR])
                sn = SNUM[0:4, hh * 260:(hh + 1) * 260]
                S.op("vector", TT(sn, sn, SPR[0:4, 0:260], ALU.add), reads=[rSPR, rSNUM], writes=[rSNUM])
                S.op("vector", TT(sn, sn, po[0:4, 0:260], ALU.add), reads=[ro, rSNUM], writes=[rSNUM])
        S.barrier()
        if upto == 3:
            if dbgfn is not None:
                dbgfn(locals())
                S.barrier()
            S.emit()
            return nc

        A.top = SCR0
        A.reserve_top = 4096 + 1792
        WOUT = arena_t[:, AW - 4096:AW].bitcast(BF16).rearrange("p (k n) -> p k n", k=8)
        rWOUT = Res("wout")
        hi0 = AW - 4096 - 1792

        def hv(off, words, k):
            return arena_t[:, hi0 + off:hi0 + off + words].bitcast(BF16).rearrange("p (k n) -> p k n", k=k)

        WGR = [hv(0, 512, 8), None]
        WGA = [hv(512, 512, 8), None]
        WRO = [hv(1024, 512, 8), None]
        WAO = [hv(1536, 256, 4), None]
        rW5 = [Res("w5_0"), Res("w5_1")]

        def load_w5(j):
            b = j % 2
            S.dma("gpsimd", DMA(WGR[b], win[56 + j].rearrange("p (k n) -> p k n", k=8)), rW5[b], writes=[rW5[b]])
            S.dma("gpsimd", DMA(WGA[b], win[64 + j].rearrange("p (k n) -> p k n", k=8)), rW5[b], writes=[rW5[b]])
            S.dma("gpsimd", DMA(WRO[b], wro[j].rearrange("p (k n) -> p k n", k=8)), rW5[b], writes=[rW5[b]])
            S.dma("gpsimd", DMA(WAO[b], wao[j].rearrange("p (k n) -> p k n", k=4)), rW5[b], writes=[rW5[b]])
        AZT = A.bf16(4 * FW).rearrange("p (k n) -> p k n", k=4)
        SCR1 = A.top
        SZA = A.f32(4 * FW).rearrange("p (k n) -> p k n", k=4)
        WZA = [A.bf16(1024).rearrange("p (k n) -> p k n", k=8) for _ in range(2)]
        NO = 3
        OT = [A.f32(3 * 520).rearrange("p (g n) -> p g n", g=3) for _ in range(NO)]
        MG = [A.f32(512) for _ in range(NO)]
        rSZA = [Res(f"sza{i}") for i in range(4)]
        rWZA = [Res("wza0"), Res("wza1")]
        rOT = [Res("ot%d" % i) for i in range(4)]
        rMG = [Res("mg%d" % i) for i in range(4)]
        rAZT = Res("azt")
        ttiles = [(2048 + t * 512, t * 512, 512) for t in range(4)] + [(4096, 2048, 4)]
        for c4 in range(4):
            wb = c4 % 2
            S.dma("gpsimd", DMA(WZA[wb], win[52 + c4].rearrange("p (k n) -> p k n", k=8)), rWZA[wb], writes=[rWZA[wb]])
            if c4 == 1:
                load_w5(0)
            for (xc, fc, nn) in ttiles:
                pb, rb = nextbank()
                for kc in range(8):
                    S.op("tensor", MM(pb[:, 0:nn], WZA[wb][:, kc, :], XNT[:, kc, xc:xc + nn], kc == 0, kc == 7), reads=[rWZA[wb], rX], writes=[rb])
                S.op("scalar", ACT(SZA[:, c4, fc:fc + nn], pb[:, 0:nn], AF.Silu), reads=[rb], writes=[rSZA[c4]])

        for kq in range(4):
            S.dma("gpsimd", DMA(WOUT[:, 2 * kq:2 * kq + 2, :], wout[:, kq * 2048:(kq + 1) * 2048].rearrange("p (k n) -> p k n", k=2)), rWOUT, writes=[rWOUT])

        def load_ot(t):
            b = t % NO
            for g in range(3):
                S.dma("sync", DMA(OT[b][:, g, :], osc[g][t * 128:(t + 1) * 128].rearrange("p h n -> p (h n)")), rOT[b], writes=[rOT[b]])

        for t0 in range(NO - 1):
            load_ot(t0)
        for t in range(17):
            b = t % NO
            if t < 16:
                if t + NO - 1 < 16:
                    load_ot(t + NO - 1)
                np_ = 128
                s1 = OT[b][:, 0, :]
                S.op("vector", TT(s1, s1, OT[b][:, 1, :], ALU.add), reads=[rOT[b]], writes=[rOT[b]])
                S.op("vector", TT(s1, s1, OT[b][:, 2, :], ALU.add), reads=[rOT[b]], writes=[rOT[b]])
                rsrc = rOT[b]
                fc = t * 128
            else:
                np_ = 4
                s1 = SNUM[0:4, :]
                rsrc = rSNUM
                fc = 2048
            s3 = s1[0:np_, :].rearrange("p (h e) -> p h e", h=8)
            rd = SMALL[0:np_, 48:56]
            S.op("vector", RECIP(rd, s3[:, :, 64]), reads=[rsrc], writes=[rSM])
            S.op("vector", TT(MG[b][0:np_, :].rearrange("p (h e) -> p h e", h=8), s3[:, :, 0:64], rd.unsqueeze(2).to_broadcast([np_, 8, 64]), ALU.mult),
                 reads=[rsrc, rSM], writes=[rMG[b]])
            pb, rb = nextbank()
            for c4 in range(4):
                S.op("tensor", TR(pb[:, c4 * np_:(c4 + 1) * np_], MG[b][0:np_, c4 * 128:(c4 + 1) * 128], IDENTF[0:np_, 0:np_]), reads=[rMG[b], rC], writes=[rb])
            S.op("vector", TT(AZT[:, :, fc:fc + np_], pb[:, 0:4 * np_].rearrange("p (k n) -> p k n", k=4), SZA[:, :, fc:fc + np_], ALU.mult),
                 reads=[rb] + rSZA, writes=[rAZT])
        S.barrier()
        if upto == 4:
            if dbgfn is not None:
                dbgfn(locals())
                S.barrier()
            S.emit()
            return nc

        A.top = SCR1
        MIXT = A.bf16(8 * FW).rearrange("p (k n) -> p k n", k=8)
        WGR[1] = A.bf16(1024).rearrange("p (k n) -> p k n", k=8)
        WGA[1] = A.bf16(1024).rearrange("p (k n) -> p k n", k=8)
        WRO[1] = A.bf16(1024).rearrange("p (k n) -> p k n", k=8)
        WAO[1] = A.bf16(512).rearrange("p (k n) -> p k n", k=4)
        SGR = [A.f32(512) for _ in range(2)]
        SGA = [A.f32(512) for _ in range(2)]
        rSGR = [Res("sgr0"), Res("sgr1")]
        rSGA = [Res("sga0"), Res("sga1")]
        rMIX = [Res(f"mix{j}") for j in range(8)]

        k5 = 0
        for j in range(8):
            wb = j % 2
            if j + 1 < 8:
                load_w5(j + 1)
            for (xc, fc, nn) in ttiles:
                sb = k5 % 2
                k5 += 1
                p1, r1 = nextbank()
                for kc in range(8):
                    S.op("tensor", MM(p1[:, 0:nn], WGR[wb][:, kc, :], XNT[:, kc, xc:xc + nn], kc == 0, kc == 7), reads=[rW5[wb], rX], writes=[r1])
                S.op("scalar", ACT(SGR[sb][:, 0:nn], p1[:, 0:nn], AF.Sigmoid), reads=[r1], writes=[rSGR[sb]])
                p2, r2 = nextbank()
                for kc in range(8):
                    S.op("tensor", MM(p2[:, 0:nn], WGA[wb][:, kc, :], XNT[:, kc, xc:xc + nn], kc == 0, kc == 7), reads=[rW5[wb], rX], writes=[r2])
                S.op("scalar", ACT(SGA[sb][:, 0:nn], p2[:, 0:nn], AF.Sigmoid), reads=[r2], writes=[rSGA[sb]])
                p3, r3 = nextbank()
                for kc in range(8):
                    S.op("tensor", MM(p3[:, 0:nn], WRO[wb][:, kc, :], HZT[:, kc, fc:fc + nn], kc == 0, kc == 7), reads=[rW5[wb]], writes=[r3])
                S.op("vector", TT(SGR[sb][:, 0:nn], SGR[sb][:, 0:nn], p3[:, 0:nn], ALU.mult), reads=[r3, rSGR[sb]], writes=[rSGR[sb]])
                p4, r4 = nextbank()
                for kc in range(4):
                    S.op("tensor", MM(p4[:, 0:nn], WAO[wb][:, kc, :], AZT[:, kc, fc:fc + nn], kc == 0, kc == 3), reads=[rW5[wb]], writes=[r4])
                S.op("vector", TT(SGA[sb][:, 0:nn], SGA[sb][:, 0:nn], p4[:, 0:nn], ALU.mult), reads=[r4, rSGA[sb]], writes=[rSGA[sb]])
                S.op("vector", TT(MIXT[:, j, fc:fc + nn], SGR[sb][:, 0:nn], SGA[sb][:, 0:nn], ALU.add), reads=[rSGR[sb], rSGA[sb]], writes=[rMIX[j]])
        S.barrier()
        if upto == 5:
            if dbgfn is not None:
                dbgfn(locals())
                S.barrier()
            S.emit()
            return nc

        A.top = 0
        N6 = 4
        XT6 = [A.f32(1024) for _ in range(N6)]
        YT = [A.f32(1024) for _ in range(N6)]
        SQ6 = A.f32(512)
        SM6 = A.f32(16)
        GPOST = A.f32(1024)
        rGP = Res("gpost")
        S.dma("sync", DMA(GPOST, gpost.partition_broadcast(128)), rGP, writes=[rGP])
        rXT6 = [Res("xt6_%d" % i) for i in range(4)]
        rYT = [Res("yt%d" % i) for i in range(4)]
        rSQ6 = Res("sq6")
        t6 = [(xm, t * 128, y, t * 128, t * 128, 128) for t in range(16)] + [(xsd, 0, ys, 0, 2048, 4)]

        def load_x6(i):
            src, r0, _, _, _, np_ = t6[i]
            b = i % N6
            S.dma("sync", DMA(XT6[b][0:np_, :], src[r0:r0 + np_, :]), rXT6[b], writes=[rXT6[b]])

        for i0 in range(N6 - 1):
            load_x6(i0)
        for i, (src, r0, dst, d0, fc, np_) in enumerate(t6):
            b = i % N6
            if i + N6 - 1 < len(t6):
                load_x6(i + N6 - 1)
            banks = []
            for hf in range(2):
                pb, rb = nextbank()
                for kc in range(8):
                    S.op("tensor", MM(pb[0:np_, :], MIXT[:, kc, fc:fc + np_], WOUT[:, kc, hf * 512:(hf + 1) * 512], kc == 0, kc == 7), reads=[rWOUT], writes=[rb])
                banks.append((pb, rb))
            rss = Res("ss6")
            ssa = SM6[0:np_, 4 * b:4 * b + 1]
            ssb = SM6[0:np_, 4 * b + 1:4 * b + 2]
            rs = SM6[0:np_, 4 * b + 2:4 * b + 3]
            S.op("scalar", ACT(SQ6[0:np_, :], banks[0][0][0:np_, :], AF.Square, accum_out=ssa), reads=[banks[0][1]], writes=[rSQ6, rss])
            S.op("scalar", ACT(SQ6[0:np_, :], banks[1][0][0:np_, :], AF.Square, accum_out=ssb), reads=[banks[1][1], rss], writes=[rSQ6, rss])
            S.op("vector", TT(rs, ssa, ssb, ALU.add), reads=[rss], writes=[rss])
            S.op("scalar", ACT(rs, rs, AF.Sqrt, scale=1.0 / 1024, bias=EPS), reads=[rss], writes=[rss])
            S.op("vector", RECIP(rs, rs), reads=[rss], writes=[rss])
            for hf in range(2):
                sl = slice(hf * 512, (hf + 1) * 512)
                S.op("vector", STT(YT[b][0:np_, sl], banks[hf][0][0:np_, :], rs, GPOST[0:np_, sl], ALU.mult, ALU.mult),
                     reads=[banks[hf][1], rss, rGP], writes=[rYT[b]])
            S.op("vector", TT(YT[b][0:np_, :], YT[b][0:np_, :], XT6[b][0:np_, :], ALU.add), reads=[rYT[b], rXT6[b]], writes=[rYT[b]])
            S.dma("sync", DMA(dst[d0:d0 + np_, :], YT[b][0:np_, :]), rYT[b], reads=[rYT[b]])
        S.barrier()
        assert S.nsem <= 100, S.nsem
        S.emit()
    return nc


def _alibi_slopes():
    n = 24
    return (2.0 ** (-8.0 * np.arange(1, n + 1, dtype=np.float64) / n)).reshape(3, 8)


def _tables(half):
    sl = _alibi_slopes()
    i = np.arange(128)
    et = np.zeros((3, 3, 128, 8, 128), np.float32)
    for g in range(3):
        d = DIL[g]
        for h in range(8):
            s = sl[g, h]
            steps_cur = (i[None, :] - i[:, None]).astype(np.float64)
            cur = np.where(steps_cur >= 0, np.exp(-s * d * steps_cur), 0.0)
            steps_prev = 128 + steps_cur
            prev = np.where(steps_prev <= 128, np.exp(-s * d * steps_prev), 0.0)
            et[g, 0, :, h, :] = cur
            et[g, 1, :, h, :] = prev
            et[g, 2, :, h, :] = prev * float(half)
    sb = np.zeros((128, 3, 8), np.float32)
    m = np.arange(128)
    for g in range(3):
        d = DIL[g]
        for h in range(8):
            sb[:, g, h] = -sl[g, h] * d * (128 - m)
    return et.reshape(3, 3, 128, 1024), sb.reshape(128, 24)


_NC_CACHE = {}


def prep_inputs(x_prompt, x_sample, state_conv, state_h, cache_kv_w128, cache_kv_w512, cache_kv_w2048,
                norm_pre, norm_post, w_in, conv_w, conv_b, lru_w_a, lru_b_a, lru_w_x, lru_b_x, lru_lambda,
                w_rnn_out, w_attn_out, w_out):
    f = lambda a: np.ascontiguousarray(np.asarray(a, dtype=np.float32))
    x_prompt, x_sample, state_conv, state_h = f(x_prompt), f(x_sample), f(state_conv), f(state_h)
    caches = [f(cache_kv_w128), f(cache_kv_w512), f(cache_kv_w2048)]
    w_in = f(w_in)[0]
    def tile_w(w, nchunk, kcs):
        return np.ascontiguousarray(w.reshape(kcs, 128, nchunk, 128).transpose(2, 1, 0, 3).reshape(nchunk, 128, kcs * 128))
    win = tile_w(w_in, 72, 8)
    w4 = w_in.reshape(8, 128, 9216)
    parts = []
    for g in range(3):
        for hh in range(2):
            cs = g * 512 + hh * 256
            parts.append(np.concatenate([w4[:, :, 2048 + cs:2048 + cs + 256], w4[:, :, 3584 + cs:3584 + cs + 256],
                                         w4[:, :, 5120 + cs:5120 + cs + 256]], axis=2).transpose(1, 0, 2).reshape(128, 6144))
    wqkv = np.ascontiguousarray(np.stack(parts))
    wro = tile_w(f(w_rnn_out)[0], 8, 8)
    wao = tile_w(f(w_attn_out)[0], 8, 4)
    wout = np.ascontiguousarray(f(w_out)[0].reshape(8, 128, 1024).transpose(1, 0, 2).reshape(128, 8192))
    wga = np.stack([np.ascontiguousarray(w.reshape(8, 2, 64, 64).transpose(1, 2, 0, 3).reshape(2, 64, 512))
                    for w in (f(lru_w_a)[0], f(lru_w_x)[0])])
    prm = np.concatenate([f(conv_w)[0].reshape(32, 128), f(conv_b)[0].reshape(8, 128), f(lru_b_a)[0].reshape(8, 128),
                          f(lru_b_x)[0].reshape(8, 128), f(lru_lambda)[0].reshape(8, 128)], axis=0)
    ident = np.eye(128, dtype=np.float32)
    selq = np.zeros((4, 4, 128), np.float32)
    selo = np.zeros((128, 4, 4), np.float32)
    for b in range(4):
        selq[b, b, :] = 1.0
        selo[:, b, b] = 1.0
    selq = selq.reshape(4, 512)
    selo = selo.reshape(128, 16)
    tabs = [_tables(0), _tables(1)]
    in_maps = []
    for c in range(8):
        b, half = c // 2, c % 2
        xm = x_prompt[b, half * 2048:(half + 1) * 2048]
        xp = x_prompt[b, 0:2048] if half == 1 else np.zeros((2048, 1024), np.float32)
        sl = slice(4 * c, 4 * c + 4)
        sst = np.concatenate([state_conv[0, sl].reshape(12, 1024), state_h[0, sl]], axis=0)
        in_maps.append({
            "xm": np.ascontiguousarray(xm), "xp": np.ascontiguousarray(xp),
            "pv": np.full((128, 1), float(half), np.float32),
            "xs": np.ascontiguousarray(x_sample[sl, 0, :]), "sst": np.ascontiguousarray(sst),
            "sconv": np.ascontiguousarray(state_conv[0, sl]),
            "c0": np.ascontiguousarray(caches[0][0, sl].reshape(4, 128, 2, 512)),
            "c1": np.ascontiguousarray(caches[1][0, sl].reshape(4, 512, 2, 512)),
            "c2": np.ascontiguousarray(caches[2][0, sl].reshape(4, 2048, 2, 512)),
            "win": win, "wqkv": wqkv, "wro": wro, "wao": wao, "wout": wout, "wga": wga, "prm": prm, "cbrow": f(conv_b).reshape(1, 1024),
            "gpre": f(norm_pre).reshape(1, 1024), "gpost": f(norm_post).reshape(1, 1024),
            "etab": tabs[half][0], "sbias": tabs[half][1], "ident": ident, "selq": selq, "selo": selo,
        })
    return in_maps


def kernel(**inputs):
    in_maps = prep_inputs(**inputs)
    if "nc" not in _NC_CACHE:
        _NC_CACHE["nc"] = build_program()
    nc = _NC_CACHE["nc"]
    res = run_bass_kernel_spmd(nc, in_maps, core_ids=list(range(8)))
    R = res.results
    yp = np.zeros((4, 4096, 1024), np.float32)
    ys = np.zeros((32, 1, 1024), np.float32)
    conv_p = np.zeros((1, 4, 3, 1024), np.float32)
    conv_s = np.zeros((1, 32, 3, 1024), np.float32)
    h_p = np.zeros((1, 4, 1024), np.float32)
    h_s = np.zeros((1, 32, 1024), np.float32)
    kvp = [np.zeros((1, 4, 128 * DIL[g], 2, 8, 64), np.float32) for g in range(3)]
    kvsm = [np.zeros((1, 32, 1, 2, 8, 64), np.float32) for g in range(3)]
    for c in range(8):
        b, half = c // 2, c % 2
        r = R[c]
        yp[b, half * 2048:(half + 1) * 2048] = r["y"]
        sl = slice(4 * c, 4 * c + 4)
        ys[sl, 0] = r["ys"]
        conv_s[0, sl, 0:2] = r["convs_s"]
        conv_s[0, sl, 2] = r["convs_u"].reshape(128, 8, 4).transpose(2, 1, 0).reshape(4, 1024)
        h_s[0, sl] = r["hs"].reshape(128, 8, 4).transpose(2, 1, 0).reshape(4, 1024)
        for g in range(3):
            kvsm[g][0, sl, 0] = r["kvs"][g].reshape(4, 2, 8, 64)
        if half == 1:
            conv_p[0, b] = r["convp"].reshape(128, 8, 3).transpose(2, 1, 0).reshape(3, 1024)
            h_p[0, b] = r["hp"].reshape(128, 8).transpose(1, 0).reshape(1024)
            for g in range(3):
                kvp[g][0, b] = r[f"kvo{g}"].reshape(128 * DIL[g], 2, 8, 64)
    return (yp, ys, conv_p, conv_s, h_p, h_s, kvp[0], kvsm[0], kvp[1], kvsm[1], kvp[2], kvsm[2])
```

```python
from contextlib import ExitStack
import numpy as np
import concourse.bass as bass
import concourse.mybir as mybir
from concourse.bass_utils import run_bass_kernel_spmd

F32 = mybir.dt.float32
BF16 = mybir.dt.bfloat16
AF = mybir.ActivationFunctionType
ALU = mybir.AluOpType
AX = mybir.AxisListType

NT = 2048
XW = 4100
FW = 2052
DIL = (1, 4, 16)
EPS = 1e-6


class Res:
    __slots__ = ("name", "lastw", "readers", "sem", "semcnt", "excl")

    def __init__(self, name):
        self.name = name
        self.lastw = None
        self.readers = []
        self.sem = None
        self.semcnt = 0
        self.excl = name.startswith("pb") and name[2:].isdigit()


class Op:
    __slots__ = ("idx", "eng", "fn", "owner", "preds", "dur", "tok", "fin")

    def __init__(self, idx, eng, fn, owner, preds, dur):
        self.idx = idx
        self.eng = eng
        self.fn = fn
        self.owner = owner
        self.preds = preds
        self.dur = dur
        self.tok = None
        self.fin = 0.0


class Sched:
    ENGS = ("sync", "scalar", "gpsimd", "vector", "tensor")
    WINDOW = 160

    def __init__(self, nc, stack):
        self.nc = nc
        self.stack = stack
        self.out = {e: [] for e in self.ENGS}
        self.esem = {}
        self.ecnt = {e: 0 for e in self.ENGS}
        self.known = {e: {} for e in self.ENGS}
        self.nsem = 0
        self.owners = []
        self.seg = []
        self.nops = 0
        for e in ("scalar", "gpsimd", "vector", "tensor"):
            self.esem[e] = self.newsem("e_" + e)

    def newsem(self, name):
        self.nsem += 1
        return self.stack.enter_context(self.nc.semaphore(name))

    def _record(self, eng, fn, owner, reads, writes):
        ex = [r for r in reads if r.excl]
        if ex:
            writes = list(writes) + [r for r in ex if r not in writes]
            reads = [r for r in reads if not r.excl]
        preds = set()
        for r in reads:
            if r.lastw is not None:
                preds.add(r.lastw)
        for w in writes:
            if w.lastw is not None:
                preds.add(w.lastw)
            preds.update(w.readers)
        op = Op(self.nops, eng, fn, owner, preds, getattr(fn, "dur", 0.2))
        self.nops += 1
        for w in writes:
            w.lastw = op
            w.readers = []
        for r in reads:
            if r not in writes:
                r.readers.append(op)
        self.seg.append(op)
        return op

    def op(self, eng, fn, reads=(), writes=()):
        self._record(eng, fn, None, reads, writes)

    def dma(self, eng, fn, owner, reads=(), writes=()):
        if owner.sem is None:
            owner.sem = self.newsem("d_" + owner.name)
            self.owners.append(owner)
        self._record(eng, fn, owner, reads, writes)

    def _schedule_segment(self):
        seg = self.seg
        self.seg = []
        if not seg:
            return
        segset = set(id(o) for o in seg)
        pend = {e: [] for e in self.ENGS}
        for o in seg:
            o.preds = [p for p in o.preds if id(p) in segset]
            pend[o.eng].append(o)
        pos = {e: 0 for e in self.ENGS}
        free = {e: 0.0 for e in self.ENGS}
        done = set()
        order = {e: [] for e in self.ENGS}
        remaining = len(seg)
        W = self.WINDOW
        while remaining:
            best = None
            for e in self.ENGS:
                lst = pend[e]
                if not lst:
                    continue
                fe = free[e]
                cnt = 0
                for o in lst:
                    cnt += 1
                    if cnt > W:
                        break
                    rdy = 0.0
                    ok = True
                    for p in o.preds:
                        if id(p) not in done:
                            ok = False
                            break
                        if p.fin > rdy:
                            rdy = p.fin
                    if not ok:
                        continue
                    st = fe if fe > rdy else rdy
                    if best is None or st < best[0] - 1e-9:
                        best = (st, e, o)
                    if rdy <= fe:
                        break
            assert best is not None, "scheduler deadlock"
            st, e, o = best
            pend[e].remove(o)
            if e in ("sync", "gpsimd") and o.owner is not None:
                free[e] = st + 0.06
                o.fin = st + o.dur
            else:
                free[e] = st + o.dur
                o.fin = st + o.dur
            done.add(id(o))
            order[e].append(o)
            remaining -= 1
        allorder = sorted(seg, key=lambda o: (o.fin - o.dur if o.owner is None else o.fin - o.dur, o.idx))
        for e in self.ENGS:
            for o in order[e]:
                if o.owner is not None:
                    o.owner.semcnt += 16
                    o.tok = (o.owner.sem, o.owner.semcnt)
                else:
                    self.ecnt[e] += 1
                    o.tok = (self.esem[e], self.ecnt[e])
        for e in self.ENGS:
            kn = self.known[e]
            for o in order[e]:
                waits = []
                for p in o.preds:
                    if e == "tensor" and p.eng == "tensor" and p.owner is None:
                        continue
                    s_, v_ = p.tok
                    if kn.get(id(s_), 0) < v_:
                        kn[id(s_)] = v_
                        waits.append((s_, v_))
                self.out[e].append((waits, o.fn, o.tok[0], 16 if o.owner is not None else 1))

    def barrier(self):
        self._schedule_segment()
        allsems = [(self.esem[e], self.ecnt[e]) for e in self.esem] + [(o.sem, o.semcnt) for o in self.owners]
        for eng in self.ENGS:
            kn = self.known[eng]
            waits = []
            for s, v in allsems:
                if v > 0 and kn.get(id(s), 0) < v:
                    kn[id(s)] = v
                    waits.append((s, v))
            self.out[eng].append((waits, None, None, 0))

    def emit(self):
        self._schedule_segment()
        nc = self.nc
        with nc.Block() as block:
            for ename in self.ENGS:
                ops = self.out[ename]
                if not ops:
                    continue

                def body(e, ops=ops):
                    for waits, fn, sem, inc in ops:
                        for s, v in waits:
                            e.wait_ge(s, v)
                        if fn is not None:
                            fn(e).then_inc(sem, inc)

                getattr(block, ename)(body)


def _fsz(ap):
    n = 1
    for d in ap.shape[1:]:
        n *= d
    return n


def _d(f, dur):
    f.dur = dur
    return f


def COPY(out, in_):
    return _d(lambda e: e.tensor_copy(out=out, in_=in_), (_fsz(in_) + 60) / 960.0)


def ACT(out, in_, func, **kw):
    return _d(lambda e: e.activation(out=out, in_=in_, func=func, **kw), (_fsz(in_) + 250) / 1200.0)


def MM(out, lhsT, rhs, start, stop, skip=False):
    n = max(_fsz(rhs), 64) * (4 if rhs.dtype == F32 else 1)
    if skip:
        return _d(lambda e: e.matmul(out, lhsT=lhsT, rhs=rhs, start=start, stop=stop, skip_group_check=True), n / 2400.0 + 0.03)
    return _d(lambda e: e.matmul(out, lhsT=lhsT, rhs=rhs, start=start, stop=stop), n / 2400.0 + 0.03)


def TR(out, in_, ident):
    return _d(lambda e: e.transpose(out=out, in_=in_, identity=ident), 0.12 * (4 if in_.dtype == F32 else 1))


def TT(out, a, b, op):
    return _d(lambda e: e.tensor_tensor(out=out, in0=a, in1=b, op=op), (_fsz(a) + 60) / 960.0)


def STT(out, in0, scalar, in1, op0, op1):
    return _d(lambda e: e.scalar_tensor_tensor(out=out, in0=in0, scalar=scalar, in1=in1, op0=op0, op1=op1), (_fsz(in0) + 60) / 960.0)


def TS(out, in0, s1, s2, op0, op1=None):
    du = (_fsz(in0) + 60) / 960.0
    if op1 is None:
        return _d(lambda e: e.tensor_scalar(out=out, in0=in0, scalar1=s1, scalar2=None, op0=op0), du)
    return _d(lambda e: e.tensor_scalar(out=out, in0=in0, scalar1=s1, scalar2=s2, op0=op0, op1=op1), du)


def DMA(out, in_):
    nb = 4 * in_.shape[0] * _fsz(in_)
    return _d(lambda e: e.dma_start(out=out, in_=in_), 2.0 + nb / 150000.0)


def MEMSET(ap, v):
    return _d(lambda e: e.memset(ap, v), (_fsz(ap) + 60) / 960.0)


def RECIP(out, in_):
    return _d(lambda e: e.reciprocal(out=out, in_=in_), (_fsz(in_) + 60) / 960.0)


def SCAN(out, d0, d1, init):
    return _d(lambda e: e.tensor_tensor_scan(out=out, data0=d0, data1=d1, initial=init, op0=ALU.mult, op1=ALU.add), (2 * _fsz(d0) + 60) / 960.0)


def REDUCE(out, in_):
    return _d(lambda e: e.tensor_reduce(out=out, in_=in_, axis=AX.X, op=ALU.add), (_fsz(in_) + 60) / 960.0)


class Arena:
    def __init__(self, t, size):
        self.t = t
        self.size = size
        self.top = 0
        self.reserve_top = 0

    def f32(self, n):
        off = self.top
        self.top += n
        assert self.top <= self.size - self.reserve_top, (self.top, self.size, self.reserve_top)
        return self.t[:, off:off + n]

    def bf16(self, n):
        words = (n + 1) // 2
        return self.f32(words).bitcast(BF16)[:, 0:n]


def SL(start, n, step):
    return slice(start, start + (n - 1) * step + 1, step)


def block_cols(g, rho, n):
    d = DIL[g]
    return 2048 + rho + d * 128 * n, d


def build_program(upto=99, dbgfn=None, sub=99):
    nc = bass.Bass("TRN2", target_bir_lowering=False)

    def din(name, shape):
        return nc.dram_tensor(name, shape, F32, kind="ExternalInput").ap()

    def dout(name, shape):
        return nc.dram_tensor(name, shape, F32, kind="ExternalOutput").ap()

    xm = din("xm", [2048, 1024])
    xp = din("xp", [2048, 1024])
    pvd = din("pv", [128, 1])
    xsd = din("xs", [4, 1024])
    sst = din("sst", [16, 1024])
    sconv = din("sconv", [4, 3, 1024])
    cache = [din("c0", [4, 128, 2, 512]), din("c1", [4, 512, 2, 512]), din("c2", [4, 2048, 2, 512])]
    win = din("win", [72, 128, 1024])
    wqkv = din("wqkv", [6, 128, 6144])
    wro = din("wro", [8, 128, 1024])
    wao = din("wao", [8, 128, 512])
    wout = din("wout", [128, 8192])
    wga = din("wga", [2, 2, 64, 512])
    prm = din("prm", [64, 128])
    cbrow = din("cbrow", [1, 1024])
    gpre = din("gpre", [1, 1024])
    gpost = din("gpost", [1, 1024])
    etab = din("etab", [3, 3, 128, 1024])
    sbias = din("sbias", [128, 24])
    identin = din("ident", [128, 128])
    selq = din("selq", [4, 512])
    selo = din("selo", [128, 16])

    y = dout("y", [2048, 1024])
    convp = dout("convp", [128, 24])
    hp = dout("hp", [128, 8])
    kvo = [dout("kvo0", [128, 2, 512]), dout("kvo1", [512, 2, 512]), dout("kvo2", [2048, 2, 512])]
    ys = dout("ys", [4, 1024])
    convs_u = dout("convs_u", [128, 32])
    convs_s = dout("convs_s", [4, 2, 1024])
    hs = dout("hs", [128, 32])
    kvs = dout("kvs", [3, 4, 2, 512])
    dbg = dout("dbg", [128, 8192]) if upto != 99 else None
    osc = [nc.dram_tensor(f"osc{g}", [2048, 2, 260], F32).ap() for g in range(3)]

    with ExitStack() as st:
        S = Sched(nc, st)
        AW = 53200
        arena_t = st.enter_context(nc.sbuf_tensor("arena", [128, AW], F32))
        A = Arena(arena_t, AW)
        ps = [st.enter_context(nc.psum_tensor(f"ps{i}", [128, 512], F32)) for i in range(8)]
        PB = [Res(f"pb{i}") for i in range(8)]
        bank_i = [0]

        def nextbank():
            i = bank_i[0]
            bank_i[0] = (i + 1) % 8
            return ps[i], PB[i]

        XNT = A.bf16(8 * XW).rearrange("p (k n) -> p k n", k=8)
        HZT = A.bf16(8 * FW).rearrange("p (k n) -> p k n", k=8)
        IDENTF = A.f32(128)
        IDENT = A.bf16(128)
        PRM = A.f32(64)
        CLT = A.f32(16)
        PVT = A.f32(1)
        HPV = A.f32(1)
        HB = A.f32(16)
        WG = A.bf16(8 * 2 * 128).rearrange("p (c g o) -> p c g o", c=8, g=2)
        HOUT = A.f32(8)
        CONVP = A.f32(24).rearrange("p (c t) -> p c t", c=8)
        SNUM = A.f32(520)
        SELQ = A.f32(512)
        SELO = A.f32(16)
        SBIAS = A.f32(24)
        SMALL = A.f32(64)
        SSTT = A.f32(128)
        SUC = A.f32(32)
        SZS = A.f32(32)
        SH = A.f32(32)
        rXNT = [Res(f"xnt{i}") for i in range(33)]
        rHZT = [Res(f"hzt{c}") for c in range(8)]
        rHZTs = Res("hzts")
        rC = Res("consts")
        rSNUM = Res("snum")
        rSM = Res("small")
        rSST = Res("sstt")
        rSUC = Res("suc")
        rSZS = Res("szs")
        rSH = Res("sh")
        rOUTS = Res("outs_small")
        SCR0 = A.top

        A.top = SCR0
        PRMRAW = A.f32(128)
        SSRAW = A.f32(1024)
        GPRE = A.f32(1024)
        rT = Res("p0tmp")
        rG = Res("gpre")
        rWGL = Res("wgload")
        NX = 3
        XT = [A.f32(1024) for _ in range(NX)]
        XB = [A.bf16(1024) for _ in range(NX)]
        rXT = [Res("xt%d" % i) for i in range(NX)]
        rXB = [Res("xb%d" % i) for i in range(NX)]
        tiles = [(xp, i, i * 128, 128) for i in range(16)] + [(xm, i, 2048 + i * 128, 128) for i in range(16)] + [(xsd, 0, 4096, 4)]

        def load_x(idx):
            src, i, col, np_ = tiles[idx]
            b = idx % NX
            S.dma("sync", DMA(XT[b][0:np_, :], src[i * 128:i * 128 + np_, :]), rXT[b], writes=[rXT[b]])

        load_x(0)
        S.dma("sync", DMA(IDENTF, identin), rC, writes=[rC])
        S.dma("sync", DMA(GPRE, gpre.partition_broadcast(128)), rG, writes=[rG])
        load_x(1)
        S.dma("sync", DMA(PRMRAW[0:64, :], prm), rT, writes=[rT])
        S.dma("sync", DMA(SSRAW[0:16, :], sst), rT, writes=[rT])
        S.dma("sync", DMA(PVT, pvd), rC, writes=[rC])
        rC2 = Res("consts_late")
        S.dma("sync", DMA(SELQ[0:4, :], selq), rC2, writes=[Res("c2a")])
        S.dma("sync", DMA(SELO, selo), rC2, writes=[Res("c2b")])
        S.dma("sync", DMA(SBIAS, sbias), rC2, writes=[Res("c2c")])
        S.op("vector", COPY(IDENT, IDENTF), reads=[rC], writes=[rC])
        S.op("vector", MEMSET(WG.rearrange("p c g o -> p (c g o)"), 0.0), writes=[rC])
        S.op("vector", MEMSET(SNUM[0:4, :], 0.0), writes=[rSNUM])
        for g in range(2):
            for hf in range(2):
                S.dma("gpsimd", DMA(WG[hf * 64:(hf + 1) * 64, :, g, hf * 64:(hf + 1) * 64],
                                    wga[g, hf].rearrange("i (c o) -> i c o", c=8)), rWGL, reads=[rC], writes=[rC])
        pb, rb = nextbank()
        S.op("tensor", TR(pb[:, 0:64], PRMRAW[0:64, :], IDENTF[0:64, 0:64]), reads=[rT, rC], writes=[rb])
        S.op("vector", COPY(PRM, pb[:, 0:64]), reads=[rb], writes=[rC])
        pb, rb = nextbank()
        for c in range(8):
            S.op("tensor", TR(pb[:, c * 16:(c + 1) * 16], SSRAW[0:16, c * 128:(c + 1) * 128], IDENTF[0:16, 0:16]),
                 reads=[rT, rC], writes=[rb])
        S.op("vector", COPY(SSTT, pb[:, 0:128]), reads=[rb], writes=[rSST])
        SSTT3 = SSTT.rearrange("p (c r) -> p c r", c=8)
        T1 = SMALL[:, 0:8]
        T2 = SMALL[:, 8:16]
        T3 = SMALL[:, 16:24]
        T4 = SMALL[:, 24:32]
        S.op("scalar", ACT(T1, PRM[:, 56:64], AF.Exp, scale=-1.0), reads=[rC], writes=[rSM])
        S.op("vector", TS(T2, T1, 2.0, None, ALU.add), reads=[rSM], writes=[rSM])
        S.op("vector", RECIP(T2, T2), reads=[rSM], writes=[rSM])
        S.op("vector", TT(T3, T1, T2, ALU.mult), reads=[rSM], writes=[rSM])
        S.op("vector", TT(T4, T3, T3, ALU.mult), reads=[rSM], writes=[rSM])
        S.op("vector", TS(T2, T4, 1.0 / 9, 1.0 / 7, ALU.mult, ALU.add), reads=[rSM], writes=[rSM])
        for cst in (1.0 / 5, 1.0 / 3, 1.0):
            S.op("vector", TT(T2, T2, T4, ALU.mult), reads=[rSM], writes=[rSM])
            S.op("vector", TS(T2, T2, cst, None, ALU.add), reads=[rSM], writes=[rSM])
        S.op("vector", TT(T2, T2, T3, ALU.mult), reads=[rSM], writes=[rSM])
        S.op("vector", TS(CLT[:, 0:8], T2, -8.0, None, ALU.mult), reads=[rSM], writes=[rC])
        S.op("vector", TS(CLT[:, 8:16], T2, -16.0, None, ALU.mult), reads=[rSM], writes=[rC])
        S.op("vector", TS(HB, PRM[:, 40:56], 0.5, None, ALU.mult), reads=[rC], writes=[rC])
        S.op("vector", TS(HPV, PVT, 0.5, None, ALU.mult), reads=[rC], writes=[rC])
        S.dma("sync", DMA(convs_s, sconv[:, 1:3, :]), rOUTS, writes=[rOUTS])

        for idx, (src, i, col, np_) in enumerate(tiles):
            b = idx % NX
            if idx + 2 < len(tiles):
                load_x(idx + 2)
            ss = SMALL[0:np_, 32 + b:33 + b]
            rs = SMALL[0:np_, 36 + b:37 + b]
            rss = Res("ss")
            S.op("scalar", ACT(XB[b][0:np_, :], XT[b][0:np_, :], AF.Square, accum_out=ss), reads=[rXT[b]], writes=[rXB[b], rss])
            S.op("scalar", ACT(rs, ss, AF.Sqrt, scale=1.0 / 1024, bias=EPS), reads=[rss], writes=[rss])
            S.op("vector", RECIP(rs, rs), reads=[rss], writes=[rss])
            S.op("vector", STT(XB[b][0:np_, :], XT[b][0:np_, :], rs, GPRE[0:np_, :], ALU.mult, ALU.mult),
                 reads=[rXT[b], rss, rG], writes=[rXB[b]])
            pb, rb = nextbank()
            pbb = pb[:, :].bitcast(BF16)
            for kc in range(8):
                S.op("tensor", TR(pbb[:, kc * np_:(kc + 1) * np_], XB[b][0:np_, kc * 128:(kc + 1) * 128], IDENT[0:np_, 0:np_]),
                     reads=[rXB[b], rC], writes=[rb])
            S.op("vector", COPY(XNT[:, :, col:col + np_], pbb[:, 0:8 * np_].rearrange("p (k n) -> p k n", k=8)),
                 reads=[rb], writes=[rXNT[idx]])
        def xr(col, n):
            return [rXNT[i] for i in range(col // 128, (col + n - 1) // 128 + 1)]

        NS = 1024
        WU = [A.bf16(1024).rearrange("p (k n) -> p k n", k=8) for _ in range(2)]
        WZ = [A.bf16(1024).rearrange("p (k n) -> p k n", k=8) for _ in range(2)]
        rWU = [Res("wu0"), Res("wu1")]
        rWZ = [Res("wz0"), Res("wz1")]
        NSET = 3
        Ub = [A.bf16(NS + 4) for _ in range(NSET)]
        DG = [A.bf16(512).rearrange("p (t n) -> p t n", t=4) for _ in range(2)]
        CBROW = A.bf16(1024)
        ONES = A.bf16(512)
        INITS = A.f32(8)
        rDG = [Res("dg0"), Res("dg1")]
        rCB = Res("cbrow")
        S.dma("gpsimd", DMA(CBROW[0:1, :], cbrow), rCB, writes=[rCB])
        S.op("vector", MEMSET(ONES[0:1, :], 1.0), writes=[rCB])
        UC = [A.f32(NS) for _ in range(NSET)]
        Rb = [A.f32(NS) for _ in range(NSET)]
        Ib = [A.f32(NS) for _ in range(NSET)]
        Ab = [A.f32(NS) for _ in range(NSET)]
        UCB = [A.bf16(NS) for _ in range(NSET)]
        rU = [Res("u%d" % i) for i in range(NSET)]
        rUC = [Res("uc%d" % i) for i in range(NSET)]
        rR = [Res("r%d" % i) for i in range(NSET)]
        rI = [Res("i%d" % i) for i in range(NSET)]
        rA = [Res("a%d" % i) for i in range(NSET)]
        rUCB = [Res("ucb%d" % i) for i in range(NSET)]

        def load_w2(c):
            b = c % 2
            S.dma("gpsimd", DMA(WU[b], win[c].rearrange("p (k n) -> p k n", k=8)), rWU[b], writes=[rWU[b]])
            S.dma("gpsimd", DMA(WZ[b], win[8 + c].rearrange("p (k n) -> p k n", k=8)), rWZ[b], writes=[rWZ[b]])

        load_w2(0)

        def prm_of(c):
            return dict(w0=PRM[:, 0 + c:1 + c], w1=PRM[:, 8 + c:9 + c], w2=PRM[:, 16 + c:17 + c], w3=PRM[:, 24 + c:25 + c],
                        cb=PRM[:, 32 + c:33 + c], ba=HB[:, c:c + 1], bx=HB[:, 8 + c:9 + c], clh=CLT[:, c:c + 1], cl=CLT[:, 8 + c:9 + c])

        def stageA(it):
            c, s = it // 4, it % 4
            P = prm_of(c)
            wb = c % 2
            tb = it % NSET
            pbuf = (it - 1) % NSET
            col0 = s * NS
            U, rUU = Ub[tb], rU[tb]
            dg, rdg = DG[c % 2], rDG[c % 2]
            if s == 0:
                for tap, wt in enumerate((P["w0"], P["w1"], P["w2"], P["w3"])):
                    S.op("vector", TS(dg[:, tap, :], IDENTF, wt, None, ALU.mult), reads=[rC], writes=[rdg])
                S.op("vector", MEMSET(U[:, 0:3], 0.0), writes=[rUU])
            else:
                S.op("vector", COPY(U[:, 0:3], Ub[pbuf][:, NS:NS + 3]), reads=[rU[pbuf]], writes=[rUU])
            for tt in range(2):
                pb, rb = nextbank()
                for kc in range(8):
                    S.op("tensor", MM(pb[:, :], WU[wb][:, kc, :], XNT[:, kc, col0 + tt * 512:col0 + (tt + 1) * 512], kc == 0, kc == 7),
                         reads=[rWU[wb]] + xr(col0 + tt * 512, 512), writes=[rb])
                S.op("vector", COPY(U[:, 3 + tt * 512:3 + (tt + 1) * 512], pb[:, :]), reads=[rb], writes=[rUU])
                if s == 3 and tt == 1:
                    S.op("vector", COPY(CONVP[:, c, :], pb[:, 509:512]), reads=[rb], writes=[rOUTS])

        def stageA2(it):
            c, s = it // 4, it % 4
            P = prm_of(c)
            tb = it % NSET
            U, rUU = Ub[tb], rU[tb]
            dg, rdg = DG[c % 2], rDG[c % 2]
            cbanks = []
            for tt in range(2):
                pb, rb = nextbank()
                for tap in range(4):
                    S.op("tensor", MM(pb[:, :], dg[:, tap, :], U[:, tap + tt * 512:tap + (tt + 1) * 512], tap == 0, False),
                         reads=[rdg, rUU], writes=[rb])
                S.op("tensor", MM(pb[:, :], CBROW[0:1, c * 128:(c + 1) * 128], ONES[0:1, :], False, True), reads=[rCB], writes=[rb])
                sl = slice(tt * 512, (tt + 1) * 512)
                if tt == 0:
                    S.op("vector", COPY(UCB[tb][:, sl], pb[:, :]), reads=[rb], writes=[rUCB[tb]])
                else:
                    S.op("scalar", ACT(UCB[tb][:, sl], pb[:, :], AF.Copy), reads=[rb], writes=[rUCB[tb]])
                cbanks.append((pb, rb))
            for tt in range(2):
                sl = slice(tt * 512, (tt + 1) * 512)
                pb, rb = nextbank()
                S.op("tensor", MM(pb[:, :], WG[:, c, 0, :], UCB[tb][:, sl], True, True), reads=[rC, rUCB[tb]], writes=[rb])
                S.op("scalar", ACT(Rb[tb][:, sl], pb[:, :], AF.Tanh, bias=P["ba"], scale=0.5), reads=[rb, rC], writes=[rR[tb]])
                pb, rb = nextbank()
                S.op("tensor", MM(pb[:, :], WG[:, c, 1, :], UCB[tb][:, sl], True, True), reads=[rC, rUCB[tb]], writes=[rb])
                S.op("scalar", ACT(Ib[tb][:, sl], pb[:, :], AF.Tanh, bias=P["bx"], scale=0.5), reads=[rb, rC], writes=[rI[tb]])
                cpb, crb = cbanks[tt]
                S.op("vector", STT(Ib[tb][:, sl], Ib[tb][:, sl], 1.0, cpb[:, :], ALU.add, ALU.mult), reads=[crb, rI[tb]], writes=[rI[tb]])
            S.op("scalar", ACT(Ab[tb], Rb[tb], AF.Exp, scale=P["clh"], bias=P["clh"]), reads=[rR[tb], rC], writes=[rA[tb]])
            S.op("scalar", ACT(Rb[tb], Rb[tb], AF.Exp, scale=P["cl"], bias=P["cl"]), reads=[rR[tb], rC], writes=[rR[tb]])

        def stageB(it):
            c, s = it // 4, it % 4
            wb = c % 2
            tb = it % NSET
            pbuf = (it - 1) % NSET
            col0 = s * NS
            S.op("scalar", ACT(Rb[tb], Rb[tb], AF.Sqrt, scale=-1.0, bias=1.0), reads=[rR[tb]], writes=[rR[tb]])
            S.op("vector", TT(Ib[tb], Ib[tb], Rb[tb], ALU.mult), reads=[rI[tb], rR[tb]], writes=[rI[tb]])
            if s == 0:
                init, rds = 0.0, [rA[tb], rI[tb]]
            elif s == 2:
                init = INITS[:, c:c + 1]
                S.op("vector", TT(init, UC[pbuf][:, NS - 1:NS], PVT[:, 0:1], ALU.mult), reads=[rUC[pbuf], rC], writes=[rSM])
                rds = [rA[tb], rI[tb], rSM]
            else:
                init, rds = UC[pbuf][:, NS - 1:NS], [rA[tb], rI[tb], rUC[pbuf]]
            S.op("vector", SCAN(UC[tb], Ab[tb], Ib[tb], init), reads=rds, writes=[rUC[tb]])
            if s >= 2:
                for tt in range(2):
                    sl = slice(tt * 512, (tt + 1) * 512)
                    pb, rb = nextbank()
                    for kc in range(8):
                        S.op("tensor", MM(pb[:, :], WZ[wb][:, kc, :], XNT[:, kc, col0 + tt * 512:col0 + (tt + 1) * 512], kc == 0, kc == 7),
                             reads=[rWZ[wb]] + xr(col0 + tt * 512, 512), writes=[rb])
                    S.op("scalar", ACT(Rb[tb][:, sl], pb[:, :], AF.Tanh, scale=0.5), reads=[rb], writes=[rR[tb]])
                    S.op("vector", STT(Rb[tb][:, sl], Rb[tb][:, sl], 1.0, pb[:, :], ALU.add, ALU.mult), reads=[rb, rR[tb]], writes=[rR[tb]])
                S.op("vector", STT(HZT[:, c, (s - 2) * NS:(s - 1) * NS], Rb[tb], 0.25, UC[tb], ALU.mult, ALU.mult), reads=[rUC[tb], rR[tb]], writes=[rHZT[c]])
            if s == 3:
                S.op("vector", TS(HOUT[:, c:c + 1], UC[tb][:, NS - 1:NS], 0.5, None, ALU.mult), reads=[rUC[tb]], writes=[rOUTS])
                pb, rb = nextbank()
                for kc in range(8):
                    S.op("tensor", MM(pb[:, 0:4], WU[wb][:, kc, :], XNT[:, kc, 4096:4100], kc == 0, kc == 7), reads=[rWU[wb], rXNT[32]], writes=[rb])
                S.op("vector", COPY(SUC[:, c * 4:(c + 1) * 4], pb[:, 0:4]), reads=[rb], writes=[rSUC])
                pb, rb = nextbank()
                for kc in range(8):
                    S.op("tensor", MM(pb[:, 0:4], WZ[wb][:, kc, :], XNT[:, kc, 4096:4100], kc == 0, kc == 7), reads=[rWZ[wb], rXNT[32]], writes=[rb])
                S.op("scalar", ACT(SZS[:, c * 4:(c + 1) * 4], pb[:, 0:4], AF.Tanh, scale=0.5), reads=[rb], writes=[rSZS])
                S.op("vector", STT(SZS[:, c * 4:(c + 1) * 4], SZS[:, c * 4:(c + 1) * 4], 1.0, pb[:, 0:4], ALU.add, ALU.mult), reads=[rb, rSZS], writes=[rSZS])
                S.op("vector", TS(SZS[:, c * 4:(c + 1) * 4], SZS[:, c * 4:(c + 1) * 4], 0.5, None, ALU.mult), reads=[rSZS], writes=[rSZS])

        SW = A.f32(64)
        SWB = A.bf16(8)
        rSW = Res("sw")
        def sample_step(c):
                usl = SUC[:, c * 4:(c + 1) * 4]
                uc = SW[:, 0:4]
                st_ = SSTT3[:, c, 0:12].rearrange("p (b t) -> p b t", b=4)
                S.op("vector", TS(uc, usl, PRM[:, 24 + c:25 + c], PRM[:, 32 + c:33 + c], ALU.mult, ALU.add), reads=[rSUC, rC], writes=[rSW])
                for tap in range(3):
                    S.op("vector", STT(uc, st_[:, :, tap], PRM[:, tap * 8 + c:tap * 8 + c + 1], uc, ALU.mult, ALU.add), reads=[rSST, rC, rSW], writes=[rSW])
                S.op("vector", COPY(SWB[:, 0:4], uc), reads=[rSW], writes=[rSW])
                pb, rb = nextbank()
                S.op("tensor", MM(pb[:, 0:4], WG[:, c, 0, :], SWB[:, 0:4], True, True), reads=[rC, rSW], writes=[rb])
                S.op("tensor", MM(pb[:, 4:8], WG[:, c, 1, :], SWB[:, 0:4], True, True), reads=[rC, rSW], writes=[rb])
                r_ = SW[:, 4:8]
                i_ = SW[:, 8:12]
                a_ = SW[:, 12:16]
                S.op("scalar", ACT(r_, pb[:, 0:4], AF.Tanh, bias=HB[:, c:c + 1], scale=0.5), reads=[rb, rC], writes=[rSW])
                S.op("scalar", ACT(i_, pb[:, 4:8], AF.Tanh, bias=HB[:, 8 + c:9 + c], scale=0.5), reads=[rb, rC], writes=[rSW])
                S.op("scalar", ACT(a_, r_, AF.Exp, scale=CLT[:, c:c + 1], bias=CLT[:, c:c + 1]), reads=[rSW, rC], writes=[rSW])
                S.op("scalar", ACT(r_, r_, AF.Exp, scale=CLT[:, 8 + c:9 + c], bias=CLT[:, 8 + c:9 + c]), reads=[rSW, rC], writes=[rSW])
                S.op("scalar", ACT(r_, r_, AF.Sqrt, scale=-1.0, bias=1.0), reads=[rSW], writes=[rSW])
                S.op("vector", STT(i_, i_, 1.0, r_, ALU.add, ALU.mult), reads=[rSW], writes=[rSW])
                S.op("vector", STT(i_, i_, 0.5, uc, ALU.mult, ALU.mult), reads=[rSW], writes=[rSW])
                S.op("vector", TT(a_, a_, SSTT3[:, c, 12:16], ALU.mult), reads=[rSW, rSST], writes=[rSW])
                S.op("vector", TT(SH[:, c * 4:(c + 1) * 4], a_, i_, ALU.add), reads=[rSW], writes=[rSH])
                S.op("vector", TT(HZT[:, c, 2048:2052], SH[:, c * 4:(c + 1) * 4], SZS[:, c * 4:(c + 1) * 4], ALU.mult), reads=[rSH, rSZS], writes=[rHZTs])

        NIT = 32
        load_w2(1)
        WQKV_first = arena_t[:, SCR0:SCR0 + 3072].bitcast(BF16).rearrange("p (k n) -> p k n", k=8)
        rWQ2 = [Res("wq0"), Res("wq1")]
        stageA(0)
        stageA(1)
        stageA2(0)
        for it in range(NIT):
            if it + 2 < NIT:
                stageA(it + 2)
            if it + 1 < NIT:
                stageA2(it + 1)
            stageB(it)
            if it % 4 == 3:
                if it // 4 + 2 < 8:
                    load_w2(it // 4 + 2)
                sample_step(it // 4)
            if it == 13:
                S.dma("gpsimd", DMA(WQKV_first, wqkv[0].rearrange("p (k n) -> p k n", k=8)), rWQ2[0],
                      writes=[rWQ2[0], rT, rG, rXT[0], rXT[1]])
        S.dma("sync", DMA(convs_u, SUC), rSUC, reads=[rSUC])
        S.dma("sync", DMA(hs, SH), rSH, reads=[rSH])
        S.dma("sync", DMA(hp, HOUT), rOUTS, reads=[rOUTS])
        S.dma("sync", DMA(convp, CONVP.rearrange("p c t -> p (c t)")), rOUTS, reads=[rOUTS])
        S.barrier()
        rX = Res("xnt_all")
        if upto == 2:
            if dbgfn is not None:
                dbgfn(locals())
                S.barrier()
            S.emit()
            return nc

        A.top = SCR0
        WQKV2 = [A.bf16(8 * 768).rearrange("p (k n) -> p k n", k=8) for _ in range(2)]
        assert A.top == SCR0 + 6144
        NRB = 6
        KT = A.bf16(2 * NRB * 128).rearrange("p (h b n) -> p h b n", h=2, b=NRB)
        Vflat = A.bf16(NRB * 4 * 66)
        Vb = Vflat.rearrange("p (b h e) -> p b h e", b=NRB, h=4)
        Vm = Vflat.rearrange("p (m e) -> p m e", e=66)
        QT2 = [A.bf16(2 * 2048).rearrange("p (h n) -> p h n", h=2) for _ in range(2)]
        ET2 = [A.f32(3 * 512).rearrange("p (k n) -> p k n", k=3) for _ in range(2)]
        NR = 3
        KVF = [A.f32(512) for _ in range(2)] + [None]
        KB = [A.bf16(256) for _ in range(NR)]
        EX = [A.f32(512) for _ in range(NR)]
        PT = [A.bf16(512) for _ in range(NR)]
        OS = [A.f32(260) for _ in range(NR)]
        CK = [A.f32(512) for _ in range(2)]
        SQKV = A.f32(768)
        SPR = A.f32(260)
        SPV = A.f32(260)
        slot_of = {}
        rKT = [Res(f"kt{i}") for i in range(32)]
        rV = [Res(f"v{i}") for i in range(32)]
        rVones = Res("vones")
        rQT2 = [[[Res(f"qt{q}{h}{t}") for t in range(4)] for h in range(2)] for q in range(2)]
        rET2 = [Res("et0"), Res("et1")]
        rKVF = [Res("kvf0"), Res("kvf1")]
        rKB = [Res("kb%d" % i) for i in range(3)]
        rEX = [Res("ex%d" % i) for i in range(3)]
        rPT = [Res("pt%d" % i) for i in range(3)]
        rOS = [Res("os%d" % i) for i in range(3)]
        rCK = [Res("ck0"), Res("ck1")]
        rSQKV, rSPR, rSPV = Res("sqkv"), Res("spr"), Res("spv")
        rOSC = Res("osc")
        cnt = {"kvf": 0, "kb": 0, "ex": 0, "os": 0, "ck": 0, "blk": 0}
        S.op("vector", MEMSET(Vm[:, :, 64:65], 1.0), writes=[rVones] + rV[:NRB])
        for g in range(3):
            d = DIL[g]
            nb = 16 // d
            for hh in range(2):
                sp = g * 2 + hh
                if sp + 1 < 6:
                    S.dma("gpsimd", DMA(WQKV2[(sp + 1) % 2], wqkv[sp + 1].rearrange("p (k n) -> p k n", k=8)), rWQ2[(sp + 1) % 2], writes=[rWQ2[(sp + 1) % 2]])
                WQKV = WQKV2[sp % 2]
                WQ = WQKV[:, :, 0:256]
                WKV = WQKV[:, :, 256:768]
                rWQ = rWQ2[sp % 2]
                rWKV = rWQ
                QT, rQT = QT2[sp % 2], rQT2[sp % 2]
                ET, rET = ET2[sp % 2], rET2[sp % 2]
                for kind in range(3):
                    S.dma("sync", DMA(ET[:, kind, :], etab[g, kind][:, hh * 512:(hh + 1) * 512]), rET, writes=[rET])
                for hpp in range(2):
                    for tt in range(4):
                        pb, rb = nextbank()
                        for kc in range(8):
                            S.op("tensor", MM(pb[:, :], WQ[:, kc, hpp * 128:(hpp + 1) * 128], XNT[:, kc, 2048 + tt * 512:2048 + (tt + 1) * 512], kc == 0, kc == 7),
                                 reads=[rWQ, rX], writes=[rb])
                        md = 512 // d
                        qdst = QT[:, hpp, :].rearrange("p (r m) -> p r m", r=d)[:, :, tt * md:(tt + 1) * md]
                        S.op("scalar", ACT(qdst, pb[:, :].rearrange("p (m r) -> p r m", r=d), AF.Copy, scale=0.125), reads=[rb], writes=[rQT[hpp][tt]])
                pb, rb = nextbank()
                for kc in range(8):
                    S.op("tensor", MM(pb[0:4, 0:256], XNT[:, kc, 4096:4100], WQ[:, kc, :], kc == 0, kc == 7), reads=[rWQ, rX], writes=[rb])
                S.op("vector", COPY(SQKV[0:4, 0:256], pb[0:4, 0:256]), reads=[rb], writes=[rSQKV])
                pb, rb = nextbank()
                for kc in range(8):
                    S.op("tensor", MM(pb[0:4, :], XNT[:, kc, 4096:4100], WKV[:, kc, :], kc == 0, kc == 7), reads=[rWKV, rX], writes=[rb])
                S.op("vector", COPY(SQKV[0:4, 256:768], pb[0:4, :]), reads=[rb], writes=[rSQKV])
                S.dma("sync", DMA(kvs[g, :, :, hh * 256:(hh + 1) * 256], SQKV[0:4, 256:768].rearrange("p (t n) -> p t n", t=2)), rSQKV, reads=[rSQKV])

                def produce(rho, n):
                    bi = cnt["blk"] % NRB
                    cnt["blk"] += 1
                    slot_of[(g, hh, rho, n)] = bi
                    c0, stp = block_cols(g, rho, n)
                    pb, rb = nextbank()
                    for kc in range(8):
                        S.op("tensor", MM(pb[:, :], XNT[:, kc, SL(c0, 128, stp)], WKV[:, kc, :], kc == 0, kc == 7),
                             reads=[rX, rWKV], writes=[rb])
                    if n == nb - 1:
                        fb = cnt["kvf"] % 2
                        cnt["kvf"] += 1
                        S.op("scalar", ACT(KVF[fb], pb[:, :], AF.Copy), reads=[rb], writes=[rKVF[fb]])
                        dst = kvo[g][SL(rho, 128, d), :, hh * 256:(hh + 1) * 256]
                        S.dma("sync", DMA(dst, KVF[fb].rearrange("p (t n) -> p t n", t=2)), rKVF[fb], reads=[rKVF[fb]])
                    kb = cnt["kb"] % NR
                    cnt["kb"] += 1
                    S.op("vector", COPY(KB[kb], pb[:, 0:256]), reads=[rb], writes=[rKB[kb]])
                    S.op("vector", COPY(Vb[:, bi, :, 0:64], pb[:, 256:512].rearrange("p (h e) -> p h e", h=4)), reads=[rb, rVones], writes=[rV[bi]])
                    pb2, rb2 = nextbank()
                    pbb = pb2[:, :].bitcast(BF16)
                    for hpp in range(2):
                        S.op("tensor", TR(pbb[:, hpp * 128:(hpp + 1) * 128], KB[kb][:, hpp * 128:(hpp + 1) * 128], IDENT), reads=[rKB[kb], rC], writes=[rb2])
                    S.op("scalar", ACT(KT[:, :, bi, :], pbb[:, 0:256].rearrange("p (h n) -> p h n", h=2), AF.Copy), reads=[rb2], writes=[rKT[bi]])

                def attend(rho, n):
                    q0 = rho + d * 128 * n
                    qb0 = rho * (2048 // d) + 128 * n
                    qr = [rQT[0][t] for t in range(4)] + [rQT[1][t] for t in range(4)]
                    po, ro = nextbank()
                    for bk, nn in enumerate((n - 1, n)):
                        bi = slot_of[(g, hh, rho, nn)]
                        pbs = [nextbank(), nextbank()]
                        for h4 in range(4):
                            hpp, hf = h4 // 2, h4 % 2
                            pb, rb = pbs[hf]
                            S.op("tensor", MM(pb[:, hpp * 128:(hpp + 1) * 128], KT[hf * 64:(hf + 1) * 64, hpp, bi, :],
                                              QT[hf * 64:(hf + 1) * 64, hpp, qb0:qb0 + 128], True, True),
                                 reads=[rKT[bi]] + qr, writes=[rb])
                        xb = cnt["ex"] % NR
                        cnt["ex"] += 1
                        EX4 = EX[xb].rearrange("p (a f n) -> p a f n", a=2, f=2)
                        for hf in range(2):
                            pb, rb = pbs[hf]
                            S.op("scalar", ACT(EX4[:, :, hf, :], pb[:, 0:256].rearrange("p (a n) -> p a n", a=2), AF.Exp), reads=[rb], writes=[rEX[xb]])
                        kind = 0 if bk == 1 else (2 if n == 0 else 1)
                        S.op("vector", TT(PT[xb], EX[xb], ET[:, kind, :], ALU.mult), reads=[rEX[xb], rET], writes=[rPT[xb]])
                        for h4 in range(4):
                            S.op("tensor", MM(po[:, h4 * 65:(h4 + 1) * 65], PT[xb][:, h4 * 128:(h4 + 1) * 128], Vb[:, bi, h4, 0:65],
                                              bk == 0 and h4 == 0, bk == 1, skip=True),
                                 reads=[rPT[xb], rV[bi], rVones], writes=[ro])
                    ob = cnt["os"] % NR
                    cnt["os"] += 1
                    S.op("vector", COPY(OS[ob], po[:, 0:260]), reads=[ro], writes=[rOS[ob]])
                    S.dma("sync", DMA(osc[g][SL(q0, 128, d), hh, :], OS[ob]), rOS[ob], reads=[rOS[ob]], writes=[rOSC])

                for rho in range(d):
                    for n in range(-1, nb):
                        produce(rho, n)
                        if n >= 0:
                            attend(rho, n)
                po, ro = nextbank()
                for b in range(4):
                    cb_ = cnt["ck"] % 2
                    cnt["ck"] += 1
                    S.dma("sync", DMA(CK[cb_].rearrange("p (t n) -> p t n", t=2), cache[g][b, SL(0, 128, d), :, hh * 256:(hh + 1) * 256]), rCK[cb_], writes=[rCK[cb_]])
                    pb, rb = nextbank()
                    S.op("tensor", MM(pb[:, 0:256], SELQ[0:4, b * 128:(b + 1) * 128], SQKV[0:4, 0:256], True, True), reads=[rC, rSQKV], writes=[rb])
                    S.op("vector", TT(SPR[:, 0:256], CK[cb_][:, 0:256], pb[:, 0:256], ALU.mult), reads=[rCK[cb_], rb], writes=[rSPR])
                    sc = SMALL[:, 40:44]
                    S.op("vector", REDUCE(sc, SPR[:, 0:256].rearrange("p (h e) -> p h e", h=4)), reads=[rSPR], writes=[rSM])
                    S.op("vector", STT(sc, sc, 0.125, SBIAS[:, g * 8 + hh * 4:g * 8 + hh * 4 + 4], ALU.mult, ALU.add), reads=[rSM, rC], writes=[rSM])
                    SPV3 = SPV.rearrange("p (h e) -> p h e", h=4)
                    S.op("scalar", ACT(SPV3[:, :, 64], sc, AF.Exp), reads=[rSM], writes=[rSPV])
                    S.op("vector", TT(SPV3[:, :, 0:64], CK[cb_][:, 256:512].rearrange("p (h e) -> p h e", h=4),
                                      SPV3[:, :, 64:65].to_broadcast([128, 4, 64]), ALU.mult), reads=[rCK[cb_], rSPV], writes=[rSPV])
                    S.op("tensor", MM(po[0:4, 0:260], SELO[:, b * 4:(b + 1) * 4], SPV, b == 0, b == 3), reads=[rC, rSPV], writes=[ro])
                if sub == 5:
                    S.barrier()
                    S.emit()
                    return nc
                q_s, k_s, v_s = SQKV[0:4, 0:256], SQKV[0:4, 256:512], SQKV[0:4, 512:768]
                S.op("vector", TT(SPR[0:4, 0:256], q_s, k_s, ALU.mult), reads=[rSQKV], writes=[rSPR])
                sc = SMALL[0:4, 44:48]
                S.op("vector", REDUCE(sc, SPR[0:4, 0:256].rearrange("p (h e) -> p h e", h=4)), reads=[rSPR], writes=[rSM])
                SPVn = SPR[0:4, 0:260].rearrange("p (h e) -> p h e", h=4)
                S.op("scalar", ACT(SPVn[:, :, 64], sc, AF.Exp, scale=0.125), reads=[rSM], writes=[rSPR])
                S.op("vector", TT(SPVn[:, :, 0:64], v_s.rearrange("p (h e) -> p h e", h=4), SPVn[:, :, 64:65].to_broadcast([4, 4, 64]), ALU.mult),
                     reads=[rSQKV, rSPR], writes=[rSPR])
                sn = SNUM[0:4, hh * 260:(hh + 1) * 260]
                S.op("vector", TT(sn, sn, SPR[0:4, 0:260], ALU.add), reads=[rSPR, rSNUM], writes=[rSNUM])
                S.op("vector", TT(sn, sn, po[0:4, 0:260], ALU.add), reads=[ro, rSNUM], writes=[rSNUM])
        S.barrier()
        if upto == 3:
            if dbgfn is not None:
                dbgfn(locals())
                S.barrier()
            S.emit()
            return nc

        A.top = SCR0
        A.reserve_top = 4096 + 1792
        WOUT = arena_t[:, AW - 4096:AW].bitcast(BF16).rearrange("p (k n) -> p k n", k=8)
        rWOUT = Res("wout")
        hi0 = AW - 4096 - 1792

        def hv(off, words, k):
            return arena_t[:, hi0 + off:hi0 + off + words].bitcast(BF16).rearrange("p (k n) -> p k n", k=k)

        WGR = [hv(0, 512, 8), None]
        WGA = [hv(512, 512, 8), None]
        WRO = [hv(1024, 512, 8), None]
        WAO = [hv(1536, 256, 4), None]
        rW5 = [Res("w5_0"), Res("w5_1")]

        def load_w5(j):
            b = j % 2
            S.dma("gpsimd", DMA(WGR[b], win[56 + j].rearrange("p (k n) -> p k n", k=8)), rW5[b], writes=[rW5[b]])
            S.dma("gpsimd", DMA(WGA[b], win[64 + j].rearrange("p (k n) -> p k n", k=8)), rW5[b], writes=[rW5[b]])
            S.dma("gpsimd", DMA(WRO[b], wro[j].rearrange("p (k n) -> p k n", k=8)), rW5[b], writes=[rW5[b]])
            S.dma("gpsimd", DMA(WAO[b], wao[j].rearrange("p (k n) -> p k n", k=4)), rW5[b], writes=[rW5[b]])
        AZT = A.bf16(4 * FW).rearrange("p (k n) -> p k n", k=4)
        SCR1 = A.top
        SZA = A.f32(4 * FW).rearrange("p (k n) -> p k n", k=4)
        WZA = [A.bf16(1024).rearrange("p (k n) -> p k n", k=8) for _ in range(2)]
        NO = 3
        OT = [A.f32(3 * 520).rearrange("p (g n) -> p g n", g=3) for _ in range(NO)]
        MG = [A.f32(512) for _ in range(NO)]
        rSZA = [Res(f"sza{i}") for i in range(4)]
        rWZA = [Res("wza0"), Res("wza1")]
        rOT = [Res("ot%d" % i) for i in range(4)]
        rMG = [Res("mg%d" % i) for i in range(4)]
        rAZT = Res("azt")
        ttiles = [(2048 + t * 512, t * 512, 512) for t in range(4)] + [(4096, 2048, 4)]
        for c4 in range(4):
            wb = c4 % 2
            S.dma("gpsimd", DMA(WZA[wb], win[52 + c4].rearrange("p (k n) -> p k n", k=8)), rWZA[wb], writes=[rWZA[wb]])
            if c4 == 1:
                load_w5(0)
            for (xc, fc, nn) in ttiles:
                pb, rb = nextbank()
                for kc in range(8):
                    S.op("tensor", MM(pb[:, 0:nn], WZA[wb][:, kc, :], XNT[:, kc, xc:xc + nn], kc == 0, kc == 7), reads=[rWZA[wb], rX], writes=[rb])
                S.op("scalar", ACT(SZA[:, c4, fc:fc + nn], pb[:, 0:nn], AF.Silu), reads=[rb], writes=[rSZA[c4]])

        for kq in range(4):
            S.dma("gpsimd", DMA(WOUT[:, 2 * kq:2 * kq + 2, :], wout[:, kq * 2048:(kq + 1) * 2048].rearrange("p (k n) -> p k n", k=2)), rWOUT, writes=[rWOUT])

        def load_ot(t):
            b = t % NO
            for g in range(3):
                S.dma("sync", DMA(OT[b][:, g, :], osc[g][t * 128:(t + 1) * 128].rearrange("p h n -> p (h n)")), rOT[b], writes=[rOT[b]])

        for t0 in range(NO - 1):
            load_ot(t0)
        for t in range(17):
            b = t % NO
            if t < 16:
                if t + NO - 1 < 16:
                    load_ot(t + NO - 1)
                np_ = 128
                s1 = OT[b][:, 0, :]
                S.op("vector", TT(s1, s1, OT[b][:, 1, :], ALU.add), reads=[rOT[b]], writes=[rOT[b]])
                S.op("vector", TT(s1, s1, OT[b][:, 2, :], ALU.add), reads=[rOT[b]], writes=[rOT[b]])
                rsrc = rOT[b]
                fc = t * 128
            else:
                np_ = 4
                s1 = SNUM[0:4, :]
                rsrc = rSNUM
                fc = 2048
            s3 = s1[0:np_, :].rearrange("p (h e) -> p h e", h=8)
            rd = SMALL[0:np_, 48:56]
            S.op("vector", RECIP(rd, s3[:, :, 64]), reads=[rsrc], writes=[rSM])
            S.op("vector", TT(MG[b][0:np_, :].rearrange("p (h e) -> p h e", h=8), s3[:, :, 0:64], rd.unsqueeze(2).to_broadcast([np_, 8, 64]), ALU.mult),
                 reads=[rsrc, rSM], writes=[rMG[b]])
            pb, rb = nextbank()
            for c4 in range(4):
                S.op("tensor", TR(pb[:, c4 * np_:(c4 + 1) * np_], MG[b][0:np_, c4 * 128:(c4 + 1) * 128], IDENTF[0:np_, 0:np_]), reads=[rMG[b], rC], writes=[rb])
            S.op("vector", TT(AZT[:, :, fc:fc + np_], pb[:, 0:4 * np_].rearrange("p (k n) -> p k n", k=4), SZA[:, :, fc:fc + np_], ALU.mult),
                 reads=[rb] + rSZA, writes=[rAZT])
        S.barrier()
        if upto == 4:
            if dbgfn is not None:
                dbgfn(locals())
                S.barrier()
            S.emit()
            return nc

        A.top = SCR1
        MIXT = A.bf16(8 * FW).rearrange("p (k n) -> p k n", k=8)
        WGR[1] = A.bf16(1024).rearrange("p (k n) -> p k n", k=8)
        WGA[1] = A.bf16(1024).rearrange("p (k n) -> p k n", k=8)
        WRO[1] = A.bf16(1024).rearrange("p (k n) -> p k n", k=8)
        WAO[1] = A.bf16(512).rearrange("p (k n) -> p k n", k=4)
        SGR = [A.f32(512) for _ in range(2)]
        SGA = [A.f32(512) for _ in range(2)]
        rSGR = [Res("sgr0"), Res("sgr1")]
        rSGA = [Res("sga0"), Res("sga1")]
        rMIX = [Res(f"mix{j}") for j in range(8)]

        k5 = 0
        for j in range(8):
            wb = j % 2
            if j + 1 < 8:
                load_w5(j + 1)
            for (xc, fc, nn) in ttiles:
                sb = k5 % 2
                k5 += 1
                p1, r1 = nextbank()
                for kc in range(8):
                    S.op("tensor", MM(p1[:, 0:nn], WGR[wb][:, kc, :], XNT[:, kc, xc:xc + nn], kc == 0, kc == 7), reads=[rW5[wb], rX], writes=[r1])
                S.op("scalar", ACT(SGR[sb][:, 0:nn], p1[:, 0:nn], AF.Sigmoid), reads=[r1], writes=[rSGR[sb]])
                p2, r2 = nextbank()
                for kc in range(8):
                    S.op("tensor", MM(p2[:, 0:nn], WGA[wb][:, kc, :], XNT[:, kc, xc:xc + nn], kc == 0, kc == 7), reads=[rW5[wb], rX], writes=[r2])
                S.op("scalar", ACT(SGA[sb][:, 0:nn], p2[:, 0:nn], AF.Sigmoid), reads=[r2], writes=[rSGA[sb]])
                p3, r3 = nextbank()
                for kc in range(8):
                    S.op("tensor", MM(p3[:, 0:nn], WRO[wb][:, kc, :], HZT[:, kc, fc:fc + nn], kc == 0, kc == 7), reads=[rW5[wb]], writes=[r3])
                S.op("vector", TT(SGR[sb][:, 0:nn], SGR[sb][:, 0:nn], p3[:, 0:nn], ALU.mult), reads=[r3, rSGR[sb]], writes=[rSGR[sb]])
                p4, r4 = nextbank()
                for kc in range(4):
                    S.op("tensor", MM(p4[:, 0:nn], WAO[wb][:, kc, :], AZT[:, kc, fc:fc + nn], kc == 0, kc == 3), reads=[rW5[wb]], writes=[r4])
                S.op("vector", TT(SGA[sb][:, 0:nn], SGA[sb][:, 0:nn], p4[:, 0:nn], ALU.mult), reads=[r4, rSGA[sb]], writes=[rSGA[sb]])
                S.op("vector", TT(MIXT[:, j, fc:fc + nn], SGR[sb][:, 0:nn], SGA[sb][:, 0:nn], ALU.add), reads=[rSGR[sb], rSGA[sb]], writes=[rMIX[j]])
        S.barrier()
        if upto == 5:
            if dbgfn is not None:
                dbgfn(locals())
                S.barrier()
            S.emit()
            return nc

        A.top = 0
        N6 = 4
        XT6 = [A.f32(1024) for _ in range(N6)]
        YT = [A.f32(1024) for _ in range(N6)]
        SQ6 = A.f32(512)
        SM6 = A.f32(16)
        GPOST = A.f32(1024)
        rGP = Res("gpost")
        S.dma("sync", DMA(GPOST, gpost.partition_broadcast(128)), rGP, writes=[rGP])
        rXT6 = [Res("xt6_%d" % i) for i in range(4)]
        rYT = [Res("yt%d" % i) for i in range(4)]
        rSQ6 = Res("sq6")
        t6 = [(xm, t * 128, y, t * 128, t * 128, 128) for t in range(16)] + [(xsd, 0, ys, 0, 2048, 4)]

        def load_x6(i):
            src, r0, _, _, _, np_ = t6[i]
            b = i % N6
            S.dma("sync", DMA(XT6[b][0:np_, :], src[r0:r0 + np_, :]), rXT6[b], writes=[rXT6[b]])

        for i0 in range(N6 - 1):
            load_x6(i0)
        for i, (src, r0, dst, d0, fc, np_) in enumerate(t6):
            b = i % N6
            if i + N6 - 1 < len(t6):
                load_x6(i + N6 - 1)
            banks = []
            for hf in range(2):
                pb, rb = nextbank()
                for kc in range(8):
                    S.op("tensor", MM(pb[0:np_, :], MIXT[:, kc, fc:fc + np_], WOUT[:, kc, hf * 512:(hf + 1) * 512], kc == 0, kc == 7), reads=[rWOUT], writes=[rb])
                banks.append((pb, rb))
            rss = Res("ss6")
            ssa = SM6[0:np_, 4 * b:4 * b + 1]
            ssb = SM6[0:np_, 4 * b + 1:4 * b + 2]
            rs = SM6[0:np_, 4 * b + 2:4 * b + 3]
            S.op("scalar", ACT(SQ6[0:np_, :], banks[0][0][0:np_, :], AF.Square, accum_out=ssa), reads=[banks[0][1]], writes=[rSQ6, rss])
            S.op("scalar", ACT(SQ6[0:np_, :], banks[1][0][0:np_, :], AF.Square, accum_out=ssb), reads=[banks[1][1], rss], writes=[rSQ6, rss])
            S.op("vector", TT(rs, ssa, ssb, ALU.add), reads=[rss], writes=[rss])
            S.op("scalar", ACT(rs, rs, AF.Sqrt, scale=1.0 / 1024, bias=EPS), reads=[rss], writes=[rss])
            S.op("vector", RECIP(rs, rs), reads=[rss], writes=[rss])
            for hf in range(2):
                sl = slice(hf * 512, (hf + 1) * 512)
                S.op("vector", STT(YT[b][0:np_, sl], banks[hf][0][0:np_, :], rs, GPOST[0:np_, sl], ALU.mult, ALU.mult),
                     reads=[banks[hf][1], rss, rGP], writes=[rYT[b]])
            S.op("vector", TT(YT[b][0:np_, :], YT[b][0:np_, :], XT6[b][0:np_, :], ALU.add), reads=[rYT[b], rXT6[b]], writes=[rYT[b]])
            S.dma("sync", DMA(dst[d0:d0 + np_, :], YT[b][0:np_, :]), rYT[b], reads=[rYT[b]])
        S.barrier()
        assert S.nsem <= 100, S.nsem
        S.emit()
    return nc


def _alibi_slopes():
    n = 24
    return (2.0 ** (-8.0 * np.arange(1, n + 1, dtype=np.float64) / n)).reshape(3, 8)


def _tables(half):
    sl = _alibi_slopes()
    i = np.arange(128)
    et = np.zeros((3, 3, 128, 8, 128), np.float32)
    for g in range(3):
        d = DIL[g]
        for h in range(8):
            s = sl[g, h]
            steps_cur = (i[None, :] - i[:, None]).astype(np.float64)
            cur = np.where(steps_cur >= 0, np.exp(-s * d * steps_cur), 0.0)
            steps_prev = 128 + steps_cur
            prev = np.where(steps_prev <= 128, np.exp(-s * d * steps_prev), 0.0)
            et[g, 0, :, h, :] = cur
            et[g, 1, :, h, :] = prev
            et[g, 2, :, h, :] = prev * float(half)
    sb = np.zeros((128, 3, 8), np.float32)
    m = np.arange(128)
    for g in range(3):
        d = DIL[g]
        for h in range(8):
            sb[:, g, h] = -sl[g, h] * d * (128 - m)
    return et.reshape(3, 3, 128, 1024), sb.reshape(128, 24)


_NC_CACHE = {}


def prep_inputs(x_prompt, x_sample, state_conv, state_h, cache_kv_w128, cache_kv_w512, cache_kv_w2048,
                norm_pre, norm_post, w_in, conv_w, conv_b, lru_w_a, lru_b_a, lru_w_x, lru_b_x, lru_lambda,
                w_rnn_out, w_attn_out, w_out):
    f = lambda a: np.ascontiguousarray(np.asarray(a, dtype=np.float32))
    x_prompt, x_sample, state_conv, state_h = f(x_prompt), f(x_sample), f(state_conv), f(state_h)
    caches = [f(cache_kv_w128), f(cache_kv_w512), f(cache_kv_w2048)]
    w_in = f(w_in)[0]
    def tile_w(w, nchunk, kcs):
        return np.ascontiguousarray(w.reshape(kcs, 128, nchunk, 128).transpose(2, 1, 0, 3).reshape(nchunk, 128, kcs * 128))
    win = tile_w(w_in, 72, 8)
    w4 = w_in.reshape(8, 128, 9216)
    parts = []
    for g in range(3):
        for hh in range(2):
            cs = g * 512 + hh * 256
            parts.append(np.concatenate([w4[:, :, 2048 + cs:2048 + cs + 256], w4[:, :, 3584 + cs:3584 + cs + 256],
                                         w4[:, :, 5120 + cs:5120 + cs + 256]], axis=2).transpose(1, 0, 2).reshape(128, 6144))
    wqkv = np.ascontiguousarray(np.stack(parts))
    wro = tile_w(f(w_rnn_out)[0], 8, 8)
    wao = tile_w(f(w_attn_out)[0], 8, 4)
    wout = np.ascontiguousarray(f(w_out)[0].reshape(8, 128, 1024).transpose(1, 0, 2).reshape(128, 8192))
    wga = np.stack([np.ascontiguousarray(w.reshape(8, 2, 64, 64).transpose(1, 2, 0, 3).reshape(2, 64, 512))
                    for w in (f(lru_w_a)[0], f(lru_w_x)[0])])
    prm = np.concatenate([f(conv_w)[0].reshape(32, 128), f(conv_b)[0].reshape(8, 128), f(lru_b_a)[0].reshape(8, 128),
                          f(lru_b_x)[0].reshape(8, 128), f(lru_lambda)[0].reshape(8, 128)], axis=0)
    ident = np.eye(128, dtype=np.float32)
    selq = np.zeros((4, 4, 128), np.float32)
    selo = np.zeros((128, 4, 4), np.float32)
    for b in range(4):
        selq[b, b, :] = 1.0
        selo[:, b, b] = 1.0
    selq = selq.reshape(4, 512)
    selo = selo.reshape(128, 16)
    tabs = [_tables(0), _tables(1)]
    in_maps = []
    for c in range(8):
        b, half = c // 2, c % 2
        xm = x_prompt[b, half * 2048:(half + 1) * 2048]
        xp = x_prompt[b, 0:2048] if half == 1 else np.zeros((2048, 1024), np.float32)
        sl = slice(4 * c, 4 * c + 4)
        sst = np.concatenate([state_conv[0, sl].reshape(12, 1024), state_h[0, sl]], axis=0)
        in_maps.append({
            "xm": np.ascontiguousarray(xm), "xp": np.ascontiguousarray(xp),
            "pv": np.full((128, 1), float(half), np.float32),
            "xs": np.ascontiguousarray(x_sample[sl, 0, :]), "sst": np.ascontiguousarray(sst),
            "sconv": np.ascontiguousarray(state_conv[0, sl]),
            "c0": np.ascontiguousarray(caches[0][0, sl].reshape(4, 128, 2, 512)),
            "c1": np.ascontiguousarray(caches[1][0, sl].reshape(4, 512, 2, 512)),
            "c2": np.ascontiguousarray(caches[2][0, sl].reshape(4, 2048, 2, 512)),
            "win": win, "wqkv": wqkv, "wro": wro, "wao": wao, "wout": wout, "wga": wga, "prm": prm, "cbrow": f(conv_b).reshape(1, 1024),
            "gpre": f(norm_pre).reshape(1, 1024), "gpost": f(norm_post).reshape(1, 1024),
            "etab": tabs[half][0], "sbias": tabs[half][1], "ident": ident, "selq": selq, "selo": selo,
        })
    return in_maps


def kernel(**inputs):
    in_maps = prep_inputs(**inputs)
    if "nc" not in _NC_CACHE:
        _NC_CACHE["nc"] = build_program()
    nc = _NC_CACHE["nc"]
    res = run_bass_kernel_spmd(nc, in_maps, core_ids=list(range(8)))
    R = res.results
    yp = np.zeros((4, 4096, 1024), np.float32)
    ys = np.zeros((32, 1, 1024), np.float32)
    conv_p = np.zeros((1, 4, 3, 1024), np.float32)
    conv_s = np.zeros((1, 32, 3, 1024), np.float32)
    h_p = np.zeros((1, 4, 1024), np.float32)
    h_s = np.zeros((1, 32, 1024), np.float32)
    kvp = [np.zeros((1, 4, 128 * DIL[g], 2, 8, 64), np.float32) for g in range(3)]
    kvsm = [np.zeros((1, 32, 1, 2, 8, 64), np.float32) for g in range(3)]
    for c in range(8):
        b, half = c // 2, c % 2
        r = R[c]
        yp[b, half * 2048:(half + 1) * 2048] = r["y"]
        sl = slice(4 * c, 4 * c + 4)
        ys[sl, 0] = r["ys"]
        conv_s[0, sl, 0:2] = r["convs_s"]
        conv_s[0, sl, 2] = r["convs_u"].reshape(128, 8, 4).transpose(2, 1, 0).reshape(4, 1024)
        h_s[0, sl] = r["hs"].reshape(128, 8, 4).transpose(2, 1, 0).reshape(4, 1024)
        for g in range(3):
            kvsm[g][0, sl, 0] = r["kvs"][g].reshape(4, 2, 8, 64)
        if half == 1:
            conv_p[0, b] = r["convp"].reshape(128, 8, 3).transpose(2, 1, 0).reshape(3, 1024)
            h_p[0, b] = r["hp"].reshape(128, 8).transpose(1, 0).reshape(1024)
            for g in range(3):
                kvp[g][0, b] = r[f"kvo{g}"].reshape(128 * DIL[g], 2, 8, 64)
    return (yp, ys, conv_p, conv_s, h_p, h_s, kvp[0], kvsm[0], kvp[1], kvsm[1], kvp[2], kvsm[2])
```

```python
from contextlib import ExitStack
import numpy as np
import concourse.bass as bass
import concourse.mybir as mybir
from concourse.bass_utils import run_bass_kernel_spmd

F32 = mybir.dt.float32
BF16 = mybir.dt.bfloat16
AF = mybir.ActivationFunctionType
ALU = mybir.AluOpType
AX = mybir.AxisListType

NT = 2048
XW = 4100
FW = 2052
DIL = (1, 4, 16)
EPS = 1e-6


class Res:
    __slots__ = ("name", "lastw", "readers", "sem", "semcnt", "excl")

    def __init__(self, name):
        self.name = name
        self.lastw = None
        self.readers = []
        self.sem = None
        self.semcnt = 0
        self.excl = name.startswith("pb") and name[2:].isdigit()


class Op:
    __slots__ = ("idx", "eng", "fn", "owner", "preds", "dur", "tok", "fin")

    def __init__(self, idx, eng, fn, owner, preds, dur):
        self.idx = idx
        self.eng = eng
        self.fn = fn
        self.owner = owner
        self.preds = preds
        self.dur = dur
        self.tok = None
        self.fin = 0.0


class Sched:
    ENGS = ("sync", "scalar", "gpsimd", "vector", "tensor")
    WINDOW = 160

    def __init__(self, nc, stack):
        self.nc = nc
        self.stack = stack
        self.out = {e: [] for e in self.ENGS}
        self.esem = {}
        self.ecnt = {e: 0 for e in self.ENGS}
        self.known = {e: {} for e in self.ENGS}
        self.nsem = 0
        self.owners = []
        self.seg = []
        self.nops = 0
        for e in ("scalar", "gpsimd", "vector", "tensor"):
            self.esem[e] = self.newsem("e_" + e)

    def newsem(self, name):
        self.nsem += 1
        return self.stack.enter_context(self.nc.semaphore(name))

    def _record(self, eng, fn, owner, reads, writes):
        ex = [r for r in reads if r.excl]
        if ex:
            writes = list(writes) + [r for r in ex if r not in writes]
            reads = [r for r in reads if not r.excl]
        preds = set()
        for r in reads:
            if r.lastw is not None:
                preds.add(r.lastw)
        for w in writes:
            if w.lastw is not None:
                preds.add(w.lastw)
            preds.update(w.readers)
        op = Op(self.nops, eng, fn, owner, preds, getattr(fn, "dur", 0.2))
        self.nops += 1
        for w in writes:
            w.lastw = op
            w.readers = []
        for r in reads:
            if r not in writes:
                r.readers.append(op)
        self.seg.append(op)
        return op

    def op(self, eng, fn, reads=(), writes=()):
        self._record(eng, fn, None, reads, writes)

    def dma(self, eng, fn, owner, reads=(), writes=()):
        if owner.sem is None:
            owner.sem = self.newsem("d_" + owner.name)
            self.owners.append(owner)
        self._record(eng, fn, owner, reads, writes)

    def _schedule_segment(self):
        seg = self.seg
        self.seg = []
        if not seg:
            return
        segset = set(id(o) for o in seg)
        pend = {e: [] for e in self.ENGS}
        for o in seg:
            o.preds = [p for p in o.preds if id(p) in segset]
            pend[o.eng].append(o)
        pos = {e: 0 for e in self.ENGS}
        free = {e: 0.0 for e in self.ENGS}
        done = set()
        order = {e: [] for e in self.ENGS}
        remaining = len(seg)
        W = self.WINDOW
        while remaining:
            best = None
            for e in self.ENGS:
                lst = pend[e]
                if not lst:
                    continue
                fe = free[e]
                cnt = 0
                for o in lst:
                    cnt += 1
                    if cnt > W:
                        break
                    rdy = 0.0
                    ok = True
                    for p in o.preds:
                        if id(p) not in done:
                            ok = False
                            break
                        if p.fin > rdy:
                            rdy = p.fin
                    if not ok:
                        continue
                    st = fe if fe > rdy else rdy
                    if best is None or st < best[0] - 1e-9:
                        best = (st, e, o)
                    if rdy <= fe:
                        break
            assert best is not None, "scheduler deadlock"
            st, e, o = best
            pend[e].remove(o)
            if e in ("sync", "gpsimd") and o.owner is not None:
                free[e] = st + 0.06
                o.fin = st + o.dur
            else:
                free[e] = st + o.dur
                o.fin = st + o.dur
            done.add(id(o))
            order[e].append(o)
            remaining -= 1
        allorder = sorted(seg, key=lambda o: (o.fin - o.dur if o.owner is None else o.fin - o.dur, o.idx))
        for e in self.ENGS:
            for o in order[e]:
                if o.owner is not None:
                    o.owner.semcnt += 16
                    o.tok = (o.owner.sem, o.owner.semcnt)
                else:
                    self.ecnt[e] += 1
                    o.tok = (self.esem[e], self.ecnt[e])
        for e in self.ENGS:
            kn = self.known[e]
            for o in order[e]:
                waits = []
                for p in o.preds:
                    if e == "tensor" and p.eng == "tensor" and p.owner is None:
                        continue
                    s_, v_ = p.tok
                    if kn.get(id(s_), 0) < v_:
                        kn[id(s_)] = v_
                        waits.append((s_, v_))
                self.out[e].append((waits, o.fn, o.tok[0], 16 if o.owner is not None else 1))

    def barrier(self):
        self._schedule_segment()
        allsems = [(self.esem[e], self.ecnt[e]) for e in self.esem] + [(o.sem, o.semcnt) for o in self.owners]
        for eng in self.ENGS:
            kn = self.known[eng]
            waits = []
            for s, v in allsems:
                if v > 0 and kn.get(id(s), 0) < v:
                    kn[id(s)] = v
                    waits.append((s, v))
            self.out[eng].append((waits, None, None, 0))

    def emit(self):
        self._schedule_segment()
        nc = self.nc
        with nc.Block() as block:
            for ename in self.ENGS:
                ops = self.out[ename]
                if not ops:
                    continue

                def body(e, ops=ops):
                    for waits, fn, sem, inc in ops:
                        for s, v in waits:
                            e.wait_ge(s, v)
                        if fn is not None:
                            fn(e).then_inc(sem, inc)

                getattr(block, ename)(body)


def _fsz(ap):
    n = 1
    for d in ap.shape[1:]:
        n *= d
    return n


def _d(f, dur):
    f.dur = dur
    return f


def COPY(out, in_):
    return _d(lambda e: e.tensor_copy(out=out, in_=in_), (_fsz(in_) + 60) / 960.0)


def ACT(out, in_, func, **kw):
    return _d(lambda e: e.activation(out=out, in_=in_, func=func, **kw), (_fsz(in_) + 250) / 1200.0)


def MM(out, lhsT, rhs, start, stop, skip=False):
    n = max(_fsz(rhs), 64) * (4 if rhs.dtype == F32 else 1)
    if skip:
        return _d(lambda e: e.matmul(out, lhsT=lhsT, rhs=rhs, start=start, stop=stop, skip_group_check=True), n / 2400.0 + 0.03)
    return _d(lambda e: e.matmul(out, lhsT=lhsT, rhs=rhs, start=start, stop=stop), n / 2400.0 + 0.03)


def TR(out, in_, ident):
    return _d(lambda e: e.transpose(out=out, in_=in_, identity=ident), 0.12 * (4 if in_.dtype == F32 else 1))


def TT(out, a, b, op):
    return _d(lambda e: e.tensor_tensor(out=out, in0=a, in1=b, op=op), (_fsz(a) + 60) / 960.0)


def STT(out, in0, scalar, in1, op0, op1):
    return _d(lambda e: e.scalar_tensor_tensor(out=out, in0=in0, scalar=scalar, in1=in1, op0=op0, op1=op1), (_fsz(in0) + 60) / 960.0)


def TS(out, in0, s1, s2, op0, op1=None):
    du = (_fsz(in0) + 60) / 960.0
    if op1 is None:
        return _d(lambda e: e.tensor_scalar(out=out, in0=in0, scalar1=s1, scalar2=None, op0=op0), du)
    return _d(lambda e: e.tensor_scalar(out=out, in0=in0, scalar1=s1, scalar2=s2, op0=op0, op1=op1), du)


def DMA(out, in_):
    nb = 4 * in_.shape[0] * _fsz(in_)
    return _d(lambda e: e.dma_start(out=out, in_=in_), 2.0 + nb / 150000.0)


def MEMSET(ap, v):
    return _d(lambda e: e.memset(ap, v), (_fsz(ap) + 60) / 960.0)


def RECIP(out, in_):
    return _d(lambda e: e.reciprocal(out=out, in_=in_), (_fsz(in_) + 60) / 960.0)


def SCAN(out, d0, d1, init):
    return _d(lambda e: e.tensor_tensor_scan(out=out, data0=d0, data1=d1, initial=init, op0=ALU.mult, op1=ALU.add), (2 * _fsz(d0) + 60) / 960.0)


def REDUCE(out, in_):
    return _d(lambda e: e.tensor_reduce(out=out, in_=in_, axis=AX.X, op=ALU.add), (_fsz(in_) + 60) / 960.0)


class Arena:
    def __init__(self, t, size):
        self.t = t
        self.size = size
        self.top = 0
        self.reserve_top = 0

    def f32(self, n):
        off = self.top
        self.top += n
        assert self.top <= self.size - self.reserve_top, (self.top, self.size, self.reserve_top)
        return self.t[:, off:off + n]

    def bf16(self, n):
        words = (n + 1) // 2
        return self.f32(words).bitcast(BF16)[:, 0:n]


def SL(start, n, step):
    return slice(start, start + (n - 1) * step + 1, step)


def block_cols(g, rho, n):
    d = DIL[g]
    return 2048 + rho + d * 128 * n, d


def build_program(upto=99, dbgfn=None, sub=99):
    nc = bass.Bass("TRN2", target_bir_lowering=False)

    def din(name, shape):
        return nc.dram_tensor(name, shape, F32, kind="ExternalInput").ap()

    def dout(name, shape):
        return nc.dram_tensor(name, shape, F32, kind="ExternalOutput").ap()

    xm = din("xm", [2048, 1024])
    xp = din("xp", [2048, 1024])
    pvd = din("pv", [128, 1])
    xsd = din("xs", [4, 1024])
    sst = din("sst", [16, 1024])
    sconv = din("sconv", [4, 3, 1024])
    cache = [din("c0", [4, 128, 2, 512]), din("c1", [4, 512, 2, 512]), din("c2", [4, 2048, 2, 512])]
    win = din("win", [72, 128, 1024])
    wqkv = din("wqkv", [6, 128, 6144])
    wro = din("wro", [8, 128, 1024])
    wao = din("wao", [8, 128, 512])
    wout = din("wout", [128, 8192])
    wga = din("wga", [2, 2, 64, 512])
    prm = din("prm", [64, 128])
    cbrow = din("cbrow", [1, 1024])
    gpre = din("gpre", [1, 1024])
    gpost = din("gpost", [1, 1024])
    etab = din("etab", [3, 3, 128, 1024])
    sbias = din("sbias", [128, 24])
    identin = din("ident", [128, 128])
    selq = din("selq", [4, 512])
    selo = din("selo", [128, 16])

    y = dout("y", [2048, 1024])
    convp = dout("convp", [128, 24])
    hp = dout("hp", [128, 8])
    kvo = [dout("kvo0", [128, 2, 512]), dout("kvo1", [512, 2, 512]), dout("kvo2", [2048, 2, 512])]
    ys = dout("ys", [4, 1024])
    convs_u = dout("convs_u", [128, 32])
    convs_s = dout("convs_s", [4, 2, 1024])
    hs = dout("hs", [128, 32])
    kvs = dout("kvs", [3, 4, 2, 512])
    dbg = dout("dbg", [128, 8192]) if upto != 99 else None
    osc = [nc.dram_tensor(f"osc{g}", [2048, 2, 260], F32).ap() for g in range(3)]

    with ExitStack() as st:
        S = Sched(nc, st)
        AW = 53200
        arena_t = st.enter_context(nc.sbuf_tensor("arena", [128, AW], F32))
        A = Arena(arena_t, AW)
        ps = [st.enter_context(nc.psum_tensor(f"ps{i}", [128, 512], F32)) for i in range(8)]
        PB = [Res(f"pb{i}") for i in range(8)]
        bank_i = [0]

        def nextbank():
            i = bank_i[0]
            bank_i[0] = (i + 1) % 8
            return ps[i], PB[i]

        XNT = A.bf16(8 * XW).rearrange("p (k n) -> p k n", k=8)
        HZT = A.bf16(8 * FW).rearrange("p (k n) -> p k n", k=8)
        IDENTF = A.f32(128)
        IDENT = A.bf16(128)
        PRM = A.f32(64)
        CLT = A.f32(16)
        PVT = A.f32(1)
        HPV = A.f32(1)
        HB = A.f32(16)
        WG = A.bf16(8 * 2 * 128).rearrange("p (c g o) -> p c g o", c=8, g=2)
        HOUT = A.f32(8)
        CONVP = A.f32(24).rearrange("p (c t) -> p c t", c=8)
        SNUM = A.f32(520)
        SELQ = A.f32(512)
        SELO = A.f32(16)
        SBIAS = A.f32(24)
        SMALL = A.f32(64)
        SSTT = A.f32(128)
        SUC = A.f32(32)
        SZS = A.f32(32)
        SH = A.f32(32)
        rXNT = [Res(f"xnt{i}") for i in range(33)]
        rHZT = [Res(f"hzt{c}") for c in range(8)]
        rHZTs = Res("hzts")
        rC = Res("consts")
        rSNUM = Res("snum")
        rSM = Res("small")
        rSST = Res("sstt")
        rSUC = Res("suc")
        rSZS = Res("szs")
        rSH = Res("sh")
        rOUTS = Res("outs_small")
        SCR0 = A.top

        A.top = SCR0
        PRMRAW = A.f32(128)
        SSRAW = A.f32(1024)
        GPRE = A.f32(1024)
        rT = Res("p0tmp")
        rG = Res("gpre")
        rWGL = Res("wgload")
        NX = 3
        XT = [A.f32(1024) for _ in range(NX)]
        XB = [A.bf16(1024) for _ in range(NX)]
        rXT = [Res("xt%d" % i) for i in range(NX)]
        rXB = [Res("xb%d" % i) for i in range(NX)]
        tiles = [(xp, i, i * 128, 128) for i in range(16)] + [(xm, i, 2048 + i * 128, 128) for i in range(16)] + [(xsd, 0, 4096, 4)]

        def load_x(idx):
            src, i, col, np_ = tiles[idx]
            b = idx % NX
            S.dma("sync", DMA(XT[b][0:np_, :], src[i * 128:i * 128 + np_, :]), rXT[b], writes=[rXT[b]])

        load_x(0)
        S.dma("sync", DMA(IDENTF, identin), rC, writes=[rC])
        S.dma("sync", DMA(GPRE, gpre.partition_broadcast(128)), rG, writes=[rG])
        load_x(1)
        S.dma("sync", DMA(PRMRAW[0:64, :], prm), rT, writes=[rT])
        S.dma("sync", DMA(SSRAW[0:16, :], sst), rT, writes=[rT])
        S.dma("sync", DMA(PVT, pvd), rC, writes=[rC])
        rC2 = Res("consts_late")
        S.dma("sync", DMA(SELQ[0:4, :], selq), rC2, writes=[Res("c2a")])
        S.dma("sync", DMA(SELO, selo), rC2, writes=[Res("c2b")])
        S.dma("sync", DMA(SBIAS, sbias), rC2, writes=[Res("c2c")])
        S.op("vector", COPY(IDENT, IDENTF), reads=[rC], writes=[rC])
        S.op("vector", MEMSET(WG.rearrange("p c g o -> p (c g o)"), 0.0), writes=[rC])
        S.op("vector", MEMSET(SNUM[0:4, :], 0.0), writes=[rSNUM])
        for g in range(2):
            for hf in range(2):
                S.dma("gpsimd", DMA(WG[hf * 64:(hf + 1) * 64, :, g, hf * 64:(hf + 1) * 64],
                                    wga[g, hf].rearrange("i (c o) -> i c o", c=8)), rWGL, reads=[rC], writes=[rC])
        pb, rb = nextbank()
        S.op("tensor", TR(pb[:, 0:64], PRMRAW[0:64, :], IDENTF[0:64, 0:64]), reads=[rT, rC], writes=[rb])
        S.op("vector", COPY(PRM, pb[:, 0:64]), reads=[rb], writes=[rC])
        pb, rb = nextbank()
        for c in range(8):
            S.op("tensor", TR(pb[:, c * 16:(c + 1) * 16], SSRAW[0:16, c * 128:(c + 1) * 128], IDENTF[0:16, 0:16]),
                 reads=[rT, rC], writes=[rb])
        S.op("vector", COPY(SSTT, pb[:, 0:128]), reads=[rb], writes=[rSST])
        SSTT3 = SSTT.rearrange("p (c r) -> p c r", c=8)
        T1 = SMALL[:, 0:8]
        T2 = SMALL[:, 8:16]
        T3 = SMALL[:, 16:24]
        T4 = SMALL[:, 24:32]
        S.op("scalar", ACT(T1, PRM[:, 56:64], AF.Exp, scale=-1.0), reads=[rC], writes=[rSM])
        S.op("vector", TS(T2, T1, 2.0, None, ALU.add), reads=[rSM], writes=[rSM])
        S.op("vector", RECIP(T2, T2), reads=[rSM], writes=[rSM])
        S.op("vector", TT(T3, T1, T2, ALU.mult), reads=[rSM], writes=[rSM])
        S.op("vector", TT(T4, T3, T3, ALU.mult), reads=[rSM], writes=[rSM])
        S.op("vector", TS(T2, T4, 1.0 / 9, 1.0 / 7, ALU.mult, ALU.add), reads=[rSM], writes=[rSM])
        for cst in (1.0 / 5, 1.0 / 3, 1.0):
            S.op("vector", TT(T2, T2, T4, ALU.mult), reads=[rSM], writes=[rSM])
            S.op("vector", TS(T2, T2, cst, None, ALU.add), reads=[rSM], writes=[rSM])
        S.op("vector", TT(T2, T2, T3, ALU.mult), reads=[rSM], writes=[rSM])
        S.op("vector", TS(CLT[:, 0:8], T2, -8.0, None, ALU.mult), reads=[rSM], writes=[rC])
        S.op("vector", TS(CLT[:, 8:16], T2, -16.0, None, ALU.mult), reads=[rSM], writes=[rC])
        S.op("vector", TS(HB, PRM[:, 40:56], 0.5, None, ALU.mult), reads=[rC], writes=[rC])
        S.op("vector", TS(HPV, PVT, 0.5, None, ALU.mult), reads=[rC], writes=[rC])
        S.dma("sync", DMA(convs_s, sconv[:, 1:3, :]), rOUTS, writes=[rOUTS])

        for idx, (src, i, col, np_) in enumerate(tiles):
            b = idx % NX
            if idx + 2 < len(tiles):
                load_x(idx + 2)
            ss = SMALL[0:np_, 32 + b:33 + b]
            rs = SMALL[0:np_, 36 + b:37 + b]
            rss = Res("ss")
            S.op("scalar", ACT(XB[b][0:np_, :], XT[b][0:np_, :], AF.Square, accum_out=ss), reads=[rXT[b]], writes=[rXB[b], rss])
            S.op("scalar", ACT(rs, ss, AF.Sqrt, scale=1.0 / 1024, bias=EPS), reads=[rss], writes=[rss])
            S.op("vector", RECIP(rs, rs), reads=[rss], writes=[rss])
            S.op("vector", STT(XB[b][0:np_, :], XT[b][0:np_, :], rs, GPRE[0:np_, :], ALU.mult, ALU.mult),
                 reads=[rXT[b], rss, rG], writes=[rXB[b]])
            pb, rb = nextbank()
            pbb = pb[:, :].bitcast(BF16)
            for kc in range(8):
                S.op("tensor", TR(pbb[:, kc * np_:(kc + 1) * np_], XB[b][0:np_, kc * 128:(kc + 1) * 128], IDENT[0:np_, 0:np_]),
                     reads=[rXB[b], rC], writes=[rb])
            S.op("vector", COPY(XNT[:, :, col:col + np_], pbb[:, 0:8 * np_].rearrange("p (k n) -> p k n", k=8)),
                 reads=[rb], writes=[rXNT[idx]])
        def xr(col, n):
            return [rXNT[i] for i in range(col // 128, (col + n - 1) // 128 + 1)]

        NS = 1024
        WU = [A.bf16(1024).rearrange("p (k n) -> p k n", k=8) for _ in range(2)]
        WZ = [A.bf16(1024).rearrange("p (k n) -> p k n", k=8) for _ in range(2)]
        rWU = [Res("wu0"), Res("wu1")]
        rWZ = [Res("wz0"), Res("wz1")]
        NSET = 3
        Ub = [A.bf16(NS + 4) for _ in range(NSET)]
        DG = [A.bf16(512).rearrange("p (t n) -> p t n", t=4) for _ in range(2)]
        CBROW = A.bf16(1024)
        ONES = A.bf16(512)
        INITS = A.f32(8)
        rDG = [Res("dg0"), Res("dg1")]
        rCB = Res("cbrow")
        S.dma("gpsimd", DMA(CBROW[0:1, :], cbrow), rCB, writes=[rCB])
        S.op("vector", MEMSET(ONES[0:1, :], 1.0), writes=[rCB])
        UC = [A.f32(NS) for _ in range(NSET)]
        Rb = [A.f32(NS) for _ in range(NSET)]
        Ib = [A.f32(NS) for _ in range(NSET)]
        Ab = [A.f32(NS) for _ in range(NSET)]
        UCB = [A.bf16(NS) for _ in range(NSET)]
        rU = [Res("u%d" % i) for i in range(NSET)]
        rUC = [Res("uc%d" % i) for i in range(NSET)]
        rR = [Res("r%d" % i) for i in range(NSET)]
        rI = [Res("i%d" % i) for i in range(NSET)]
        rA = [Res("a%d" % i) for i in range(NSET)]
        rUCB = [Res("ucb%d" % i) for i in range(NSET)]

        def load_w2(c):
            b = c % 2
            S.dma("gpsimd", DMA(WU[b], win[c].rearrange("p (k n) -> p k n", k=8)), rWU[b], writes=[rWU[b]])
            S.dma("gpsimd", DMA(WZ[b], win[8 + c].rearrange("p (k n) -> p k n", k=8)), rWZ[b], writes=[rWZ[b]])

        load_w2(0)

        def prm_of(c):
            return dict(w0=PRM[:, 0 + c:1 + c], w1=PRM[:, 8 + c:9 + c], w2=PRM[:, 16 + c:17 + c], w3=PRM[:, 24 + c:25 + c],
                        cb=PRM[:, 32 + c:33 + c], ba=HB[:, c:c + 1], bx=HB[:, 8 + c:9 + c], clh=CLT[:, c:c + 1], cl=CLT[:, 8 + c:9 + c])

        def stageA(it):
            c, s = it // 4, it % 4
            P = prm_of(c)
            wb = c % 2
            tb = it % NSET
            pbuf = (it - 1) % NSET
            col0 = s * NS
            U, rUU = Ub[tb], rU[tb]
            dg, rdg = DG[c % 2], rDG[c % 2]
            if s == 0:
                for tap, wt in enumerate((P["w0"], P["w1"], P["w2"], P["w3"])):
                    S.op("vector", TS(dg[:, tap, :], IDENTF, wt, None, ALU.mult), reads=[rC], writes=[rdg])
                S.op("vector", MEMSET(U[:, 0:3], 0.0), writes=[rUU])
            else:
                S.op("vector", COPY(U[:, 0:3], Ub[pbuf][:, NS:NS + 3]), reads=[rU[pbuf]], writes=[rUU])
            for tt in range(2):
                pb, rb = nextbank()
                for kc in range(8):
                    S.op("tensor", MM(pb[:, :], WU[wb][:, kc, :], XNT[:, kc, col0 + tt * 512:col0 + (tt + 1) * 512], kc == 0, kc == 7),
                         reads=[rWU[wb]] + xr(col0 + tt * 512, 512), writes=[rb])
                S.op("vector", COPY(U[:, 3 + tt * 512:3 + (tt + 1) * 512], pb[:, :]), reads=[rb], writes=[rUU])
                if s == 3 and tt == 1:
                    S.op("vector", COPY(CONVP[:, c, :], pb[:, 509:512]), reads=[rb], writes=[rOUTS])

        def stageA2(it):
            c, s = it // 4, it % 4
            P = prm_of(c)
            tb = it % NSET
            U, rUU = Ub[tb], rU[tb]
            dg, rdg = DG[c % 2], rDG[c % 2]
            cbanks = []
            for tt in range(2):
                pb, rb = nextbank()
                for tap in range(4):
                    S.op("tensor", MM(pb[:, :], dg[:, tap, :], U[:, tap + tt * 512:tap + (tt + 1) * 512], tap == 0, False),
                         reads=[rdg, rUU], writes=[rb])
                S.op("tensor", MM(pb[:, :], CBROW[0:1, c * 128:(c + 1) * 128], ONES[0:1, :], False, True), reads=[rCB], writes=[rb])
                sl = slice(tt * 512, (tt + 1) * 512)
                if tt == 0:
                    S.op("vector", COPY(UCB[tb][:, sl], pb[:, :]), reads=[rb], writes=[rUCB[tb]])
                else:
                    S.op("scalar", ACT(UCB[tb][:, sl], pb[:, :], AF.Copy), reads=[rb], writes=[rUCB[tb]])
                cbanks.append((pb, rb))
            for tt in range(2):
                sl = slice(tt * 512, (tt + 1) * 512)
                pb, rb = nextbank()
                S.op("tensor", MM(pb[:, :], WG[:, c, 0, :], UCB[tb][:, sl], True, True), reads=[rC, rUCB[tb]], writes=[rb])
                S.op("scalar", ACT(Rb[tb][:, sl], pb[:, :], AF.Tanh, bias=P["ba"], scale=0.5), reads=[rb, rC], writes=[rR[tb]])
                pb, rb = nextbank()
                S.op("tensor", MM(pb[:, :], WG[:, c, 1, :], UCB[tb][:, sl], True, True), reads=[rC, rUCB[tb]], writes=[rb])
                S.op("scalar", ACT(Ib[tb][:, sl], pb[:, :], AF.Tanh, bias=P["bx"], scale=0.5), reads=[rb, rC], writes=[rI[tb]])
                cpb, crb = cbanks[tt]
                S.op("vector", STT(Ib[tb][:, sl], Ib[tb][:, sl], 1.0, cpb[:, :], ALU.add, ALU.mult), reads=[crb, rI[tb]], writes=[rI[tb]])
            S.op("scalar", ACT(Ab[tb], Rb[tb], AF.Exp, scale=P["clh"], bias=P["clh"]), reads=[rR[tb], rC], writes=[rA[tb]])
            S.op("scalar", ACT(Rb[tb], Rb[tb], AF.Exp, scale=P["cl"], bias=P["cl"]), reads=[rR[tb], rC], writes=[rR[tb]])

        def stageB(it):
            c, s = it // 4, it % 4
            wb = c % 2
            tb = it % NSET
            pbuf = (it - 1) % NSET
            col0 = s * NS
            S.op("scalar", ACT(Rb[tb], Rb[tb], AF.Sqrt, scale=-1.0, bias=1.0), reads=[rR[tb]], writes=[rR[tb]])
            S.op("vector", TT(Ib[tb], Ib[tb], Rb[tb], ALU.mult), reads=[rI[tb], rR[tb]], writes=[rI[tb]])
            if s == 0:
                init, rds = 0.0, [rA[tb], rI[tb]]
            elif s == 2:
                init = INITS[:, c:c + 1]
                S.op("vector", TT(init, UC[pbuf][:, NS - 1:NS], PVT[:, 0:1], ALU.mult), reads=[rUC[pbuf], rC], writes=[rSM])
                rds = [rA[tb], rI[tb], rSM]
            else:
                init, rds = UC[pbuf][:, NS - 1:NS], [rA[tb], rI[tb], rUC[pbuf]]
            S.op("vector", SCAN(UC[tb], Ab[tb], Ib[tb], init), reads=rds, writes=[rUC[tb]])
            if s >= 2:
                for tt in range(2):
                    sl = slice(tt * 512, (tt + 1) * 512)
                    pb, rb = nextbank()
                    for kc in range(8):
                        S.op("tensor", MM(pb[:, :], WZ[wb][:, kc, :], XNT[:, kc, col0 + tt * 512:col0 + (tt + 1) * 512], kc == 0, kc == 7),
                             reads=[rWZ[wb]] + xr(col0 + tt * 512, 512), writes=[rb])
                    S.op("scalar", ACT(Rb[tb][:, sl], pb[:, :], AF.Tanh, scale=0.5), reads=[rb], writes=[rR[tb]])
                    S.op("vector", STT(Rb[tb][:, sl], Rb[tb][:, sl], 1.0, pb[:, :], ALU.add, ALU.mult), reads=[rb, rR[tb]], writes=[rR[tb]])
                S.op("vector", STT(HZT[:, c, (s - 2) * NS:(s - 1) * NS], Rb[tb], 0.25, UC[tb], ALU.mult, ALU.mult), reads=[rUC[tb], rR[tb]], writes=[rHZT[c]])
            if s == 3:
                S.op("vector", TS(HOUT[:, c:c + 1], UC[tb][:, NS - 1:NS], 0.5, None, ALU.mult), reads=[rUC[tb]], writes=[rOUTS])
                pb, rb = nextbank()
                for kc in range(8):
                    S.op("tensor", MM(pb[:, 0:4], WU[wb][:, kc, :], XNT[:, kc, 4096:4100], kc == 0, kc == 7), reads=[rWU[wb], rXNT[32]], writes=[rb])
                S.op("vector", COPY(SUC[:, c * 4:(c + 1) * 4], pb[:, 0:4]), reads=[rb], writes=[rSUC])
                pb, rb = nextbank()
                for kc in range(8):
                    S.op("tensor", MM(pb[:, 0:4], WZ[wb][:, kc, :], XNT[:, kc, 4096:4100], kc == 0, kc == 7), reads=[rWZ[wb], rXNT[32]], writes=[rb])
                S.op("scalar", ACT(SZS[:, c * 4:(c + 1) * 4], pb[:, 0:4], AF.Tanh, scale=0.5), reads=[rb], writes=[rSZS])
                S.op("vector", STT(SZS[:, c * 4:(c + 1) * 4], SZS[:, c * 4:(c + 1) * 4], 1.0, pb[:, 0:4], ALU.add, ALU.mult), reads=[rb, rSZS], writes=[rSZS])
                S.op("vector", TS(SZS[:, c * 4:(c + 1) * 4], SZS[:, c * 4:(c + 1) * 4], 0.5, None, ALU.mult), reads=[rSZS], writes=[rSZS])

        SW = A.f32(64)
        SWB = A.bf16(8)
        rSW = Res("sw")
        def sample_step(c):
                usl = SUC[:, c * 4:(c + 1) * 4]
                uc = SW[:, 0:4]
                st_ = SSTT3[:, c, 0:12].rearrange("p (b t) -> p b t", b=4)
                S.op("vector", TS(uc, usl, PRM[:, 24 + c:25 + c], PRM[:, 32 + c:33 + c], ALU.mult, ALU.add), reads=[rSUC, rC], writes=[rSW])
                for tap in range(3):
                    S.op("vector", STT(uc, st_[:, :, tap], PRM[:, tap * 8 + c:tap * 8 + c + 1], uc, ALU.mult, ALU.add), reads=[rSST, rC, rSW], writes=[rSW])
                S.op("vector", COPY(SWB[:, 0:4], uc), reads=[rSW], writes=[rSW])
                pb, rb = nextbank()
                S.op("tensor", MM(pb[:, 0:4], WG[:, c, 0, :], SWB[:, 0:4], True, True), reads=[rC, rSW], writes=[rb])
                S.op("tensor", MM(pb[:, 4:8], WG[:, c, 1, :], SWB[:, 0:4], True, True), reads=[rC, rSW], writes=[rb])
                r_ = SW[:, 4:8]
                i_ = SW[:, 8:12]
                a_ = SW[:, 12:16]
                S.op("scalar", ACT(r_, pb[:, 0:4], AF.Tanh, bias=HB[:, c:c + 1], scale=0.5), reads=[rb, rC], writes=[rSW])
                S.op("scalar", ACT(i_, pb[:, 4:8], AF.Tanh, bias=HB[:, 8 + c:9 + c], scale=0.5), reads=[rb, rC], writes=[rSW])
                S.op("scalar", ACT(a_, r_, AF.Exp, scale=CLT[:, c:c + 1], bias=CLT[:, c:c + 1]), reads=[rSW, rC], writes=[rSW])
                S.op("scalar", ACT(r_, r_, AF.Exp, scale=CLT[:, 8 + c:9 + c], bias=CLT[:, 8 + c:9 + c]), reads=[rSW, rC], writes=[rSW])
                S.op("scalar", ACT(r_, r_, AF.Sqrt, scale=-1.0, bias=1.0), reads=[rSW], writes=[rSW])
                S.op("vector", STT(i_, i_, 1.0, r_, ALU.add, ALU.mult), reads=[rSW], writes=[rSW])
                S.op("vector", STT(i_, i_, 0.5, uc, ALU.mult, ALU.mult), reads=[rSW], writes=[rSW])
                S.op("vector", TT(a_, a_, SSTT3[:, c, 12:16], ALU.mult), reads=[rSW, rSST], writes=[rSW])
                S.op("vector", TT(SH[:, c * 4:(c + 1) * 4], a_, i_, ALU.add), reads=[rSW], writes=[rSH])
                S.op("vector", TT(HZT[:, c, 2048:2052], SH[:, c * 4:(c + 1) * 4], SZS[:, c * 4:(c + 1) * 4], ALU.mult), reads=[rSH, rSZS], writes=[rHZTs])

        NIT = 32
        load_w2(1)
        WQKV_first = arena_t[:, SCR0:SCR0 + 3072].bitcast(BF16).rearrange("p (k n) -> p k n", k=8)
        rWQ2 = [Res("wq0"), Res("wq1")]
        stageA(0)
        stageA(1)
        stageA2(0)
        for it in range(NIT):
            if it + 2 < NIT:
                stageA(it + 2)
            if it + 1 < NIT:
                stageA2(it + 1)
            stageB(it)
            if it % 4 == 3:
                if it // 4 + 2 < 8:
                    load_w2(it // 4 + 2)
                sample_step(it // 4)
            if it == 13:
                S.dma("gpsimd", DMA(WQKV_first, wqkv[0].rearrange("p (k n) -> p k n", k=8)), rWQ2[0],
                      writes=[rWQ2[0], rT, rG, rXT[0], rXT[1]])
        S.dma("sync", DMA(convs_u, SUC), rSUC, reads=[rSUC])
        S.dma("sync", DMA(hs, SH), rSH, reads=[rSH])
        S.dma("sync", DMA(hp, HOUT), rOUTS, reads=[rOUTS])
        S.dma("sync", DMA(convp, CONVP.rearrange("p c t -> p (c t)")), rOUTS, reads=[rOUTS])
        S.barrier()
        rX = Res("xnt_all")
        if upto == 2:
            if dbgfn is not None:
                dbgfn(locals())
                S.barrier()
            S.emit()
            return nc

        A.top = SCR0
        WQKV2 = [A.bf16(8 * 768).rearrange("p (k n) -> p k n", k=8) for _ in range(2)]
        assert A.top == SCR0 + 6144
        NRB = 6
        KT = A.bf16(2 * NRB * 128).rearrange("p (h b n) -> p h b n", h=2, b=NRB)
        Vflat = A.bf16(NRB * 4 * 66)
        Vb = Vflat.rearrange("p (b h e) -> p b h e", b=NRB, h=4)
        Vm = Vflat.rearrange("p (m e) -> p m e", e=66)
        QT2 = [A.bf16(2 * 2048).rearrange("p (h n) -> p h n", h=2) for _ in range(2)]
        ET2 = [A.f32(3 * 512).rearrange("p (k n) -> p k n", k=3) for _ in range(2)]
        NR = 3
        KVF = [A.f32(512) for _ in range(2)] + [None]
        KB = [A.bf16(256) for _ in range(NR)]
        EX = [A.f32(512) for _ in range(NR)]
        PT = [A.bf16(512) for _ in range(NR)]
        OS = [A.f32(260) for _ in range(NR)]
        CK = [A.f32(512) for _ in range(2)]
        SQKV = A.f32(768)
        SPR = A.f32(260)
        SPV = A.bf16(260)
        SELQB = A.bf16(512)
        SELOB = A.bf16(16)
        SQB = A.bf16(256)
        rSELB = Res("selb")
        slot_of = {}
        rKT = [Res(f"kt{i}") for i in range(32)]
        rV = [Res(f"v{i}") for i in range(32)]
        rVones = Res("vones")
        rQT2 = [[[Res(f"qt{q}{h}{t}") for t in range(4)] for h in range(2)] for q in range(2)]
        rET2 = [Res("et0"), Res("et1")]
        rKVF = [Res("kvf0"), Res("kvf1")]
        rKB = [Res("kb%d" % i) for i in range(3)]
        rEX = [Res("ex%d" % i) for i in range(3)]
        rPT = [Res("pt%d" % i) for i in range(3)]
        rOS = [Res("os%d" % i) for i in range(3)]
        rCK = [Res("ck0"), Res("ck1")]
        rSQKV, rSPR, rSPV = Res("sqkv"), Res("spr"), Res("spv")
        rOSC = Res("osc")
        cnt = {"kvf": 0, "kb": 0, "ex": 0, "os": 0, "ck": 0, "blk": 0}
        S.op("vector", COPY(SELQB[0:4, :], SELQ[0:4, :]), reads=[rC], writes=[rSELB])
        S.op("vector", COPY(SELOB, SELO), reads=[rC], writes=[rSELB])
        S.op("vector", MEMSET(Vm[:, :, 64:65], 1.0), writes=[rVones] + rV[:NRB])
        for g in range(3):
            d = DIL[g]
            nb = 16 // d
            for hh in range(2):
                sp = g * 2 + hh
                if sp + 1 < 6:
                    S.dma("gpsimd", DMA(WQKV2[(sp + 1) % 2], wqkv[sp + 1].rearrange("p (k n) -> p k n", k=8)), rWQ2[(sp + 1) % 2], writes=[rWQ2[(sp + 1) % 2]])
                WQKV = WQKV2[sp % 2]
                WQ = WQKV[:, :, 0:256]
                WKV = WQKV[:, :, 256:768]
                rWQ = rWQ2[sp % 2]
                rWKV = rWQ
                QT, rQT = QT2[sp % 2], rQT2[sp % 2]
                ET, rET = ET2[sp % 2], rET2[sp % 2]
                for kind in range(3):
                    S.dma("sync", DMA(ET[:, kind, :], etab[g, kind][:, hh * 512:(hh + 1) * 512]), rET, writes=[rET])
                for hpp in range(2):
                    for tt in range(4):
                        pb, rb = nextbank()
                        for kc in range(8):
                            S.op("tensor", MM(pb[:, :], WQ[:, kc, hpp * 128:(hpp + 1) * 128], XNT[:, kc, 2048 + tt * 512:2048 + (tt + 1) * 512], kc == 0, kc == 7),
                                 reads=[rWQ, rX], writes=[rb])
                        md = 512 // d
                        qdst = QT[:, hpp, :].rearrange("p (r m) -> p r m", r=d)[:, :, tt * md:(tt + 1) * md]
                        S.op("scalar", ACT(qdst, pb[:, :].rearrange("p (m r) -> p r m", r=d), AF.Copy, scale=0.125), reads=[rb], writes=[rQT[hpp][tt]])
                pb, rb = nextbank()
                for kc in range(8):
                    S.op("tensor", MM(pb[0:4, 0:256], XNT[:, kc, 4096:4100], WQ[:, kc, :], kc == 0, kc == 7), reads=[rWQ, rX], writes=[rb])
                S.op("vector", COPY(SQKV[0:4, 0:256], pb[0:4, 0:256]), reads=[rb], writes=[rSQKV])
                S.op("vector", COPY(SQB[0:4, :], pb[0:4, 0:256]), reads=[rb], writes=[rSQKV])
                pb, rb = nextbank()
                for kc in range(8):
                    S.op("tensor", MM(pb[0:4, :], XNT[:, kc, 4096:4100], WKV[:, kc, :], kc == 0, kc == 7), reads=[rWKV, rX], writes=[rb])
                S.op("vector", COPY(SQKV[0:4, 256:768], pb[0:4, :]), reads=[rb], writes=[rSQKV])
                S.dma("sync", DMA(kvs[g, :, :, hh * 256:(hh + 1) * 256], SQKV[0:4, 256:768].rearrange("p (t n) -> p t n", t=2)), rSQKV, reads=[rSQKV])

                def produce(rho, n):
                    bi = cnt["blk"] % NRB
                    cnt["blk"] += 1
                    slot_of[(g, hh, rho, n)] = bi
                    c0, stp = block_cols(g, rho, n)
                    pb, rb = nextbank()
                    for kc in range(8):
                        S.op("tensor", MM(pb[:, :], XNT[:, kc, SL(c0, 128, stp)], WKV[:, kc, :], kc == 0, kc == 7),
                             reads=[rX, rWKV], writes=[rb])
                    if n == nb - 1:
                        fb = cnt["kvf"] % 2
                        cnt["kvf"] += 1
                        S.op("scalar", ACT(KVF[fb], pb[:, :], AF.Copy), reads=[rb], writes=[rKVF[fb]])
                        dst = kvo[g][SL(rho, 128, d), :, hh * 256:(hh + 1) * 256]
                        S.dma("sync", DMA(dst, KVF[fb].rearrange("p (t n) -> p t n", t=2)), rKVF[fb], reads=[rKVF[fb]])
                    kb = cnt["kb"] % NR
                    cnt["kb"] += 1
                    S.op("vector", COPY(KB[kb], pb[:, 0:256]), reads=[rb], writes=[rKB[kb]])
                    S.op("vector", COPY(Vb[:, bi, :, 0:64], pb[:, 256:512].rearrange("p (h e) -> p h e", h=4)), reads=[rb, rVones], writes=[rV[bi]])
                    pb2, rb2 = nextbank()
                    pbb = pb2[:, :].bitcast(BF16)
                    for hpp in range(2):
                        S.op("tensor", TR(pbb[:, hpp * 128:(hpp + 1) * 128], KB[kb][:, hpp * 128:(hpp + 1) * 128], IDENT), reads=[rKB[kb], rC], writes=[rb2])
                    S.op("scalar", ACT(KT[:, :, bi, :], pbb[:, 0:256].rearrange("p (h n) -> p h n", h=2), AF.Copy), reads=[rb2], writes=[rKT[bi]])

                def attend(rho, n):
                    q0 = rho + d * 128 * n
                    qb0 = rho * (2048 // d) + 128 * n
                    qr = [rQT[0][t] for t in range(4)] + [rQT[1][t] for t in range(4)]
                    po, ro = nextbank()
                    for bk, nn in enumerate((n - 1, n)):
                        bi = slot_of[(g, hh, rho, nn)]
                        pbs = [nextbank(), nextbank()]
                        for h4 in range(4):
                            hpp, hf = h4 // 2, h4 % 2
                            pb, rb = pbs[hf]
                            S.op("tensor", MM(pb[:, hpp * 128:(hpp + 1) * 128], KT[hf * 64:(hf + 1) * 64, hpp, bi, :],
                                              QT[hf * 64:(hf + 1) * 64, hpp, qb0:qb0 + 128], True, True),
                                 reads=[rKT[bi]] + qr, writes=[rb])
                        xb = cnt["ex"] % NR
                        cnt["ex"] += 1
                        EX4 = EX[xb].rearrange("p (a f n) -> p a f n", a=2, f=2)
                        for hf in range(2):
                            pb, rb = pbs[hf]
                            S.op("scalar", ACT(EX4[:, :, hf, :], pb[:, 0:256].rearrange("p (a n) -> p a n", a=2), AF.Exp), reads=[rb], writes=[rEX[xb]])
                        kind = 0 if bk == 1 else (2 if n == 0 else 1)
                        S.op("vector", TT(PT[xb], EX[xb], ET[:, kind, :], ALU.mult), reads=[rEX[xb], rET], writes=[rPT[xb]])
                        for h4 in range(4):
                            S.op("tensor", MM(po[:, h4 * 65:(h4 + 1) * 65], PT[xb][:, h4 * 128:(h4 + 1) * 128], Vb[:, bi, h4, 0:65],
                                              bk == 0 and h4 == 0, bk == 1, skip=True),
                                 reads=[rPT[xb], rV[bi], rVones], writes=[ro])
                    ob = cnt["os"] % NR
                    cnt["os"] += 1
                    S.op("vector", COPY(OS[ob], po[:, 0:260]), reads=[ro], writes=[rOS[ob]])
                    S.dma("sync", DMA(osc[g][SL(q0, 128, d), hh, :], OS[ob]), rOS[ob], reads=[rOS[ob]], writes=[rOSC])

                for rho in range(d):
                    for n in range(-1, nb):
                        produce(rho, n)
                        if n >= 0:
                            attend(rho, n)
                po, ro = nextbank()
                for b in range(4):
                    cb_ = cnt["ck"] % 2
                    cnt["ck"] += 1
                    S.dma("sync", DMA(CK[cb_].rearrange("p (t n) -> p t n", t=2), cache[g][b, SL(0, 128, d), :, hh * 256:(hh + 1) * 256]), rCK[cb_], writes=[rCK[cb_]])
                    pb, rb = nextbank()
                    S.op("tensor", MM(pb[:, 0:256], SELQB[0:4, b * 128:(b + 1) * 128], SQB[0:4, :], True, True), reads=[rSELB, rSQKV], writes=[rb])
                    S.op("vector", TT(SPR[:, 0:256], CK[cb_][:, 0:256], pb[:, 0:256], ALU.mult), reads=[rCK[cb_], rb], writes=[rSPR])
                    sc = SMALL[:, 40:44]
                    S.op("vector", REDUCE(sc, SPR[:, 0:256].rearrange("p (h e) -> p h e", h=4)), reads=[rSPR], writes=[rSM])
                    S.op("vector", STT(sc, sc, 0.125, SBIAS[:, g * 8 + hh * 4:g * 8 + hh * 4 + 4], ALU.mult, ALU.add), reads=[rSM, rC], writes=[rSM])
                    SPV3 = SPV.rearrange("p (h e) -> p h e", h=4)
                    S.op("scalar", ACT(SPV3[:, :, 64], sc, AF.Exp), reads=[rSM], writes=[rSPV])
                    S.op("vector", TT(SPV3[:, :, 0:64], CK[cb_][:, 256:512].rearrange("p (h e) -> p h e", h=4),
                                      SPV3[:, :, 64:65].to_broadcast([128, 4, 64]), ALU.mult), reads=[rCK[cb_], rSPV], writes=[rSPV])
                    S.op("tensor", MM(po[0:4, 0:260], SELOB[:, b * 4:(b + 1) * 4], SPV, b == 0, b == 3), reads=[rSELB, rSPV], writes=[ro])
                if sub == 5:
                    S.barrier()
                    S.emit()
                    return nc
                q_s, k_s, v_s = SQKV[0:4, 0:256], SQKV[0:4, 256:512], SQKV[0:4, 512:768]
                S.op("vector", TT(SPR[0:4, 0:256], q_s, k_s, ALU.mult), reads=[rSQKV], writes=[rSPR])
                sc = SMALL[0:4, 44:48]
                S.op("vector", REDUCE(sc, SPR[0:4, 0:256].rearrange("p (h e) -> p h e", h=4)), reads=[rSPR], writes=[rSM])
                SPVn = SPR[0:4, 0:260].rearrange("p (h e) -> p h e", h=4)
                S.op("scalar", ACT(SPVn[:, :, 64], sc, AF.Exp, scale=0.125), reads=[rSM], writes=[rSPR])
                S.op("vector", TT(SPVn[:, :, 0:64], v_s.rearrange("p (h e) -> p h e", h=4), SPVn[:, :, 64:65].to_broadcast([4, 4, 64]), ALU.mult),
                     reads=[rSQKV, rSPR], writes=[rSPR])
                sn = SNUM[0:4, hh * 260:(hh + 1) * 260]
                S.op("vector", TT(sn, sn, SPR[0:4, 0:260], ALU.add), reads=[rSPR, rSNUM], writes=[rSNUM])
                S.op("vector", TT(sn, sn, po[0:4, 0:260], ALU.add), reads=[ro, rSNUM], writes=[rSNUM])
        S.barrier()
        if upto == 3:
            if dbgfn is not None:
                dbgfn(locals())
                S.barrier()
            S.emit()
            return nc

        A.top = SCR0
        A.reserve_top = 4096 + 1792
        WOUT = arena_t[:, AW - 4096:AW].bitcast(BF16).rearrange("p (k n) -> p k n", k=8)
        rWOUT = Res("wout")
        hi0 = AW - 4096 - 1792

        def hv(off, words, k):
            return arena_t[:, hi0 + off:hi0 + off + words].bitcast(BF16).rearrange("p (k n) -> p k n", k=k)

        WGR = [hv(0, 512, 8), None]
        WGA = [hv(512, 512, 8), None]
        WRO = [hv(1024, 512, 8), None]
        WAO = [hv(1536, 256, 4), None]
        rW5 = [Res("w5_0"), Res("w5_1")]

        def load_w5(j):
            b = j % 2
            S.dma("gpsimd", DMA(WGR[b], win[56 + j].rearrange("p (k n) -> p k n", k=8)), rW5[b], writes=[rW5[b]])
            S.dma("gpsimd", DMA(WGA[b], win[64 + j].rearrange("p (k n) -> p k n", k=8)), rW5[b], writes=[rW5[b]])
            S.dma("gpsimd", DMA(WRO[b], wro[j].rearrange("p (k n) -> p k n", k=8)), rW5[b], writes=[rW5[b]])
            S.dma("gpsimd", DMA(WAO[b], wao[j].rearrange("p (k n) -> p k n", k=4)), rW5[b], writes=[rW5[b]])
        AZT = A.bf16(4 * FW).rearrange("p (k n) -> p k n", k=4)
        SCR1 = A.top
        SZA = A.f32(4 * FW).rearrange("p (k n) -> p k n", k=4)
        WZA = [A.bf16(1024).rearrange("p (k n) -> p k n", k=8) for _ in range(2)]
        NO = 3
        OT = [A.f32(3 * 520).rearrange("p (g n) -> p g n", g=3) for _ in range(NO)]
        MG = [A.f32(512) for _ in range(NO)]
        rSZA = [Res(f"sza{i}") for i in range(4)]
        rWZA = [Res("wza0"), Res("wza1")]
        rOT = [Res("ot%d" % i) for i in range(4)]
        rMG = [Res("mg%d" % i) for i in range(4)]
        rAZT = Res("azt")
        ttiles = [(2048 + t * 512, t * 512, 512) for t in range(4)] + [(4096, 2048, 4)]
        for c4 in range(4):
            wb = c4 % 2
            S.dma("gpsimd", DMA(WZA[wb], win[52 + c4].rearrange("p (k n) -> p k n", k=8)), rWZA[wb], writes=[rWZA[wb]])
            if c4 == 1:
                load_w5(0)
            for (xc, fc, nn) in ttiles:
                pb, rb = nextbank()
                for kc in range(8):
                    S.op("tensor", MM(pb[:, 0:nn], WZA[wb][:, kc, :], XNT[:, kc, xc:xc + nn], kc == 0, kc == 7), reads=[rWZA[wb], rX], writes=[rb])
                S.op("scalar", ACT(SZA[:, c4, fc:fc + nn], pb[:, 0:nn], AF.Silu), reads=[rb], writes=[rSZA[c4]])

        for kq in range(4):
            S.dma("gpsimd", DMA(WOUT[:, 2 * kq:2 * kq + 2, :], wout[:, kq * 2048:(kq + 1) * 2048].rearrange("p (k n) -> p k n", k=2)), rWOUT, writes=[rWOUT])

        def load_ot(t):
            b = t % NO
            for g in range(3):
                S.dma("sync", DMA(OT[b][:, g, :], osc[g][t * 128:(t + 1) * 128].rearrange("p h n -> p (h n)")), rOT[b], writes=[rOT[b]])

        for t0 in range(NO - 1):
            load_ot(t0)
        for t in range(17):
            b = t % NO
            if t < 16:
                if t + NO - 1 < 16:
                    load_ot(t + NO - 1)
                np_ = 128
                s1 = OT[b][:, 0, :]
                S.op("vector", TT(s1, s1, OT[b][:, 1, :], ALU.add), reads=[rOT[b]], writes=[rOT[b]])
                S.op("vector", TT(s1, s1, OT[b][:, 2, :], ALU.add), reads=[rOT[b]], writes=[rOT[b]])
                rsrc = rOT[b]
                fc = t * 128
            else:
                np_ = 4
                s1 = SNUM[0:4, :]
                rsrc = rSNUM
                fc = 2048
            s3 = s1[0:np_, :].rearrange("p (h e) -> p h e", h=8)
            rd = SMALL[0:np_, 48:56]
            S.op("vector", RECIP(rd, s3[:, :, 64]), reads=[rsrc], writes=[rSM])
            S.op("vector", TT(MG[b][0:np_, :].rearrange("p (h e) -> p h e", h=8), s3[:, :, 0:64], rd.unsqueeze(2).to_broadcast([np_, 8, 64]), ALU.mult),
                 reads=[rsrc, rSM], writes=[rMG[b]])
            pb, rb = nextbank()
            for c4 in range(4):
                S.op("tensor", TR(pb[:, c4 * np_:(c4 + 1) * np_], MG[b][0:np_, c4 * 128:(c4 + 1) * 128], IDENTF[0:np_, 0:np_]), reads=[rMG[b], rC], writes=[rb])
            S.op("vector", TT(AZT[:, :, fc:fc + np_], pb[:, 0:4 * np_].rearrange("p (k n) -> p k n", k=4), SZA[:, :, fc:fc + np_], ALU.mult),
                 reads=[rb] + rSZA, writes=[rAZT])
        S.barrier()
        if upto == 4:
            if dbgfn is not None:
                dbgfn(locals())
                S.barrier()
            S.emit()
            return nc

        A.top = SCR1
        MIXT = A.bf16(8 * FW).rearrange("p (k n) -> p k n", k=8)
        WGR[1] = A.bf16(1024).rearrange("p (k n) -> p k n", k=8)
        WGA[1] = A.bf16(1024).rearrange("p (k n) -> p k n", k=8)
        WRO[1] = A.bf16(1024).rearrange("p (k n) -> p k n", k=8)
        WAO[1] = A.bf16(512).rearrange("p (k n) -> p k n", k=4)
        SGR = [A.f32(512) for _ in range(2)]
        SGA = [A.f32(512) for _ in range(2)]
        rSGR = [Res("sgr0"), Res("sgr1")]
        rSGA = [Res("sga0"), Res("sga1")]
        rMIX = [Res(f"mix{j}") for j in range(8)]

        k5 = 0
        for j in range(8):
            wb = j % 2
            if j + 1 < 8:
                load_w5(j + 1)
            for (xc, fc, nn) in ttiles:
                sb = k5 % 2
                k5 += 1
                p1, r1 = nextbank()
                for kc in range(8):
                    S.op("tensor", MM(p1[:, 0:nn], WGR[wb][:, kc, :], XNT[:, kc, xc:xc + nn], kc == 0, kc == 7), reads=[rW5[wb], rX], writes=[r1])
                S.op("scalar", ACT(SGR[sb][:, 0:nn], p1[:, 0:nn], AF.Sigmoid), reads=[r1], writes=[rSGR[sb]])
                p2, r2 = nextbank()
                for kc in range(8):
                    S.op("tensor", MM(p2[:, 0:nn], WGA[wb][:, kc, :], XNT[:, kc, xc:xc + nn], kc == 0, kc == 7), reads=[rW5[wb], rX], writes=[r2])
                S.op("scalar", ACT(SGA[sb][:, 0:nn], p2[:, 0:nn], AF.Sigmoid), reads=[r2], writes=[rSGA[sb]])
                p3, r3 = nextbank()
                for kc in range(8):
                    S.op("tensor", MM(p3[:, 0:nn], WRO[wb][:, kc, :], HZT[:, kc, fc:fc + nn], kc == 0, kc == 7), reads=[rW5[wb]], writes=[r3])
                S.op("vector", TT(SGR[sb][:, 0:nn], SGR[sb][:, 0:nn], p3[:, 0:nn], ALU.mult), reads=[r3, rSGR[sb]], writes=[rSGR[sb]])
                p4, r4 = nextbank()
                for kc in range(4):
                    S.op("tensor", MM(p4[:, 0:nn], WAO[wb][:, kc, :], AZT[:, kc, fc:fc + nn], kc == 0, kc == 3), reads=[rW5[wb]], writes=[r4])
                S.op("vector", TT(SGA[sb][:, 0:nn], SGA[sb][:, 0:nn], p4[:, 0:nn], ALU.mult), reads=[r4, rSGA[sb]], writes=[rSGA[sb]])
                S.op("vector", TT(MIXT[:, j, fc:fc + nn], SGR[sb][:, 0:nn], SGA[sb][:, 0:nn], ALU.add), reads=[rSGR[sb], rSGA[sb]], writes=[rMIX[j]])
        S.barrier()
        if upto == 5:
            if dbgfn is not None:
                dbgfn(locals())
                S.barrier()
            S.emit()
            return nc

        A.top = 0
        N6 = 4
        XT6 = [A.f32(1024) for _ in range(N6)]
        YT = [A.f32(1024) for _ in range(N6)]
        SQ6 = A.f32(512)
        SM6 = A.f32(16)
        GPOST = A.f32(1024)
        rGP = Res("gpost")
        S.dma("sync", DMA(GPOST, gpost.partition_broadcast(128)), rGP, writes=[rGP])
        rXT6 = [Res("xt6_%d" % i) for i in range(4)]
        rYT = [Res("yt%d" % i) for i in range(4)]
        rSQ6 = Res("sq6")
        t6 = [(xm, t * 128, y, t * 128, t * 128, 128) for t in range(16)] + [(xsd, 0, ys, 0, 2048, 4)]

        def load_x6(i):
            src, r0, _, _, _, np_ = t6[i]
            b = i % N6
            S.dma("sync", DMA(XT6[b][0:np_, :], src[r0:r0 + np_, :]), rXT6[b], writes=[rXT6[b]])

        for i0 in range(N6 - 1):
            load_x6(i0)
        for i, (src, r0, dst, d0, fc, np_) in enumerate(t6):
            b = i % N6
            if i + N6 - 1 < len(t6):
                load_x6(i + N6 - 1)
            banks = []
            for hf in range(2):
                pb, rb = nextbank()
                for kc in range(8):
                    S.op("tensor", MM(pb[0:np_, :], MIXT[:, kc, fc:fc + np_], WOUT[:, kc, hf * 512:(hf + 1) * 512], kc == 0, kc == 7), reads=[rWOUT], writes=[rb])
                banks.append((pb, rb))
            rss = Res("ss6")
            ssa = SM6[0:np_, 4 * b:4 * b + 1]
            ssb = SM6[0:np_, 4 * b + 1:4 * b + 2]
            rs = SM6[0:np_, 4 * b + 2:4 * b + 3]
            S.op("scalar", ACT(SQ6[0:np_, :], banks[0][0][0:np_, :], AF.Square, accum_out=ssa), reads=[banks[0][1]], writes=[rSQ6, rss])
            S.op("scalar", ACT(SQ6[0:np_, :], banks[1][0][0:np_, :], AF.Square, accum_out=ssb), reads=[banks[1][1], rss], writes=[rSQ6, rss])
            S.op("vector", TT(rs, ssa, ssb, ALU.add), reads=[rss], writes=[rss])
            S.op("scalar", ACT(rs, rs, AF.Sqrt, scale=1.0 / 1024, bias=EPS), reads=[rss], writes=[rss])
            S.op("vector", RECIP(rs, rs), reads=[rss], writes=[rss])
            for hf in range(2):
                sl = slice(hf * 512, (hf + 1) * 512)
                S.op("vector", STT(YT[b][0:np_, sl], banks[hf][0][0:np_, :], rs, GPOST[0:np_, sl], ALU.mult, ALU.mult),
                     reads=[banks[hf][1], rss, rGP], writes=[rYT[b]])
            S.op("vector", TT(YT[b][0:np_, :], YT[b][0:np_, :], XT6[b][0:np_, :], ALU.add), reads=[rYT[b], rXT6[b]], writes=[rYT[b]])
            S.dma("sync", DMA(dst[d0:d0 + np_, :], YT[b][0:np_, :]), rYT[b], reads=[rYT[b]])
        S.barrier()
        assert S.nsem <= 100, S.nsem
        S.emit()
    return nc


def _alibi_slopes():
    n = 24
    return (2.0 ** (-8.0 * np.arange(1, n + 1, dtype=np.float64) / n)).reshape(3, 8)


def _tables(half):
    sl = _alibi_slopes()
    i = np.arange(128)
    et = np.zeros((3, 3, 128, 8, 128), np.float32)
    for g in range(3):
        d = DIL[g]
        for h in range(8):
            s = sl[g, h]
            steps_cur = (i[None, :] - i[:, None]).astype(np.float64)
            cur = np.where(steps_cur >= 0, np.exp(-s * d * steps_cur), 0.0)
            steps_prev = 128 + steps_cur
            prev = np.where(steps_prev <= 128, np.exp(-s * d * steps_prev), 0.0)
            et[g, 0, :, h, :] = cur
            et[g, 1, :, h, :] = prev
            et[g, 2, :, h, :] = prev * float(half)
    sb = np.zeros((128, 3, 8), np.float32)
    m = np.arange(128)
    for g in range(3):
        d = DIL[g]
        for h in range(8):
            sb[:, g, h] = -sl[g, h] * d * (128 - m)
    return et.reshape(3, 3, 128, 1024), sb.reshape(128, 24)


_NC_CACHE = {}


def prep_inputs(x_prompt, x_sample, state_conv, state_h, cache_kv_w128, cache_kv_w512, cache_kv_w2048,
                norm_pre, norm_post, w_in, conv_w, conv_b, lru_w_a, lru_b_a, lru_w_x, lru_b_x, lru_lambda,
                w_rnn_out, w_attn_out, w_out):
    f = lambda a: np.ascontiguousarray(np.asarray(a, dtype=np.float32))
    x_prompt, x_sample, state_conv, state_h = f(x_prompt), f(x_sample), f(state_conv), f(state_h)
    caches = [f(cache_kv_w128), f(cache_kv_w512), f(cache_kv_w2048)]
    w_in = f(w_in)[0]
    def tile_w(w, nchunk, kcs):
        return np.ascontiguousarray(w.reshape(kcs, 128, nchunk, 128).transpose(2, 1, 0, 3).reshape(nchunk, 128, kcs * 128))
    win = tile_w(w_in, 72, 8)
    w4 = w_in.reshape(8, 128, 9216)
    parts = []
    for g in range(3):
        for hh in range(2):
            cs = g * 512 + hh * 256
            parts.append(np.concatenate([w4[:, :, 2048 + cs:2048 + cs + 256], w4[:, :, 3584 + cs:3584 + cs + 256],
                                         w4[:, :, 5120 + cs:5120 + cs + 256]], axis=2).transpose(1, 0, 2).reshape(128, 6144))
    wqkv = np.ascontiguousarray(np.stack(parts))
    wro = tile_w(f(w_rnn_out)[0], 8, 8)
    wao = tile_w(f(w_attn_out)[0], 8, 4)
    wout = np.ascontiguousarray(f(w_out)[0].reshape(8, 128, 1024).transpose(1, 0, 2).reshape(128, 8192))
    wga = np.stack([np.ascontiguousarray(w.reshape(8, 2, 64, 64).transpose(1, 2, 0, 3).reshape(2, 64, 512))
                    for w in (f(lru_w_a)[0], f(lru_w_x)[0])])
    prm = np.concatenate([f(conv_w)[0].reshape(32, 128), f(conv_b)[0].reshape(8, 128), f(lru_b_a)[0].reshape(8, 128),
                          f(lru_b_x)[0].reshape(8, 128), f(lru_lambda)[0].reshape(8, 128)], axis=0)
    ident = np.eye(128, dtype=np.float32)
    selq = np.zeros((4, 4, 128), np.float32)
    selo = np.zeros((128, 4, 4), np.float32)
    for b in range(4):
        selq[b, b, :] = 1.0
        selo[:, b, b] = 1.0
    selq = selq.reshape(4, 512)
    selo = selo.reshape(128, 16)
    tabs = [_tables(0), _tables(1)]
    in_maps = []
    for c in range(8):
        b, half = c // 2, c % 2
        xm = x_prompt[b, half * 2048:(half + 1) * 2048]
        xp = x_prompt[b, 0:2048] if half == 1 else np.zeros((2048, 1024), np.float32)
        sl = slice(4 * c, 4 * c + 4)
        sst = np.concatenate([state_conv[0, sl].reshape(12, 1024), state_h[0, sl]], axis=0)
        in_maps.append({
            "xm": np.ascontiguousarray(xm), "xp": np.ascontiguousarray(xp),
            "pv": np.full((128, 1), float(half), np.float32),
            "xs": np.ascontiguousarray(x_sample[sl, 0, :]), "sst": np.ascontiguousarray(sst),
            "sconv": np.ascontiguousarray(state_conv[0, sl]),
            "c0": np.ascontiguousarray(caches[0][0, sl].reshape(4, 128, 2, 512)),
            "c1": np.ascontiguousarray(caches[1][0, sl].reshape(4, 512, 2, 512)),
            "c2": np.ascontiguousarray(caches[2][0, sl].reshape(4, 2048, 2, 512)),
            "win": win, "wqkv": wqkv, "wro": wro, "wao": wao, "wout": wout, "wga": wga, "prm": prm, "cbrow": f(conv_b).reshape(1, 1024),
            "gpre": f(norm_pre).reshape(1, 1024), "gpost": f(norm_post).reshape(1, 1024),
            "etab": tabs[half][0], "sbias": tabs[half][1], "ident": ident, "selq": selq, "selo": selo,
        })
    return in_maps


def kernel(**inputs):
    in_maps = prep_inputs(**inputs)
    if "nc" not in _NC_CACHE:
        _NC_CACHE["nc"] = build_program()
    nc = _NC_CACHE["nc"]
    res = run_bass_kernel_spmd(nc, in_maps, core_ids=list(range(8)))
    R = res.results
    yp = np.zeros((4, 4096, 1024), np.float32)
    ys = np.zeros((32, 1, 1024), np.float32)
    conv_p = np.zeros((1, 4, 3, 1024), np.float32)
    conv_s = np.zeros((1, 32, 3, 1024), np.float32)
    h_p = np.zeros((1, 4, 1024), np.float32)
    h_s = np.zeros((1, 32, 1024), np.float32)
    kvp = [np.zeros((1, 4, 128 * DIL[g], 2, 8, 64), np.float32) for g in range(3)]
    kvsm = [np.zeros((1, 32, 1, 2, 8, 64), np.float32) for g in range(3)]
    for c in range(8):
        b, half = c // 2, c % 2
        r = R[c]
        yp[b, half * 2048:(half + 1) * 2048] = r["y"]
        sl = slice(4 * c, 4 * c + 4)
        ys[sl, 0] = r["ys"]
        conv_s[0, sl, 0:2] = r["convs_s"]
        conv_s[0, sl, 2] = r["convs_u"].reshape(128, 8, 4).transpose(2, 1, 0).reshape(4, 1024)
        h_s[0, sl] = r["hs"].reshape(128, 8, 4).transpose(2, 1, 0).reshape(4, 1024)
        for g in range(3):
            kvsm[g][0, sl, 0] = r["kvs"][g].reshape(4, 2, 8, 64)
        if half == 1:
            conv_p[0, b] = r["convp"].reshape(128, 8, 3).transpose(2, 1, 0).reshape(3, 1024)
            h_p[0, b] = r["hp"].reshape(128, 8).transpose(1, 0).reshape(1024)
            for g in range(3):
                kvp[g][0, b] = r[f"kvo{g}"].reshape(128 * DIL[g], 2, 8, 64)
    return (yp, ys, conv_p, conv_s, h_p, h_s, kvp[0], kvsm[0], kvp[1], kvsm[1], kvp[2], kvsm[2])
```
